# Optimizing a Trainium2 kernel written in Bass

```python
import jax, jax.numpy as jnp
from jax import lax
import numpy as np

D_MODEL = 1024
BATCH = 8
SEQ = 2048
DEPTH = 1
DEC_BATCH = 32
DEC_SEQ = 1
PAST_LEN = 8192
PAGE_SIZE = 128

ATT_GROUPS = ((128, 1), (512, 4), (2048, 16))
N_GROUPS = 3
ATT_HEADS = 4
ATT_HEAD_DIM = 128
ATT_WIDTH = ATT_HEADS * ATT_HEAD_DIM
ATT_BLOCK = 128
HG_HEADS = 4
HG_KEY_DIM = 128
HG_VAL_DIM = 128
HG_KW = HG_HEADS * HG_KEY_DIM
HG_VW = HG_HEADS * HG_VAL_DIM
HG_CHUNK = 64
MEM_LEN = 256
MEM_HEADS = 4
MEM_HEAD_DIM = 128
MEM_WIDTH = MEM_HEADS * MEM_HEAD_DIM
N_BRANCH = 3
D_FF = ((8 * D_MODEL // 3 + 127) // 128) * 128
EPS = 1e-6
IN_WIDTH = 3 * N_GROUPS * ATT_WIDTH + 2 * HG_KW + HG_VW + MEM_WIDTH + N_BRANCH * D_MODEL

kernel_name = 'hybrid_dilated_hgrn2_memory_decoder_step'


def rmsnorm(x, g):
    xf = x.astype(jnp.float32)
    y = xf * lax.rsqrt(jnp.mean(xf * xf, axis=-1, keepdims=True) + EPS)
    return (y * g.astype(jnp.float32)).astype(x.dtype)


def half_ffn(x, g, w_gate, w_up, w_down):
    h = rmsnorm(x, g)
    return (jax.nn.silu(h @ w_gate) * (h @ w_up)) @ w_down


def split_in(z):
    sizes = (N_GROUPS * ATT_WIDTH,) * 3 + (HG_KW, HG_KW, HG_VW, MEM_WIDTH, N_BRANCH * D_MODEL)
    idx, acc = [], 0
    for s in sizes[:-1]:
        acc += s
        idx.append(acc)
    return jnp.split(z, idx, axis=-1)


def att_heads(a):
    B, T, _ = a.shape
    return a.reshape(B, T, N_GROUPS, ATT_HEADS, ATT_HEAD_DIM)


def mem_heads(a):
    B, T, _ = a.shape
    return a.reshape(B, T, MEM_HEADS, MEM_HEAD_DIM)


def dilated_group_prompt(q, k, v, window, dil):
    B, T, H, Dh = q.shape
    span = window // dil
    L = T // dil
    Lp = -(-L // ATT_BLOCK) * ATT_BLOCK
    nb = Lp // ATT_BLOCK
    N = B * dil

    def to_blocks(a):
        a = a.reshape(B, L, dil, H, Dh).transpose(0, 2, 1, 3, 4).reshape(N, L, H, Dh)
        a = jnp.pad(a, ((0, 0), (0, Lp - L), (0, 0), (0, 0)))
        return a.reshape(N, nb, ATT_BLOCK, H, Dh)

    def with_prev(a):
        prev = jnp.pad(a[:, :-1], ((0, 0), (1, 0), (0, 0), (0, 0), (0, 0)))
        return jnp.concatenate([prev, a], axis=2)

    qb = to_blocks(q)
    kk = with_prev(to_blocks(k))
    vv = with_prev(to_blocks(v))
    s = jnp.einsum('nbqhd,nbkhd->nbhqk', qb, kk, preferred_element_type=jnp.float32) * (Dh ** -0.5)
    qi = np.arange(ATT_BLOCK)[:, None]
    ki = np.arange(2 * ATT_BLOCK)[None, :]
    dist = ATT_BLOCK + qi - ki
    band = (dist >= 0) & (dist <= span)
    valid = np.where((np.arange(nb) == 0)[:, None, None], band & (ki >= ATT_BLOCK), band)
    s = jnp.where(valid[None, :, None], s, -jnp.inf)
    lse = jax.nn.logsumexp(s, axis=-1)
    p = jnp.exp(s - lse[..., None])
    o = jnp.einsum('nbhqk,nbkhd->nbqhd', p, vv.astype(jnp.float32))
    o = o.reshape(N, Lp, H, Dh)[:, :L].reshape(B, dil, L, H, Dh).transpose(0, 2, 1, 3, 4).reshape(B, T, H, Dh)
    lse = lse.transpose(0, 1, 3, 2).reshape(N, Lp, H)[:, :L].reshape(B, dil, L, H).transpose(0, 2, 1, 3).reshape(B, T, H)
    return o, lse


def dilated_group_sample(q, k_new, v_new, k_buf, v_buf, window, dil):
    W, S, Dh = k_buf.shape[1], q.shape[1], q.shape[-1]
    span = window // dil
    kk = jnp.concatenate([k_buf, k_new.astype(k_buf.dtype)], axis=1)
    vv = jnp.concatenate([v_buf, v_new.astype(v_buf.dtype)], axis=1)
    rows = W + np.arange(S)[:, None] - dil * np.arange(span + 1)[None, :]
    valid = rows >= 0
    rows = np.clip(rows, 0, None)
    kg = kk[:, rows]
    vg = vv[:, rows]
    s = jnp.einsum('bshd,bsjhd->bshj', q, kg, preferred_element_type=jnp.float32) * (Dh ** -0.5)
    s = jnp.where(valid[None, :, None, :], s, -jnp.inf)
    lse = jax.nn.logsumexp(s, axis=-1)
    p = jnp.exp(s - lse[..., None])
    o = jnp.einsum('bshj,bsjhd->bshd', p, vg.astype(jnp.float32))
    return o, lse, kk[:, S:], vv[:, S:]


def combine_groups(outs, lses):
    o = jnp.stack(outs, axis=0)
    a = jax.nn.softmax(jnp.stack(lses, axis=0), axis=0)
    return jnp.sum(a[..., None] * o, axis=0)


def hgrn2_gates(zq, zf, zi, lb):
    B, T, _ = zq.shape

    def heads(a, d):
        return a.astype(jnp.float32).reshape(B, T, HG_HEADS, d).transpose(0, 2, 1, 3)

    lbh = lb.reshape(HG_HEADS, 1, HG_KEY_DIM)
    q = jax.nn.sigmoid(heads(zq, HG_KEY_DIM))
    k = (1.0 - lbh) * jax.nn.sigmoid(heads(zf, HG_KEY_DIM))
    logf = jnp.log(lbh + k)
    v = heads(zi, HG_VAL_DIM)
    return q, k, v, logf


def hgrn2_chunk(S0, q, k, v, logf):
    C = q.shape[2]
    b = jnp.cumsum(logf, axis=2)
    diff = b[:, :, :, None, :] - b[:, :, None, :, :]
    causal = np.tril(np.ones((C, C), dtype=bool))[None, None, :, :, None]
    decay = jnp.where(causal, jnp.exp(jnp.minimum(diff, 0.0)), 0.0)
    A = jnp.einsum('bhtk,bhsk,bhtsk->bhts', q, k, decay)
    o = jnp.einsum('bhts,bhsv->bhtv', A, v) + jnp.einsum('bhtk,bhkv->bhtv', q * jnp.exp(b), S0)
    b_end = b[:, :, -1:, :]
    S1 = jnp.exp(b_end[:, :, 0, :, None]) * S0 + jnp.einsum('bhsk,bhsv->bhkv', k * jnp.exp(b_end - b), v)
    return S1, o


def hgrn2_prompt(q, k, v, logf):
    B, H, T, K = q.shape
    nc = T // HG_CHUNK

    def chunks(a):
        return a.reshape(B, H, nc, HG_CHUNK, a.shape[-1]).transpose(2, 0, 1, 3, 4)

    S0 = jnp.zeros((B, H, K, HG_VAL_DIM), jnp.float32)

    def step(S, inp):
        return hgrn2_chunk(S, *inp)

    S, o = lax.scan(step, S0, (chunks(q), chunks(k), chunks(v), chunks(logf)))
    o = o.transpose(1, 2, 0, 3, 4).reshape(B, H, T, HG_VAL_DIM)
    return o, S


def head_rmsnorm(o, g):
    B, H, T, V = o.shape
    return rmsnorm(o.transpose(0, 2, 1, 3), g.reshape(H, V)).reshape(B, T, H * V)


def mem_kv(mem, g_mem, w_mk, w_mv):
    B, M, _ = mem.shape
    m = rmsnorm(mem, g_mem)
    return ((m @ w_mk).reshape(B, M, MEM_HEADS, MEM_HEAD_DIM),
            (m @ w_mv).reshape(B, M, MEM_HEADS, MEM_HEAD_DIM))


def mem_attend(q, k, v):
    s = jnp.einsum('bthd,bmhd->bhtm', q, k, preferred_element_type=jnp.float32) * (MEM_HEAD_DIM ** -0.5)
    p = jax.nn.softmax(s, axis=-1)
    return jnp.einsum('bhtm,bmhd->bthd', p, v.astype(jnp.float32))


def merge_branches(att, hgo, memo, zg, w_ba, w_bb, w_bc, w_o):
    B, T, _ = zg.shape
    dt = zg.dtype
    gates = jax.nn.sigmoid(zg.astype(jnp.float32)).reshape(B, T, N_BRANCH, D_MODEL)
    m = (gates[:, :, 0] * (att.reshape(B, T, -1).astype(dt) @ w_ba)
         + gates[:, :, 1] * (hgo.astype(dt) @ w_bb)
         + gates[:, :, 2] * (memo.reshape(B, T, -1).astype(dt) @ w_bc))
    return m.astype(dt) @ w_o


def setup_inputs(seed: int = 0) -> dict:
    key = jax.random.key(seed)
    ks = iter(jax.random.split(key, 48))

    def nrm(shape, scale):
        return scale * jax.random.normal(next(ks), shape, jnp.float32)

    def gain(shape):
        return 1.0 + 0.01 * jax.random.normal(next(ks), shape, jnp.float32)

    wl = [min(w, PAST_LEN) for (w, d) in ATT_GROUPS]
    hd = (ATT_HEADS, ATT_HEAD_DIM)
    return {
        'x_prompt': nrm((BATCH, SEQ, D_MODEL), 1.0),
        'x_sample': nrm((DEC_BATCH, DEC_SEQ, D_MODEL), 1.0),
        'mem_prompt': nrm((BATCH, MEM_LEN, D_MODEL), 1.0),
        'cache_win1_k': nrm((DEPTH, DEC_BATCH, wl[0]) + hd, 1.0),
        'cache_win1_v': nrm((DEPTH, DEC_BATCH, wl[0]) + hd, 1.0),
        'cache_win4_k': nrm((DEPTH, DEC_BATCH, wl[1]) + hd, 1.0),
        'cache_win4_v': nrm((DEPTH, DEC_BATCH, wl[1]) + hd, 1.0),
        'cache_win16_k': nrm((DEPTH, DEC_BATCH, wl[2]) + hd, 1.0),
        'cache_win16_v': nrm((DEPTH, DEC_BATCH, wl[2]) + hd, 1.0),
        'cache_mem_k': nrm((DEPTH, DEC_BATCH, MEM_LEN, MEM_HEADS, MEM_HEAD_DIM), 1.0),
        'cache_mem_v': nrm((DEPTH, DEC_BATCH, MEM_LEN, MEM_HEADS, MEM_HEAD_DIM), 1.0),
        'state_hgrn': nrm((DEPTH, DEC_BATCH, HG_HEADS, HG_KEY_DIM, HG_VAL_DIM), 0.5),
        'g_ff1': gain((DEPTH, D_MODEL)),
        'w_ff1_gate': nrm((DEPTH, D_MODEL, D_FF), D_MODEL ** -0.5),
        'w_ff1_up': nrm((DEPTH, D_MODEL, D_FF), D_MODEL ** -0.5),
        'w_ff1_down': nrm((DEPTH, D_FF, D_MODEL), D_FF ** -0.5),
        'g_mix': gain((DEPTH, D_MODEL)),
        'w_in': nrm((DEPTH, D_MODEL, IN_WIDTH), D_MODEL ** -0.5),
        'hg_lb_logits': nrm((DEPTH + 1, HG_KW), 0.5),
        'g_hg_out': gain((DEPTH, HG_VW)),
        'g_mem': gain((DEPTH, D_MODEL)),
        'w_mem_k': nrm((DEPTH, D_MODEL, MEM_WIDTH), D_MODEL ** -0.5),
        'w_mem_v': nrm((DEPTH, D_MODEL, MEM_WIDTH), D_MODEL ** -0.5),
        'w_branch_att': nrm((DEPTH, ATT_WIDTH, D_MODEL), ATT_WIDTH ** -0.5),
        'w_branch_hg': nrm((DEPTH, HG_VW, D_MODEL), HG_VW ** -0.5),
        'w_branch_mem': nrm((DEPTH, MEM_WIDTH, D_MODEL), MEM_WIDTH ** -0.5),
        'w_out': nrm((DEPTH, D_MODEL, D_MODEL), D_MODEL ** -0.5),
        'g_ff2': gain((DEPTH, D_MODEL)),
        'w_ff2_gate': nrm((DEPTH, D_MODEL, D_FF), D_MODEL ** -0.5),
        'w_ff2_up': nrm((DEPTH, D_MODEL, D_FF), D_MODEL ** -0.5),
        'w_ff2_down': nrm((DEPTH, D_FF, D_MODEL), D_FF ** -0.5),
        'g_final': gain((D_MODEL,)),
    }


def reference(x_prompt, x_sample, mem_prompt,
              cache_win1_k, cache_win1_v, cache_win4_k, cache_win4_v, cache_win16_k, cache_win16_v,
              cache_mem_k, cache_mem_v, state_hgrn,
              g_ff1, w_ff1_gate, w_ff1_up, w_ff1_down,
              g_mix, w_in, hg_lb_logits, g_hg_out, g_mem, w_mem_k, w_mem_v,
              w_branch_att, w_branch_hg, w_branch_mem, w_out,
              g_ff2, w_ff2_gate, w_ff2_up, w_ff2_down, g_final):
    T = x_prompt.shape[1]
    lb_all = jnp.cumsum(jax.nn.softmax(hg_lb_logits.astype(jnp.float32), axis=0), axis=0)
    win_k = (cache_win1_k, cache_win4_k, cache_win16_k)
    win_v = (cache_win1_v, cache_win4_v, cache_win16_v)
    pk = [[] for _ in ATT_GROUPS]
    pv = [[] for _ in ATT_GROUPS]
    sk = [[] for _ in ATT_GROUPS]
    sv = [[] for _ in ATT_GROUPS]
    p_mk, p_mv, p_hs, s_hs = [], [], [], []
    xp, xs = x_prompt, x_sample
    for l in range(DEPTH):
        lb = lb_all[l]
        ffn1 = (g_ff1[l], w_ff1_gate[l], w_ff1_up[l], w_ff1_down[l])
        ffn2 = (g_ff2[l], w_ff2_gate[l], w_ff2_up[l], w_ff2_down[l])
        bw = (w_branch_att[l], w_branch_hg[l], w_branch_mem[l], w_out[l])
        xp = xp + 0.5 * half_ffn(xp, *ffn1)
        xs = xs + 0.5 * half_ffn(xs, *ffn1)

        qa, ka, va, hq, hf, hi, mq, zg = split_in(rmsnorm(xp, g_mix[l]) @ w_in[l])
        qa, ka, va = att_heads(qa), att_heads(ka), att_heads(va)
        outs, lses = [], []
        for g, (win, dil) in enumerate(ATT_GROUPS):
            o, s = dilated_group_prompt(qa[:, :, g], ka[:, :, g], va[:, :, g], win, dil)
            outs.append(o)
            lses.append(s)
            keep = min(win, T)
            pk[g].append(ka[:, T - keep:, g])
            pv[g].append(va[:, T - keep:, g])
        att = combine_groups(outs, lses)
        q, k, v, logf = hgrn2_gates(hq, hf, hi, lb)
        o_hg, S = hgrn2_prompt(q, k, v, logf)
        mk, mv = mem_kv(mem_prompt, g_mem[l], w_mem_k[l], w_mem_v[l])
        o_mem = mem_attend(mem_heads(mq), mk, mv)
        xp = xp + merge_branches(att, head_rmsnorm(o_hg, g_hg_out[l]), o_mem, zg, *bw)
        p_mk.append(mk)
        p_mv.append(mv)
        p_hs.append(S)

        qa, ka, va, hq, hf, hi, mq, zg = split_in(rmsnorm(xs, g_mix[l]) @ w_in[l])
        qa, ka, va = att_heads(qa), att_heads(ka), att_heads(va)
        outs, lses = [], []
        for g, (win, dil) in enumerate(ATT_GROUPS):
            o, s, nk, nv = dilated_group_sample(qa[:, :, g], ka[:, :, g], va[:, :, g],
                                                win_k[g][l], win_v[g][l], win, dil)
            outs.append(o)
            lses.append(s)
            sk[g].append(nk)
            sv[g].append(nv)
        att = combine_groups(outs, lses)
        q, k, v, logf = hgrn2_gates(hq, hf, hi, lb)
        S, o_hg = hgrn2_chunk(state_hgrn[l].astype(jnp.float32), q, k, v, logf)
        o_mem = mem_attend(mem_heads(mq), cache_mem_k[l], cache_mem_v[l])
        xs = xs + merge_branches(att, head_rmsnorm(o_hg, g_hg_out[l]), o_mem, zg, *bw)
        s_hs.append(S)

        xp = xp + 0.5 * half_ffn(xp, *ffn2)
        xs = xs + 0.5 * half_ffn(xs, *ffn2)

    y_prompt = rmsnorm(xp, g_final)
    y_sample = rmsnorm(xs, g_final)
    p_win1_k, p_win1_v = jnp.stack(pk[0]), jnp.stack(pv[0])
    p_win4_k, p_win4_v = jnp.stack(pk[1]), jnp.stack(pv[1])
    p_win16_k, p_win16_v = jnp.stack(pk[2]), jnp.stack(pv[2])
    p_mem_k, p_mem_v, p_hgrn = jnp.stack(p_mk), jnp.stack(p_mv), jnp.stack(p_hs)
    s_win1_k, s_win1_v = jnp.stack(sk[0]), jnp.stack(sv[0])
    s_win4_k, s_win4_v = jnp.stack(sk[1]), jnp.stack(sv[1])
    s_win16_k, s_win16_v = jnp.stack(sk[2]), jnp.stack(sv[2])
    s_hgrn = jnp.stack(s_hs)
    return (y_prompt, y_sample,
            p_win1_k, p_win1_v, p_win4_k, p_win4_v, p_win16_k, p_win16_v,
            p_mem_k, p_mem_v, p_hgrn,
            s_win1_k, s_win1_v, s_win4_k, s_win4_v, s_win16_k, s_win16_v,
            s_hgrn)
```

```python
import numpy as np
import concourse.bass as bass
import concourse.mybir as mybir
from concourse.bass_utils import run_bass_kernel_spmd

F32 = mybir.dt.float32
BF16 = mybir.dt.bfloat16
ALU = mybir.AluOpType
AF = mybir.ActivationFunctionType
AX = mybir.AxisListType

PE, ACT, DVE, POOL, SP = "pe", "act", "dve", "pool", "sp"
COMPUTE = (PE, ACT, DVE, POOL, SP)


class Dep:
    __slots__ = ("w", "r", "name")

    def __init__(self, name=""):
        self.w = None
        self.r = {}
        self.name = name


class Prog:
    def __init__(self, nc, lanes=None):
        self.nc = nc
        self.lanes = lanes or {SP: 12, POOL: 12, ACT: 8, "bulk": 24}
        self.ops = {e: [] for e in COMPUTE}
        self.stream_ops = {}
        self.lane_rr = {q: 0 for q in self.lanes}
        self.waited = {e: {} for e in COMPUTE}

    def _stream_list(self, s):
        return self.stream_ops.setdefault(s, [])

    def _collect(self, reads, writes, own_stream, skip_waw=False):
        deps = {}

        def need(p):
            if p is None:
                return
            s, k = p
            if s == own_stream and s == PE:
                return
            if deps.get(s, -1) < k:
                deps[s] = k
        for d in reads:
            need(d.w)
        for d in writes:
            if not skip_waw:
                need(d.w)
            for s, k in d.r.items():
                need((s, k))
        return deps

    def op(self, eng, fn, reads=(), writes=()):
        lst = self._stream_list(eng)
        idx = len(lst)
        deps = self._collect(reads, writes, eng)
        waits = []
        wd = self.waited[eng]
        for s, k in deps.items():
            if s == eng and k >= idx:
                continue
            if wd.get(s, -1) < k:
                waits.append((s, k))
                wd[s] = k
        rec = dict(eng=eng, stream=eng, idx=idx, fn=fn, waits=waits, sig=False, dma=False)
        lst.append(rec)
        self.ops[eng].append(rec)
        for d in reads:
            if d.r.get(eng, -1) < idx:
                d.r[eng] = idx
        for d in writes:
            d.w = (eng, idx)
            d.r = {}
        return rec

    def dma(self, q, fn, reads=(), writes=(), pool=None, skip_waw=False):
        pool = pool or q
        nl = self.lanes[pool]
        lane = ("dma", pool, self.lane_rr[pool] % nl)
        self.lane_rr[pool] += 1
        lst = self._stream_list(lane)
        idx = len(lst)
        deps = self._collect(reads, writes, lane, skip_waw)
        if idx > 0:
            deps[lane] = max(deps.get(lane, -1), idx - 1)
        waits = []
        wd = self.waited[q]
        for s, k in deps.items():
            if wd.get(s, -1) < k:
                waits.append((s, k))
                wd[s] = k
        rec = dict(eng=q, stream=lane, idx=idx, fn=fn, waits=waits, sig=True, dma=True)
        lst.append(rec)
        self.ops[q].append(rec)
        for d in reads:
            if d.r.get(lane, -1) < idx:
                d.r[lane] = idx
        for d in writes:
            d.w = (lane, idx)
            d.r = {}
        return rec

    def wait_all(self, eng, deps_list):
        def fn(e):
            return e.nop()
        return self.op(eng, fn, reads=deps_list, writes=())

    def emit(self):
        nc = self.nc
        for e in COMPUTE:
            for rec in self.ops[e]:
                for s, k in rec["waits"]:
                    self.stream_ops[s][k]["sig"] = True
        semval = {}
        for s, lst in self.stream_ops.items():
            c = 0
            vals = []
            inc = 16 if isinstance(s, tuple) else 1
            for rec in lst:
                if rec["sig"]:
                    c += inc
                vals.append(c)
            semval[s] = vals
        streams = [s for s in self.stream_ops if any(r["sig"] for r in self.stream_ops[s])]
        import contextlib
        with contextlib.ExitStack() as st:
            sems = {}
            for i, s in enumerate(streams):
                nm = "s_" + (s if isinstance(s, str) else f"{s[1]}{s[2]}")
                sems[s] = st.enter_context(nc.semaphore(nm))
            block = st.enter_context(nc.Block())

            def run(eng_name):
                def body(e):
                    for rec in self.ops[eng_name]:
                        for s, k in rec["waits"]:
                            e.wait_ge(sems[s], semval[s][k])
                        inst = rec["fn"](e)
                        if rec["sig"]:
                            inst.then_inc(sems[rec["stream"]], 16 if rec["dma"] else 1)
                return body
            block.tensor(run(PE))
            block.scalar(run(ACT))
            block.vector(run(DVE))
            block.gpsimd(run(POOL))
            block.sync(run(SP))
        return len(streams)

from contextlib import ExitStack

T = 2048
NS = 4
NT = T + NS
D = 1024
DFF = 2816
NJ = DFF // 128
INW = 9728
TG = [(0, 512), (512, 512), (1024, 512), (1536, 512), (2048, 4)]
WIN = [(128, 1), (512, 4), (2048, 16)]
SCALE = 128.0 ** -0.5
EPS = 1e-6
ARENA = 30720


class Buf:
    def __init__(self, t, nd=1):
        self.t = t
        self.ds = [Dep() for _ in range(nd)]

    @property
    def d(self):
        return self.ds[0]


class Ring:
    def __init__(self, bufs):
        self.bufs = bufs
        self.i = 0

    def next(self):
        b = self.bufs[self.i % len(self.bufs)]
        self.i += 1
        return b


def build_nc():
    nc = bass.Bass("TRN2", target_bir_lowering=False)
    P = Prog(nc)

    def din(name, shape):
        return nc.dram_tensor(name, list(shape), F32, kind="ExternalInput").ap()

    def dout(name, shape):
        return nc.dram_tensor(name, list(shape), F32, kind="ExternalOutput").ap()

    x_p = din("x_p", [T, D]); x_s = din("x_s", [NS, D]); mem = din("mem", [256, D])
    cwk = [din(f"cw{g}k", [NS, WIN[g][0], 512]) for g in range(3)]
    cwv = [din(f"cw{g}v", [NS, WIN[g][0], 512]) for g in range(3)]
    cmk = din("cmk", [NS, 256, 512]); cmv = din("cmv", [NS, 256, 512])
    st_in = din("st", [NS * 4, 128, 128])
    g_ff1 = din("g_ff1", [D]); g_mix = din("g_mix", [D]); g_mem = din("g_mem", [D]); g_ff2 = din("g_ff2", [D])
    g_fin = din("g_final", [D]); g_hg = din("g_hg_out", [512]); lbl = din("hg_lb_logits", [2, 512])
    w1g = din("w_ff1_gate", [D, DFF]); w1u = din("w_ff1_up", [D, DFF]); w1d = din("w_ff1_down", [DFF, D])
    w2g = din("w_ff2_gate", [D, DFF]); w2u = din("w_ff2_up", [D, DFF]); w2d = din("w_ff2_down", [DFF, D])
    w_in = din("w_in", [D, INW]); w_mk = din("w_mem_k", [D, 512]); w_mv = din("w_mem_v", [D, 512])
    w_ba = din("w_branch_att", [512, D]); w_bb = din("w_branch_hg", [512, D]); w_bc = din("w_branch_mem", [512, D])
    w_o = din("w_out", [D, D])

    y_p = dout("y_p", [T, D]); y_s = dout("y_s", [NS, D])
    pwk = [dout(f"pw{g}k", [WIN[g][0], 512]) for g in range(3)]
    pwv = [dout(f"pw{g}v", [WIN[g][0], 512]) for g in range(3)]
    pmk = dout("pmk", [256, 512]); pmv = dout("pmv", [256, 512]); phg = dout("phg", [4, 128, 128])
    swk = [dout(f"sw{g}k", [NS, WIN[g][0], 512]) for g in range(3)]
    swv = [dout(f"sw{g}v", [NS, WIN[g][0], 512]) for g in range(3)]
    shg = dout("shg", [NS * 4, 128, 128])

    out_deps = []

    xT = nc.alloc_sbuf_tensor("xT", [128, 8, NT], F32)
    hT = nc.alloc_sbuf_tensor("hT", [128, 8, NT], BF16)
    dx = [Dep(f"x{i}") for i in range(5)]
    dh = [Dep(f"h{i}") for i in range(5)]
    aarena = nc.alloc_sbuf_tensor("aarena", [128, NJ * 1028], BF16)
    arena = xT[:, :, :].rearrange("p c n -> p (c n)").bitcast(BF16)
    xd = nc.dram_tensor("xd_scratch", [128, 8, NT], F32).ap()
    xd2 = nc.dram_tensor("xd2_scratch", [128, 8, NT], F32).ap()
    _a2 = [0]

    def A2(nel, dt=BF16, parts=128):
        o_ = _a2[0]
        n2 = nel * (2 if dt == F32 else 1)
        _a2[0] = o_ + (n2 + 15) // 16 * 16
        assert _a2[0] <= NJ * 1028, _a2[0]
        _a2.append(_a2[0])
        v = aarena[0:parts, o_:o_ + n2]
        return v.bitcast(F32) if dt == F32 else v
    macc_bufs = [Buf(A2(512, F32)) for _ in range(2)]
    xo_r = Ring([Buf(A2(512, F32)) for _ in range(3)])
    a2_keep = _a2[0]
    wring = Ring([Buf(nc.alloc_sbuf_tensor(f"wp{i}", [128, 4608], BF16), 6) for i in range(3)])
    stage = Ring([Buf(nc.alloc_sbuf_tensor(f"stg{i}", [128, 1024], F32)) for i in range(2)])
    f32r = Ring([Buf(nc.alloc_sbuf_tensor(f"f32r{i}", [128, 512], F32)) for i in range(8)])
    bfr = Ring([Buf(nc.alloc_sbuf_tensor(f"bfr{i}", [128, 512], BF16)) for i in range(3)])
    psr = Ring([Buf(nc.alloc_psum_tensor(f"ps{i}", [128, 512], F32)) for i in range(6)])
    ps_long = Buf(nc.alloc_psum_tensor("pslong", [128, 512], F32))
    ps_long2 = Buf(nc.alloc_psum_tensor("pslong2", [128, 512], F32))
    cst = nc.alloc_sbuf_tensor("cst", [128, 1280], F32)
    cbf = nc.alloc_sbuf_tensor("cbf", [128, 1280], BF16)
    vec = nc.alloc_sbuf_tensor("vec", [128, 64], F32)
    dc = Dep("const")
    d_id = Dep("ident")

    ident_f = cst[:, 0:128]; ones_f = cst[:, 128:256]; rst = cst[:, 256:768]
    sel = cst[0:4, 768:768 + 512].rearrange("p (b n) -> p b n", b=4)
    ident_b = cbf[:, 0:128]; ones_b = cbf[:, 128:256]
    mask2 = cbf[:, 256:512]
    hmask = cbf[:, 512:640]
    hmask4 = cbf[:, 640:1152]
    gf1 = vec[:, 0:8]; gmx = vec[:, 8:16]; gme = vec[:, 16:24]; gf2 = vec[:, 24:32]; gfi = vec[:, 32:40]
    ghg = vec[:, 40:44]; lb = vec[:, 44:48]; oml = vec[:, 48:52]; l1 = vec[:, 52:56]
    epsc = vec[:, 56:57]; zeroc = vec[:, 57:58]

    def ps():
        return psr.next()

    def consts():
        tmp = f32r.next()
        P.op(POOL, lambda e: e.memset(cst[:], 1.0), writes=[dc, d_id])
        P.op(POOL, lambda e: e.memset(ident_f, 0.0), writes=[d_id])
        P.op(POOL, lambda e: e.affine_select(out=ident_f, in_=ident_f, compare_op=ALU.not_equal, fill=1.0,
                                             base=0, pattern=[[-1, 128]], channel_multiplier=1), writes=[d_id])
        P.op(POOL, lambda e: e.memset(cst[:, 256:768].rearrange("p (c n) -> p c n", n=64)[:, :, 0:1], 0.0), writes=[dc])
        P.op(POOL, lambda e: e.affine_select(out=cst[0:4, 768:1280].rearrange("p (b n) -> p b n", b=4),
                                             in_=cst[0:4, 768:1280].rearrange("p (b n) -> p b n", b=4),
                                             compare_op=ALU.is_equal, fill=0.0, base=0,
                                             pattern=[[-1, 4], [0, 128]], channel_multiplier=1), writes=[dc])
        t = tmp.t
        P.op(POOL, lambda e: e.memset(t[:, 0:384], 1.0), writes=[tmp.d])
        P.op(POOL, lambda e: e.affine_select(out=t[:, 0:128], in_=t[:, 0:128], compare_op=ALU.is_ge, fill=0.0,
                                             base=0, pattern=[[1, 128]], channel_multiplier=-1), writes=[tmp.d])
        P.op(POOL, lambda e: e.affine_select(out=t[:, 128:256], in_=t[:, 128:256], compare_op=ALU.is_ge, fill=0.0,
                                             base=0, pattern=[[-1, 128]], channel_multiplier=1), writes=[tmp.d])
        P.op(POOL, lambda e: e.tensor_copy(out=t[:, 256:384], in_=t[:, 0:128]), writes=[tmp.d])
        P.op(POOL, lambda e: e.memset(t[0:64, 320:384], 0.0), writes=[tmp.d])
        P.op(POOL, lambda e: e.tensor_copy(out=cbf[:, 256:640], in_=t[:, 0:384]), reads=[tmp.d], writes=[dc])
        for j_ in range(4):
            P.op(POOL, lambda e, j_=j_: e.tensor_copy(out=cbf[:, 640 + j_ * 128:768 + j_ * 128], in_=t[:, 256:384]), reads=[tmp.d], writes=[dc])
        P.op(POOL, lambda e: e.tensor_copy(out=cbf[:, 0:256], in_=cst[:, 0:256]), reads=[d_id], writes=[dc])
        P.op(POOL, lambda e: e.memset(vec[:, 56:57], EPS), writes=[dc])
        P.op(POOL, lambda e: e.memset(vec[:, 57:58], 0.0), writes=[dc])
        dvec = []
        for dst, src, n in ((gf1, g_ff1, 8), (gmx, g_mix, 8), (gme, g_mem, 8), (gf2, g_ff2, 8), (gfi, g_fin, 8), (ghg, g_hg, 4)):
            dd_ = Dep()
            dvec.append(dd_)
            P.dma(ACT, lambda e, dst=dst, src=src: e.dma_start(out=dst, in_=src.rearrange("(c p) -> p c", p=128),
                                                                allow_slow_non_contiguous=True), writes=[dd_])
        dlb, dl1 = Dep(), Dep()
        P.dma(ACT, lambda e: e.dma_start(out=lb, in_=lbl[0].rearrange("(c p) -> p c", p=128), allow_slow_non_contiguous=True), writes=[dlb])
        P.dma(ACT, lambda e: e.dma_start(out=l1, in_=lbl[1].rearrange("(c p) -> p c", p=128), allow_slow_non_contiguous=True), writes=[dl1])
        late["deps"] = dvec + [dlb, dl1]

    late = {}

    def consts_late():
        dvec_all = late["deps"]
        P.op(DVE, lambda e: e.tensor_copy(out=vec[:, 63:64], in_=vec[:, 57:58]), reads=dvec_all + [dc], writes=[dc])
        P.op(DVE, lambda e: e.tensor_tensor(out=lb, in0=lb, in1=l1, op=ALU.subtract), reads=[dc], writes=[dc])
        P.op(ACT, lambda e: e.activation(out=lb, in_=lb, func=AF.Sigmoid), reads=[dc], writes=[dc])
        P.op(DVE, lambda e: e.tensor_scalar(out=oml, in0=lb, scalar1=-1.0, scalar2=1.0, op0=ALU.mult, op1=ALU.add), reads=[dc], writes=[dc])

    def load_tokmajor_to_T(src_ap, nrows, dstT, tok0, ddst):
        sb = stage.next()
        P.dma(SP, lambda e: e.dma_start(out=sb.t[0:nrows, :], in_=src_ap), writes=[sb.d])
        for hf in range(2):
            pb = ps()
            for c4 in range(4):
                c = hf * 4 + c4
                P.op(PE, lambda e, c=c, c4=c4, pb=pb: e.transpose(out=pb.t[:, c4 * 128:c4 * 128 + nrows],
                                                           in_=sb.t[0:nrows, c * 128:(c + 1) * 128], identity=ident_f[0:nrows, 0:nrows]),
                     reads=[sb.d, d_id], writes=[pb.d])
            eng = ACT if hf == 0 else DVE
            if eng == ACT:
                P.op(ACT, lambda e, hf=hf, pb=pb: e.copy(out=dstT[:, hf * 4:hf * 4 + 4, tok0:tok0 + nrows],
                                                  in_=pb.t[:].rearrange("p (c n) -> p c n", c=4)[:, :, 0:nrows]),
                     reads=[pb.d], writes=[ddst])
            else:
                P.op(DVE, lambda e, hf=hf, pb=pb: e.tensor_copy(out=dstT[:, hf * 4:hf * 4 + 4, tok0:tok0 + nrows],
                                                         in_=pb.t[:].rearrange("p (c n) -> p c n", c=4)[:, :, 0:nrows]),
                     reads=[pb.d], writes=[ddst])

    def rstd_of(srcT_fn, n, dsrc, nchunks=8, denom=1024.0):
        pb = ps()
        for c in range(nchunks):
            sb = bfr.next()
            P.op(ACT, lambda e, c=c, sb=sb: e.activation(out=sb.t[:, :n], in_=srcT_fn(c), func=AF.Square),
                 reads=[dsrc], writes=[sb.d])
            P.op(PE, lambda e, c=c, sb=sb: e.matmul(pb.t[:, :n], lhsT=ones_b, rhs=sb.t[:, :n], start=(c == 0), stop=(c == nchunks - 1)),
                 reads=[sb.d, dc], writes=[pb.d])
        rs = f32r.next()
        P.op(ACT, lambda e: e.activation(out=rs.t[:, :n], in_=pb.t[:, :n], func=AF.Ln, bias=epsc, scale=1.0 / denom),
             reads=[pb.d, dc], writes=[rs.d])
        P.op(ACT, lambda e: e.activation(out=rs.t[:, :n], in_=rs.t[:, :n], func=AF.Exp, scale=-0.5), reads=[rs.d], writes=[rs.d])
        return rs

    def norm_to(srcT, dsrcs, gcol, dstT, ddsts, tgs):
        for ti, (t0, n) in tgs:
            rs = rstd_of(lambda c, t0=t0, n=n: srcT[:, c, t0:t0 + n], n, dsrcs[ti])
            for c in range(8):
                P.op(DVE, lambda e, c=c, t0=t0, n=n, rs=rs: e.scalar_tensor_tensor(
                    out=dstT[:, c, t0:t0 + n], in0=srcT[:, c, t0:t0 + n], scalar=gcol[:, c:c + 1], in1=rs.t[:, :n],
                    op0=ALU.mult, op1=ALU.mult), reads=[dsrcs[ti], rs.d, dc], writes=[ddsts[ti]])

    ALLTG = list(enumerate(TG))

    act = aarena[:, 0:NJ * 1028].rearrange("p (j n) -> p j n", j=NJ)
    dact = [Dep("act0"), Dep("act1"), Dep("act2")]

    def ffn(gcol, wg, wu, wd):
        norm_to(xT, dx, gcol, hT, dh, ALLTG)
        wg_v = wg.rearrange("(c p) n -> p c n", p=128)
        wu_v = wu.rearrange("(c p) n -> p c n", p=128)
        wd_v = wd.rearrange("(j p) n -> p j n", p=128)
        for half in ([0, 1], [2, 3, 4]):
            for j in range(NJ):
                if j % 2 == 0:
                    wb = wring.next()
                    gv2 = wb.t[:, 0:2048].rearrange("p (c n) -> p c n", c=8)
                    uv2 = wb.t[:, 2048:4096].rearrange("p (c n) -> p c n", c=8)
                    P.dma(POOL, lambda e, gv2=gv2, j=j: e.dma_start(out=gv2, in_=wg_v[:, :, j * 128:(j + 2) * 128]), writes=wb.ds)
                    P.dma(POOL, lambda e, uv2=uv2, j=j: e.dma_start(out=uv2, in_=wu_v[:, :, j * 128:(j + 2) * 128]), writes=[wb.ds[1]], skip_waw=True)
                gv = gv2[:, :, (j % 2) * 128:(j % 2 + 1) * 128]
                uv = uv2[:, :, (j % 2) * 128:(j % 2 + 1) * 128]
                for si, ti in enumerate(half):
                    t0, n = TG[ti]
                    off = si * 512
                    pg = ps(); pu = ps()
                    for c in range(8):
                        P.op(PE, lambda e, c=c, gv=gv, pg=pg, t0=t0, n=n: e.matmul(pg.t[:, :n], lhsT=gv[:, c, :], rhs=hT[:, c, t0:t0 + n],
                                                                                  start=(c == 0), stop=(c == 7)),
                             reads=[wb.ds[0], dh[ti]], writes=[pg.d])
                    for c in range(8):
                        P.op(PE, lambda e, c=c, uv=uv, pu=pu, t0=t0, n=n: e.matmul(pu.t[:, :n], lhsT=uv[:, c, :], rhs=hT[:, c, t0:t0 + n],
                                                                                  start=(c == 0), stop=(c == 7)),
                             reads=[wb.ds[1], dh[ti]], writes=[pu.d])
                    sg = f32r.next()
                    P.op(ACT, lambda e, sg=sg, pg=pg, n=n: e.activation(out=sg.t[:, :n], in_=pg.t[:, :n], func=AF.Silu),
                         reads=[pg.d], writes=[sg.d])
                    P.op(DVE, lambda e, sg=sg, pu=pu, n=n, j=j, off=off: e.tensor_tensor(out=act[:, j, off:off + n], in0=pu.t[:, :n],
                                                                                       in1=sg.t[:, :n], op=ALU.mult),
                         reads=[pu.d, sg.d], writes=[dact[si]])
            for dp in range(8):
                wb = wring.next()
                dv = wb.t[:, 0:NJ * 128].rearrange("p (j n) -> p j n", j=NJ)
                P.dma(POOL, lambda e, dv=dv, dp=dp: e.dma_start(out=dv, in_=wd_v[:, :, dp * 128:(dp + 1) * 128]), writes=wb.ds)
                for si, ti in enumerate(half):
                    t0, n = TG[ti]
                    off = si * 512
                    pd = ps()
                    for j in range(NJ):
                        P.op(PE, lambda e, j=j, dv=dv, pd=pd, off=off, n=n: e.matmul(pd.t[:, :n], lhsT=dv[:, j, :], rhs=act[:, j, off:off + n],
                                                                                    start=(j == 0), stop=(j == NJ - 1)),
                             reads=[wb.ds[0], dact[si]], writes=[pd.d])
                    P.op(DVE, lambda e, pd=pd, dp=dp, t0=t0, n=n: e.scalar_tensor_tensor(
                        out=xT[:, dp, t0:t0 + n], in0=pd.t[:, :n], scalar=0.5, in1=xT[:, dp, t0:t0 + n], op0=ALU.mult, op1=ALU.add),
                        reads=[pd.d, dx[ti]], writes=[dx[ti]])

    o = 0
    attT = arena[:, o:o + 4 * NT].rearrange("p (h n) -> p h n", h=4); o += 4 * NT
    hgoT = arena[:, o:o + 4 * NT].rearrange("p (h n) -> p h n", h=4); o += 4 * NT
    memoT = arena[:, o:o + 4 * NT].rearrange("p (h n) -> p h n", h=4); o += 4 * NT
    o = (o + 15) // 16 * 16
    scr0 = o
    d_att, d_hgo, d_memo = Dep("attT"), Dep("hgoT"), Dep("memoT")
    d_scr = Dep("scratch")

    w_in_v = w_in.rearrange("(c p) n -> p c n", p=128)

    def load_wchunk(src_v, col0, ncols=128, nk=8):
        wb = wring.next()
        v = wb.t[:, 0:nk * ncols].rearrange("p (c n) -> p c n", c=nk)
        P.dma(POOL, lambda e: e.dma_start(out=v, in_=src_v[:, :, col0:col0 + ncols]), writes=wb.ds)
        return wb, v

    def proj_fm(wb, wv, srcT, dsrc_of, tgs, sink, nk=8):
        for ti, (t0, n) in tgs:
            pb = ps()
            for c in range(nk):
                P.op(PE, lambda e, c=c, pb=pb, t0=t0, n=n: e.matmul(pb.t[:, :n], lhsT=wv[:, c, :], rhs=srcT[:, c, t0:t0 + n],
                                                                   start=(c == 0), stop=(c == nk - 1)),
                     reads=[wb.ds[0], dsrc_of(ti)], writes=[pb.d])
            sink(ti, t0, n, pb)

    vtok_o = scr0
    vtok = arena[:, vtok_o:vtok_o + 16 * 512].rearrange("p (i n) -> p i n", i=16)
    d_vtok = Dep("vtok")
    his = A2(512, F32, parts=4)
    a2_after_his = _a2[0]
    d_his = Dep("his")

    def tokmajor_pass():
        jobs = []
        for g in range(3):
            keep = WIN[g][0]
            tiles = list(range(16 - keep // 128, 16))
            jobs.append((1536 + g * 512, tiles, ("k", g)))
            jobs.append((3072 + g * 512, tiles, ("v", g)))
        jobs.append((5632, list(range(16)), ("hi", 0)))
        for col0, tiles, (kind, g) in jobs:
            wb, wv = load_wchunk(w_in_v, col0, 512)
            for i in tiles + [16]:
                tok0, m = (i * 128, 128) if i < 16 else (T, NS)
                ti = min(i // 4, 4)
                pb = ps()
                for c in range(8):
                    P.op(PE, lambda e, c=c, pb=pb, tok0=tok0, m=m, wv=wv: e.matmul(pb.t[0:m, :], lhsT=hT[:, c, tok0:tok0 + m], rhs=wv[:, c, :],
                                                                                 start=(c == 0), stop=(c == 7)),
                         reads=[wb.ds[0], dh[ti]], writes=[pb.d])
                if kind == "hi":
                    if i < 16:
                        P.op(ACT, lambda e, pb=pb, i=i: e.copy(out=vtok[:, i, :], in_=pb.t[:, :]), reads=[pb.d], writes=[d_vtok])
                    else:
                        P.op(ACT, lambda e, pb=pb: e.copy(out=his[:, :], in_=pb.t[0:NS, :]), reads=[pb.d], writes=[d_his])
                else:
                    sb = stage.next()
                    eng = ACT if (i % 2 == 0) else DVE
                    if eng == ACT:
                        P.op(ACT, lambda e, pb=pb, sb=sb, m=m: e.copy(out=sb.t[0:m, 0:512], in_=pb.t[0:m, :]), reads=[pb.d], writes=[sb.d])
                    else:
                        P.op(DVE, lambda e, pb=pb, sb=sb, m=m: e.tensor_copy(out=sb.t[0:m, 0:512], in_=pb.t[0:m, :]), reads=[pb.d], writes=[sb.d])
                    keep = WIN[g][0]
                    if i < 16:
                        dst = (pwk if kind == "k" else pwv)[g]
                        r0 = i * 128 - (T - keep)
                        P.dma(SP, lambda e, dst=dst, r0=r0, sb=sb: e.dma_start(out=dst[r0:r0 + 128, :], in_=sb.t[:, 0:512]), reads=[sb.d])
                    else:
                        dst = (swk if kind == "k" else swv)[g]
                        P.dma(SP, lambda e, dst=dst, keep=keep, sb=sb: e.dma_start(out=dst[:, keep - 1, :], in_=sb.t[0:NS, 0:512]), reads=[sb.d])
                    out_deps.append(sb.d)

    o2 = 4 * NT
    asets = []
    for base_ in (o2, scr0):
        ob_ = base_
        st_ = {}
        st_["q"] = arena[:, ob_:ob_ + T]; ob_ += T
        st_["k"] = arena[:, ob_:ob_ + T]; ob_ += T
        st_["v"] = arena[:, ob_:ob_ + T].rearrange("p (i n) -> p i n", i=16); ob_ += T
        st_["vT"] = arena[:, ob_:ob_ + T]; ob_ += T
        st_["dq"], st_["dk"], st_["dv"], st_["dvT"] = Dep(), Dep(), Dep(), Dep()
        asets.append(st_)
    o2 += 4 * T
    assert scr0 + 4 * T <= 8 * NT * 2
    acc = arena[:, o2:o2 + 8192].bitcast(F32).rearrange("p (a n) -> p a n", a=2); o2 += 8192
    att_end = o2
    d_acc = Dep("acc")
    zs_att = A2(36 * NS, F32).rearrange("p (c n) -> p c n", c=36)
    d_zs = Dep("zs_att")
    ptr = Ring([Buf(A2(256)) for i in range(4)])

    bulk = []
    for g_ in range(3):
        win_ = WIN[g_][0]
        for b_ in range(NS):
            for src_, dst_ in ((cwk[g_], swk[g_]), (cwv[g_], swv[g_])):
                r_ = 0
                while r_ < win_ - 1:
                    nr_ = min(512, win_ - 1 - r_)
                    bulk.append((src_[b_].rearrange("w n -> (w n)")[(r_ + 1) * 512:(r_ + 1 + nr_) * 512],
                                 dst_[b_].rearrange("w n -> (w n)")[r_ * 512:(r_ + nr_) * 512]))
                    r_ += nr_

    def issue_bulk(k, pace):
        for _ in range(k):
            if not bulk:
                return
            src_, dst_ = bulk.pop()
            dd = Dep()
            P.dma(SP, lambda e, src_=src_, dst_=dst_: e.dma_start(out=dst_, in_=src_), reads=pace, writes=[dd], pool="bulk")
            out_deps.append(dd)

    def attention_prompt():
        order = [(s, g) for s in range(4) for g in range(3)]
        att_proj(order[0][0], order[0][1], 0)
        for i, (s, g) in enumerate(order):
            if i + 1 < len(order):
                att_proj(order[i + 1][0], order[i + 1][1], (i + 1) % 2)
            att_blocks(s, g, i % 2)
            if i >= 2:
                issue_bulk(2, [asets[i % 2]["dq"]])
            if g == 2:
                P.op(ACT, lambda e: e.activation(out=acc[:, 1, :], in_=acc[:, 1, :], func=AF.Ln), reads=[d_acc], writes=[d_acc])
                P.op(ACT, lambda e: e.activation(out=acc[:, 1, :], in_=acc[:, 1, :], func=AF.Exp, scale=-1.0), reads=[d_acc], writes=[d_acc])
                P.op(DVE, lambda e, s=s: e.tensor_tensor(out=attT[:, s, 0:T], in0=acc[:, 0, :], in1=acc[:, 1, :], op=ALU.mult),
                     reads=[d_acc], writes=[d_att])

    def att_proj(s, g, si):
        A_ = asets[si]
        qT_, kT_, vg_, vT_ = A_["q"], A_["k"], A_["v"], A_["vT"]
        win, dil = WIN[g]

        def perm_sink(dst, ddst, kidx):
            def sink(ti, t0, n, pb):
                if ti == 4:
                    zi = kidx * 12 + g * 4 + s
                    P.op(DVE, lambda e: e.tensor_copy(out=zs_att[:, zi, :], in_=pb.t[:, 0:NS]),
                         reads=[pb.d], writes=[d_zs])
                    return
                ov = dst.rearrange("p (r l) -> p r l", r=dil)[:, :, t0 // dil:(t0 + 512) // dil]
                iv = pb.t[:, 0:512].rearrange("p (i r) -> p r i", r=dil)
                if ti % 2 == 0:
                    P.op(ACT, lambda e: e.copy(out=ov, in_=iv), reads=[pb.d], writes=[ddst])
                else:
                    P.op(DVE, lambda e: e.tensor_copy(out=ov, in_=iv), reads=[pb.d], writes=[ddst])
            return sink
        wbq, wq = load_wchunk(w_in_v, g * 512 + s * 128)
        wbk, wk = load_wchunk(w_in_v, 1536 + g * 512 + s * 128)
        wbv, wvv = load_wchunk(w_in_v, 3072 + g * 512 + s * 128)
        proj_fm(wbq, wq, hT, lambda ti: dh[ti], ALLTG, perm_sink(qT_, A_["dq"], 0))
        proj_fm(wbk, wk, hT, lambda ti: dh[ti], ALLTG, perm_sink(kT_, A_["dk"], 1))
        proj_fm(wbv, wvv, hT, lambda ti: dh[ti], ALLTG, perm_sink(vT_, A_["dvT"], 2))
        for q4 in range(4):
            tp = ps()
            tpv = tp.t[:].bitcast(BF16)
            for j in range(4):
                tile = q4 * 4 + j
                P.op(PE, lambda e, j=j, tile=tile, tpv=tpv: e.transpose(out=tpv[:, j * 128:(j + 1) * 128], in_=vT_[:, tile * 128:(tile + 1) * 128], identity=ident_b),
                     reads=[A_["dvT"], dc], writes=[tp.d])
            if q4 % 2 == 0:
                P.op(ACT, lambda e, tpv=tpv, q4=q4: e.copy(out=vg_[:, q4 * 4:q4 * 4 + 4, :], in_=tpv[:, 0:512].rearrange("p (i n) -> p i n", i=4)),
                     reads=[tp.d], writes=[A_["dv"]])
            else:
                P.op(DVE, lambda e, tpv=tpv, q4=q4: e.tensor_copy(out=vg_[:, q4 * 4:q4 * 4 + 4, :], in_=tpv[:, 0:512].rearrange("p (i n) -> p i n", i=4)),
                     reads=[tp.d], writes=[A_["dv"]])

    def att_blocks(s, g, si):
        A_ = asets[si]
        qT_, kT_, vg_ = A_["q"], A_["k"], A_["v"]
        d_q_, d_k_, d_v_ = A_["dq"], A_["dk"], A_["dv"]
        win, dil = WIN[g]
        L = T // dil
        nb = L // 128

        def scores(tile):
            r, b = tile // nb, tile % nb
            p0 = tile * 128
            w = 128 if b == 0 else 256
            sc = ps()
            P.op(PE, lambda e: e.matmul(sc.t[:, 0:128], lhsT=kT_[:, p0:p0 + 128], rhs=qT_[:, p0:p0 + 128], start=True, stop=True),
                 reads=[d_q_, d_k_], writes=[sc.d])
            if b > 0:
                P.op(PE, lambda e: e.matmul(sc.t[:, 128:256], lhsT=kT_[:, p0 - 128:p0], rhs=qT_[:, p0:p0 + 128], start=True, stop=True),
                     reads=[d_q_, d_k_], writes=[sc.d])
            pt = ptr.next()
            P.op(ACT, lambda e: e.activation(out=pt.t[:, 0:w], in_=sc.t[:, 0:w], func=AF.Exp, scale=SCALE),
                 reads=[sc.d], writes=[pt.d])
            P.op(DVE, lambda e: e.tensor_tensor(out=pt.t[:, 0:w], in0=pt.t[:, 0:w], in1=mask2[:, 0:w], op=ALU.mult),
                 reads=[pt.d, dc], writes=[pt.d])
            return pt

        def pv(tile, pt):
            r, b = tile // nb, tile % nb
            ob = ps()
            P.op(PE, lambda e: e.matmul(ob.t[:, 0:128], lhsT=vg_[:, tile, :], rhs=pt.t[:, 0:128], start=True, stop=(b == 0)),
                 reads=[pt.d, d_v_], writes=[ob.d])
            if b > 0:
                P.op(PE, lambda e: e.matmul(ob.t[:, 0:128], lhsT=vg_[:, tile - 1, :], rhs=pt.t[:, 128:256], start=False, stop=True),
                     reads=[pt.d, d_v_], writes=[ob.d])
            P.op(PE, lambda e: e.matmul(ob.t[:, 128:256], lhsT=ones_b, rhs=pt.t[:, 0:128], start=True, stop=(b == 0)),
                 reads=[pt.d, dc], writes=[ob.d])
            if b > 0:
                P.op(PE, lambda e: e.matmul(ob.t[:, 128:256], lhsT=ones_b, rhs=pt.t[:, 128:256], start=False, stop=True),
                     reads=[pt.d, dc], writes=[ob.d])
            start = r + dil * 128 * b
            stop = start + dil * 127 + 1
            av = acc[:, :, start:stop:dil]
            iv = ob.t[:, 0:256].rearrange("p (a n) -> p a n", a=2)
            if g == 0:
                P.op(ACT, lambda e: e.copy(out=av, in_=iv), reads=[ob.d], writes=[d_acc])
            else:
                P.op(DVE, lambda e: e.tensor_tensor(out=av, in0=iv, in1=av, op=ALU.add), reads=[ob.d, d_acc], writes=[d_acc])

        pts = {0: scores(0)}
        for tile in range(16):
            if tile + 1 < 16:
                pts[tile + 1] = scores(tile + 1)
            pv(tile, pts.pop(tile))

    o3 = scr0
    vs_all = arena[:, o3:o3 + 12 * 512].rearrange("p (i n) -> p i n", i=12); o3 += 12 * 512
    d_vs = [Dep(f"vs{i}") for i in range(12)]
    ks_r = Ring([Buf(A2(512)) for i in range(2)])
    kts_r = Ring([Buf(A2(512)) for i in range(2)])
    smallf = A2(256, F32)
    smallb = A2(256)
    d_sm = Dep("small")

    def attention_sample():
        qs_b = smallb[:, 0:48].rearrange("p (c n) -> p c n", c=12)
        P.op(DVE, lambda e: e.tensor_copy(out=qs_b, in_=zs_att[:, 0:12, :]), reads=[d_zs], writes=[d_sm])
        scs = ps_long
        for b in range(NS):
            for g in range(3):
                win, dil = WIN[g]
                i = b * 3 + g
                kb = ks_r.next()
                P.dma(POOL, lambda e, kb=kb, b=b, g=g, win=win, dil=dil: e.dma_start(out=kb.t[:, :], in_=cwk[g][b, 0:win - dil + 1:dil, :]), writes=[kb.d])
                P.dma(POOL, lambda e, i=i, b=b, g=g, win=win, dil=dil: e.dma_start(out=vs_all[:, i, :], in_=cwv[g][b, 0:win - dil + 1:dil, :]), writes=[d_vs[i]])
                tp = ps()
                tpv = tp.t[:].bitcast(BF16)
                for h in range(4):
                    P.op(PE, lambda e, h=h, kb=kb, tpv=tpv: e.transpose(out=tpv[:, h * 128:(h + 1) * 128], in_=kb.t[:, h * 128:(h + 1) * 128], identity=ident_b),
                         reads=[kb.d, dc], writes=[tp.d])
                kt = kts_r.next()
                P.op(ACT, lambda e, kt=kt, tpv=tpv: e.copy(out=kt.t[:, :], in_=tpv[:, 0:512]), reads=[tp.d], writes=[kt.d])
                for h in range(4):
                    col = (g * 4 + h) * 4 + b
                    P.op(PE, lambda e, h=h, kt=kt, col=col, g=g, b=b: e.matmul(scs.t[:, col:col + 1], lhsT=kt.t[:, h * 128:(h + 1) * 128],
                                                                             rhs=qs_b[:, g * 4 + h, b:b + 1], start=True, stop=True),
                         reads=[kt.d, d_sm], writes=[scs.d])
        pts = smallb[:, 64:112]
        P.op(ACT, lambda e: e.activation(out=pts, in_=scs.t[:, 0:48], func=AF.Exp, scale=SCALE), reads=[scs.d], writes=[d_sm])
        obs = ps()
        for b in range(NS):
            for g in range(3):
                i = b * 3 + g
                for h in range(4):
                    col = (g * 4 + h) * 4 + b
                    P.op(PE, lambda e, i=i, h=h, col=col: e.matmul(obs.t[:, col:col + 1], lhsT=vs_all[:, i, h * 128:(h + 1) * 128], rhs=pts[:, col:col + 1],
                                                                 start=True, stop=True), reads=[d_vs[i], d_sm], writes=[obs.d])
        P.op(PE, lambda e: e.matmul(obs.t[:, 64:112], lhsT=ones_b, rhs=pts, start=True, stop=True), reads=[d_sm, dc], writes=[obs.d])
        prod = smallf[:, 0:48]
        P.op(DVE, lambda e: e.tensor_tensor(out=prod.rearrange("p (c n) -> p c n", c=12), in0=zs_att[:, 0:12, :], in1=zs_att[:, 12:24, :], op=ALU.mult),
             reads=[d_zs], writes=[d_sm])
        sn = ps()
        P.op(PE, lambda e: e.matmul(sn.t[:, 0:48], lhsT=ones_f, rhs=prod, start=True, stop=True), reads=[d_sm, dc], writes=[sn.d])
        pn = smallf[:, 48:96]
        P.op(ACT, lambda e: e.activation(out=pn, in_=sn.t[:, 0:48], func=AF.Exp, scale=SCALE), reads=[sn.d], writes=[d_sm])
        num = smallf[:, 96:144]; den = smallf[:, 144:192]
        P.op(DVE, lambda e: e.tensor_tensor(out=num.rearrange("p (c n) -> p c n", c=12), in0=pn.rearrange("p (c n) -> p c n", c=12),
                                            in1=zs_att[:, 24:36, :], op=ALU.mult), reads=[d_sm, d_zs], writes=[d_sm])
        P.op(DVE, lambda e: e.tensor_tensor(out=num, in0=obs.t[:, 0:48], in1=num, op=ALU.add), reads=[obs.d, d_sm], writes=[d_sm])
        P.op(DVE, lambda e: e.tensor_tensor(out=den, in0=obs.t[:, 64:112], in1=pn, op=ALU.add), reads=[obs.d, d_sm], writes=[d_sm])
        for tq in (num, den):
            P.op(DVE, lambda e, tq=tq: e.tensor_tensor(out=tq[:, 0:16], in0=tq[:, 0:16], in1=tq[:, 16:32], op=ALU.add), reads=[d_sm], writes=[d_sm])
            P.op(DVE, lambda e, tq=tq: e.tensor_tensor(out=tq[:, 0:16], in0=tq[:, 0:16], in1=tq[:, 32:48], op=ALU.add), reads=[d_sm], writes=[d_sm])
        P.op(DVE, lambda e: e.reciprocal(out=den[:, 0:16], in_=den[:, 0:16]), reads=[d_sm], writes=[d_sm])
        P.op(DVE, lambda e: e.tensor_tensor(out=attT[:, :, T:NT], in0=num[:, 0:16].rearrange("p (h b) -> p h b", h=4),
                                            in1=den[:, 0:16].rearrange("p (h b) -> p h b", h=4), op=ALU.mult), reads=[d_sm], writes=[d_att])

    _a2_att_end = _a2[0]
    _a2[0] = a2_after_his
    Sbf4_r = Ring([Buf(A2(512)) for i in range(2)])
    kdt4_r = Ring([Buf(A2(512)) for i in range(2)])
    am4_r = Ring([Buf(A2(512)) for i in range(2)])
    qd4 = A2(4 * 512).rearrange("p (h n) -> p h n", h=4)
    kd4 = A2(4 * 512).rearrange("p (h n) -> p h n", h=4)
    d_qd4 = [Dep(f"qd4{h}") for h in range(4)]
    d_kd4 = [Dep(f"kd4{h}") for h in range(4)]
    S4 = A2(512, F32)
    d_S = [Dep(f"S{h}") for h in range(4)]
    kd2_4 = A2(4 * 512).rearrange("p (h n) -> p h n", h=4)
    d_kd2 = [Dep(f"kd2{h}") for h in range(4)]
    s0_r = Ring([Buf(A2(128, F32)) for i in range(2)])
    s1_r = Ring([Buf(A2(128, F32)) for i in range(2)])
    ohg4_r = Ring([Buf(A2(512, F32)) for i in range(2)])
    hsm = A2(64, F32)
    d_hsm = Dep("hsm")
    ebe4 = A2(64, F32)
    d_ebe = [[Dep() for h in range(4)] for _ in range(2)]
    oo_banks = [ps_long, ps_long2]
    _a2[0] = max(_a2[0], _a2_att_end)

    def hgrn():
        wbq, wq = load_wchunk(w_in_v, 4608, 512)
        wbf, wf = load_wchunk(w_in_v, 5120, 512)
        for h in range(4):
            P.op(POOL, lambda e, h=h: e.memset(S4[:, h * 128:(h + 1) * 128], 0.0), reads=[], writes=[d_S[h]])
        sb0 = Sbf4_r.next()
        P.op(POOL, lambda e: e.memset(sb0.t, 0.0), writes=[sb0.d])
        cur = {"sbf": sb0, "pair": 0}
        for ti, (t0, n) in ALLTG:
            if ti < 4:
                hg_gates2((0, 1), ti, t0, n, wbq, wq, wbf, wf)
                hg_gates2((2, 3), ti, t0, n, wbq, wq, wbf, wf)
            else:
                for h in range(4):
                    g_ = hg_gates(h, ti, t0, n, wbq, wq, wbf, wf)
                    hg_sample(h, t0, g_[0], g_[1])
            if ti < 4:
                for pp in range(4):
                    hg_pair(ti, t0, pp, cur)
                    issue_bulk(2, [d_hgo])
            if ti == 3:
                for h in range(4):
                    P.dma(SP, lambda e, h=h: e.dma_start(out=phg[h], in_=S4[:, h * 128:(h + 1) * 128]), reads=[d_S[h]])
        out_deps.extend(d_S)
        issue_bulk(1000, [])

    def hg_gates2(hs_, ti, t0, n, wbq, wq, wbf, wf):
        par = ti % 2
        st = {}
        for h in hs_:
            hsl = slice(h * 128, (h + 1) * 128)
            pbq = ps()
            for c in range(8):
                P.op(PE, lambda e, c=c, pbq=pbq, hsl=hsl: e.matmul(pbq.t[:, :n], lhsT=wq[:, c, hsl], rhs=hT[:, c, t0:t0 + n], start=(c == 0), stop=(c == 7)),
                     reads=[wbq.ds[0], dh[ti]], writes=[pbq.d])
            qg = f32r.next()
            P.op(ACT, lambda e, qg=qg, pbq=pbq: e.activation(out=qg.t[:, :n], in_=pbq.t[:, :n], func=AF.Sigmoid), reads=[pbq.d], writes=[qg.d])
            pbf = ps()
            for c in range(8):
                P.op(PE, lambda e, c=c, pbf=pbf, hsl=hsl: e.matmul(pbf.t[:, :n], lhsT=wf[:, c, hsl], rhs=hT[:, c, t0:t0 + n], start=(c == 0), stop=(c == 7)),
                     reads=[wbf.ds[0], dh[ti]], writes=[pbf.d])
            kk = f32r.next()
            P.op(ACT, lambda e, kk=kk, pbf=pbf: e.activation(out=kk.t[:, :n], in_=pbf.t[:, :n], func=AF.Sigmoid), reads=[pbf.d], writes=[kk.d])
            st[h] = dict(qg=qg, kk=kk)
        for h in hs_:
            kk = st[h]["kk"]
            lf = f32r.next()
            st[h]["lf"] = lf
            P.op(ACT, lambda e, lf=lf, kk=kk, h=h: e.activation(out=lf.t[:, :], in_=kk.t[:, :], func=AF.Ln, bias=lb[:, h:h + 1], scale=oml[:, h:h + 1]),
                 reads=[kk.d, dc], writes=[lf.d])
            P.op(DVE, lambda e, kk=kk, h=h: e.tensor_scalar(out=kk.t[:, :n], in0=kk.t[:, :n], scalar1=oml[:, h:h + 1], scalar2=None, op0=ALU.mult),
                 reads=[kk.d, dc], writes=[kk.d])
        for h in hs_:
            lf = st[h]["lf"]
            P.op(DVE, lambda e, lf=lf: e.tensor_tensor_scan(out=lf.t[:, :], data0=rst, data1=lf.t[:, :], initial=zeroc, op0=ALU.mult, op1=ALU.add),
                 reads=[lf.d, dc], writes=[lf.d])
        for h in hs_:
            lf = st[h]["lf"]
            eb = f32r.next()
            st[h]["eb"] = eb
            P.op(ACT, lambda e, eb=eb, lf=lf: e.activation(out=eb.t[:, :], in_=lf.t[:, :], func=AF.Exp), reads=[lf.d], writes=[eb.d])
        for h in hs_:
            eb, qg = st[h]["eb"], st[h]["qg"]
            ev = ebe4[:, par * 32 + h * 8:par * 32 + h * 8 + 8]
            st[h]["ev"] = ev
            P.op(DVE, lambda e, ev=ev, eb=eb: e.tensor_copy(out=ev, in_=eb.t[:, 63:512:64]), reads=[eb.d], writes=[d_ebe[par][h]])
            P.op(DVE, lambda e, h=h, qg=qg, eb=eb: e.tensor_tensor(out=qd4[:, h, :], in0=qg.t[:, :], in1=eb.t[:, :], op=ALU.mult),
                 reads=[qg.d, eb.d], writes=[d_qd4[h]])
        for h in hs_:
            eb, lf = st[h]["eb"], st[h]["lf"]
            P.op(ACT, lambda e, eb=eb, lf=lf: e.activation(out=eb.t[:, :], in_=lf.t[:, :], func=AF.Exp, scale=-1.0), reads=[lf.d], writes=[eb.d])
        for h in hs_:
            eb, kk, ev = st[h]["eb"], st[h]["kk"], st[h]["ev"]
            P.op(DVE, lambda e, h=h, kk=kk, eb=eb: e.tensor_tensor(out=kd4[:, h, :], in0=kk.t[:, :], in1=eb.t[:, :], op=ALU.mult),
                 reads=[kk.d, eb.d], writes=[d_kd4[h]])
            P.op(DVE, lambda e, h=h, ev=ev: e.tensor_tensor(out=kd2_4[:, h, :].rearrange("p (c s) -> p c s", s=64), in0=kd4[:, h, :].rearrange("p (c s) -> p c s", s=64),
                                                     in1=ev.unsqueeze(2).to_broadcast([128, 8, 64]), op=ALU.mult),
                 reads=[d_kd4[h], d_ebe[par][h]], writes=[d_kd2[h]])

    def hg_gates(h, ti, t0, n, wbq, wq, wbf, wf):
        hs = slice(h * 128, (h + 1) * 128)
        pbq = ps()
        for c in range(8):
            P.op(PE, lambda e, c=c: e.matmul(pbq.t[:, :n], lhsT=wq[:, c, hs], rhs=hT[:, c, t0:t0 + n], start=(c == 0), stop=(c == 7)),
                 reads=[wbq.ds[0], dh[ti]], writes=[pbq.d])
        qg = f32r.next()
        P.op(ACT, lambda e: e.activation(out=qg.t[:, :n], in_=pbq.t[:, :n], func=AF.Sigmoid), reads=[pbq.d], writes=[qg.d])
        pbf = ps()
        for c in range(8):
            P.op(PE, lambda e, c=c: e.matmul(pbf.t[:, :n], lhsT=wf[:, c, hs], rhs=hT[:, c, t0:t0 + n], start=(c == 0), stop=(c == 7)),
                 reads=[wbf.ds[0], dh[ti]], writes=[pbf.d])
        kk = f32r.next()
        P.op(ACT, lambda e: e.activation(out=kk.t[:, :n], in_=pbf.t[:, :n], func=AF.Sigmoid), reads=[pbf.d], writes=[kk.d])
        P.op(DVE, lambda e: e.tensor_scalar(out=kk.t[:, :n], in0=kk.t[:, :n], scalar1=oml[:, h:h + 1], scalar2=None, op0=ALU.mult),
             reads=[kk.d, dc], writes=[kk.d])
        if ti == 4:
            return qg, kk
        par = ti % 2
        lf = f32r.next()
        P.op(ACT, lambda e: e.activation(out=lf.t[:, :], in_=kk.t[:, :], func=AF.Ln, bias=lb[:, h:h + 1], scale=1.0),
             reads=[kk.d, dc], writes=[lf.d])
        P.op(DVE, lambda e: e.tensor_tensor_scan(out=lf.t[:, :], data0=rst, data1=lf.t[:, :], initial=zeroc, op0=ALU.mult, op1=ALU.add),
             reads=[lf.d, dc], writes=[lf.d])
        eb = f32r.next()
        P.op(ACT, lambda e: e.activation(out=eb.t[:, :], in_=lf.t[:, :], func=AF.Exp), reads=[lf.d], writes=[eb.d])
        ev = ebe4[:, par * 32 + h * 8:par * 32 + h * 8 + 8]
        P.op(DVE, lambda e: e.tensor_copy(out=ev, in_=eb.t[:, 63:512:64]), reads=[eb.d], writes=[d_ebe[par][h]])
        P.op(DVE, lambda e: e.tensor_tensor(out=qd4[:, h, :], in0=qg.t[:, :], in1=eb.t[:, :], op=ALU.mult),
             reads=[qg.d, eb.d], writes=[d_qd4[h]])
        P.op(ACT, lambda e: e.activation(out=eb.t[:, :], in_=lf.t[:, :], func=AF.Exp, scale=-1.0), reads=[lf.d], writes=[eb.d])
        P.op(DVE, lambda e: e.tensor_tensor(out=kd4[:, h, :], in0=kk.t[:, :], in1=eb.t[:, :], op=ALU.mult),
             reads=[kk.d, eb.d], writes=[d_kd4[h]])
        P.op(DVE, lambda e: e.tensor_tensor(out=kd2_4[:, h, :].rearrange("p (c s) -> p c s", s=64), in0=kd4[:, h, :].rearrange("p (c s) -> p c s", s=64),
                                            in1=ev.unsqueeze(2).to_broadcast([128, 8, 64]), op=ALU.mult),
             reads=[d_kd4[h], d_ebe[par][h]], writes=[d_kd2[h]])
        return None

    def hg_pair(ti, t0, pp, cur):
        par = ti % 2
        tile = ti * 4 + pp
        c0 = pp * 128
        tp = ps()
        tpv = tp.t[:].bitcast(BF16)
        for h in range(4):
            P.op(PE, lambda e, h=h: e.transpose(out=tpv[:, h * 128:(h + 1) * 128], in_=kd2_4[:, h, c0:c0 + 128], identity=ident_b),
                 reads=[d_kd2[h], dc], writes=[tp.d])
        kdt = kdt4_r.next()
        P.op(ACT, lambda e: e.copy(out=kdt.t, in_=tpv[:, 0:512]), reads=[tp.d], writes=[kdt.d])
        at = ps()
        for h in range(4):
            P.op(PE, lambda e, h=h: e.matmul(at.t[:, h * 128:(h + 1) * 128], lhsT=kd4[:, h, c0:c0 + 128], rhs=qd4[:, h, c0:c0 + 128], start=True, stop=True),
                 reads=[d_kd4[h], d_qd4[h]], writes=[at.d])
        am = am4_r.next()
        P.op(DVE, lambda e: e.tensor_tensor(out=am.t, in0=at.t[:, :], in1=hmask4, op=ALU.mult), reads=[at.d, dc], writes=[am.d])
        oo = oo_banks[cur["pair"] % 2]
        cur["pair"] += 1
        P.op(DVE, lambda e: e.memset(oo.t[:, :], 0.0), reads=[], writes=[oo.d])
        for h in range(4):
            P.op(PE, lambda e, h=h: e.matmul(oo.t[:, h * 128:(h + 1) * 128], lhsT=vtok[:, tile, h * 128:(h + 1) * 128], rhs=am.t[:, h * 128:(h + 1) * 128],
                                             start=False, stop=True, skip_group_check=True), reads=[am.d, d_vtok], writes=[oo.d])
        for cc in range(2):
            base = cc * 64
            sbf = cur["sbf"]
            for h in range(4):
                P.op(PE, lambda e, h=h, sbf=sbf, base=base: e.matmul(oo.t[:, h * 128 + base:h * 128 + base + 64], lhsT=sbf.t[:, h * 128:(h + 1) * 128],
                                                                     rhs=qd4[:, h, c0 + base:c0 + base + 64], start=False, stop=True, skip_group_check=True),
                     reads=[sbf.d, d_qd4[h]], writes=[oo.d])
            su = ps()
            for h in range(4):
                P.op(PE, lambda e, h=h, su=su, base=base: e.matmul(su.t[:, h * 128:(h + 1) * 128], lhsT=kdt.t[base:base + 64, h * 128:(h + 1) * 128],
                                                                   rhs=vtok[base:base + 64, tile, h * 128:(h + 1) * 128], start=True, stop=True),
                     reads=[kdt.d, d_vtok], writes=[su.d])
            for h in range(4):
                ecol = ebe4[:, par * 32 + h * 8 + pp * 2 + cc:par * 32 + h * 8 + pp * 2 + cc + 1]
                P.op(DVE, lambda e, h=h, su=su, ecol=ecol: e.scalar_tensor_tensor(out=S4[:, h * 128:(h + 1) * 128], in0=S4[:, h * 128:(h + 1) * 128],
                                                                                scalar=ecol, in1=su.t[:, h * 128:(h + 1) * 128],
                                                                                op0=ALU.mult, op1=ALU.add),
                     reads=[su.d, d_ebe[par][h], d_S[h]], writes=[d_S[h]])
            nsb = Sbf4_r.next()
            P.op(ACT, lambda e, nsb=nsb: e.copy(out=nsb.t, in_=S4[:, :]), reads=d_S, writes=[nsb.d])
            cur["sbf"] = nsb
        ohg = ohg4_r.next()
        P.op(ACT, lambda e: e.copy(out=ohg.t[:, :], in_=oo.t[:, :]), reads=[oo.d], writes=[ohg.d])
        rs = rstd_of(lambda c: ohg.t[:, :], 512, ohg.d, nchunks=1, denom=128.0)
        for h in range(4):
            P.op(DVE, lambda e, h=h: e.scalar_tensor_tensor(out=hgoT[:, h, t0 + c0:t0 + c0 + 128], in0=ohg.t[:, h * 128:(h + 1) * 128], scalar=ghg[:, h:h + 1],
                                                            in1=rs.t[:, h * 128:(h + 1) * 128], op0=ALU.mult, op1=ALU.mult),
                 reads=[ohg.d, rs.d, dc], writes=[d_hgo])

    def hg_sample(h, t0, qg, kk):
        ohg = ohg4_r.next()
        fS = hsm[:, h * 4:h * 4 + 4]
        P.op(DVE, lambda e: e.tensor_scalar(out=fS, in0=kk.t[:, 0:NS], scalar1=lb[:, h:h + 1], scalar2=None, op0=ALU.add),
             reads=[kk.d, dc], writes=[d_hsm])
        osp = ps()
        for b in range(NS):
            hg_sample_b(h, b, qg, kk, fS, osp)
        P.op(ACT, lambda e: e.copy(out=ohg.t[:, 0:NS], in_=osp.t[:, 0:NS]), reads=[osp.d], writes=[ohg.d])
        rs = rstd_of(lambda c: ohg.t[:, 0:NS], NS, ohg.d, nchunks=1, denom=128.0)
        P.op(DVE, lambda e: e.scalar_tensor_tensor(out=hgoT[:, h, t0:t0 + NS], in0=ohg.t[:, 0:NS], scalar=ghg[:, h:h + 1],
                                                   in1=rs.t[:, 0:NS], op0=ALU.mult, op1=ALU.mult),
             reads=[ohg.d, rs.d, dc], writes=[d_hgo])

    def hg_sample_b(h, b, qg, kk, fS, osp):
        s0 = s0_r.next()
        P.dma(SP, lambda e: e.dma_start(out=s0.t[:, :], in_=st_in[b * 4 + h]), writes=[s0.d])
        vb = ps()
        P.op(PE, lambda e: e.matmul(vb.t[:, 0:128], lhsT=sel[:, b, :], rhs=his[:, h * 128:(h + 1) * 128], start=True, stop=True),
             reads=[dc, d_his], writes=[vb.d])
        P.op(POOL, lambda e: e.tensor_scalar(out=s0.t[:, :], in0=s0.t[:, :], scalar1=fS[:, b:b + 1], scalar2=None, op0=ALU.mult),
             reads=[s0.d, d_hsm], writes=[s0.d])
        s1 = s1_r.next()
        P.op(DVE, lambda e: e.scalar_tensor_tensor(out=s1.t[:, :], in0=vb.t[:, 0:128], scalar=kk.t[:, b:b + 1],
                                                   in1=s0.t[:, :], op0=ALU.mult, op1=ALU.add),
             reads=[vb.d, kk.d, s0.d], writes=[s1.d])
        P.dma(SP, lambda e: e.dma_start(out=shg[b * 4 + h], in_=s1.t[:, :]), reads=[s1.d])
        out_deps.append(s1.d)
        P.op(PE, lambda e: e.matmul(osp.t[:, b:b + 1], lhsT=s1.t[:, :], rhs=qg.t[:, b:b + 1], start=True, stop=True),
             reads=[s1.d, qg.d], writes=[osp.d])

    o5 = 4 * NT
    memT = arena[:, o5:o5 + 8 * 256 * 2].bitcast(F32).rearrange("p (c n) -> p c n", c=8); o5 += 8 * 256 * 2
    mhT = arena[:, o5:o5 + 8 * 256].rearrange("p (c n) -> p c n", c=8); o5 += 8 * 256
    o5 = scr0
    mkT = arena[:, o5:o5 + 4 * 256].rearrange("p (h n) -> p h n", h=4); o5 += 4 * 256
    mvb = arena[:, o5:o5 + 2 * 512].rearrange("p (c n) -> p c n", c=2); o5 += 2 * 512
    ptm = arena[:, o5:o5 + 2 * 512].rearrange("p (c n) -> p c n", c=2); o5 += 2 * 512
    mqT4 = arena[:, o5:o5 + 4 * 512].rearrange("p (h n) -> p h n", h=4); o5 += 4 * 512
    d_mqT4 = [Dep() for _ in range(4)]
    mks = arena[:, o5:o5 + 2 * 512].rearrange("p (c n) -> p c n", c=2); o5 += 1024
    mvs = arena[:, o5:o5 + 2 * 512].rearrange("p (c n) -> p c n", c=2); o5 += 1024
    mkTs = arena[:, o5:o5 + 4 * 256].rearrange("p (h n) -> p h n", h=4); o5 += 1024
    mem_end = o5
    d_memT, d_mhT, d_mkT, d_mvb, d_ptm = Dep(), Dep(), Dep(), Dep(), Dep()
    d_mks, d_mvs, d_mkTs = Dep(), Dep(), Dep()
    assert mem_end <= 8 * NT * 2, mem_end
    mqs = A2(4 * NS).rearrange("p (h n) -> p h n", h=4)
    d_mqs = Dep("mqs")
    msm = A2(64, F32)
    msb = A2(16)
    d_msm = Dep("msm")

    def mem_attention():
        for i in range(2):
            load_tokmajor_to_T(mem[i * 128:(i + 1) * 128, :], 128, memT, i * 128, d_memT)
        norm_to(memT, {0: d_memT}, gme, mhT, {0: d_mhT}, [(0, (0, 256))])
        w_mk_v = w_mk.rearrange("(c p) n -> p c n", p=128)
        w_mv_v = w_mv.rearrange("(c p) n -> p c n", p=128)
        wbk, wk = load_wchunk(w_mk_v, 0, 512)
        wbv, wv_ = load_wchunk(w_mv_v, 0, 512)
        for hh in range(4):
            pb = ps()
            for c in range(8):
                P.op(PE, lambda e, c=c, pb=pb, hh=hh: e.matmul(pb.t[:, 0:256], lhsT=wk[:, c, hh * 128:(hh + 1) * 128], rhs=mhT[:, c, :], start=(c == 0), stop=(c == 7)),
                     reads=[wbk.ds[0], d_mhT], writes=[pb.d])
            P.op(ACT, lambda e, pb=pb, hh=hh: e.copy(out=mkT[:, hh, :], in_=pb.t[:, 0:256]), reads=[pb.d], writes=[d_mkT])
        for (wb_, w_, dst, isv) in ((wbk, wk, pmk, False), (wbv, wv_, pmv, True)):
            for i in range(2):
                pb = ps()
                for c in range(8):
                    P.op(PE, lambda e, c=c, pb=pb, i=i, w_=w_: e.matmul(pb.t[:, :], lhsT=mhT[:, c, i * 128:(i + 1) * 128], rhs=w_[:, c, :], start=(c == 0), stop=(c == 7)),
                         reads=[wb_.ds[0], d_mhT], writes=[pb.d])
                sb = stage.next()
                P.op(ACT, lambda e, pb=pb, sb=sb: e.copy(out=sb.t[:, 0:512], in_=pb.t[:, :]), reads=[pb.d], writes=[sb.d])
                if isv:
                    P.op(DVE, lambda e, sb=sb, i=i: e.tensor_copy(out=mvb[:, i, :], in_=sb.t[:, 0:512]), reads=[sb.d], writes=[d_mvb])
                P.dma(SP, lambda e, dst=dst, sb=sb, i=i: e.dma_start(out=dst[i * 128:(i + 1) * 128, :], in_=sb.t[:, 0:512]), reads=[sb.d])
                out_deps.append(sb.d)
        wbq, wq = load_wchunk(w_in_v, 6144, 512)
        for ti, (t0, n) in ALLTG:
            for hh in range(4):
                mem_q(wbq, wq, hh, ti, t0, n)
            if ti < 4:
                for hh in range(4):
                    mem_unit(hh, t0)
        for b in range(NS):
            mem_sample(b)

    def mem_q(wbq, wq, hh, ti, t0, n):
        pb = ps()
        for c in range(8):
            P.op(PE, lambda e, c=c: e.matmul(pb.t[:, :n], lhsT=wq[:, c, hh * 128:(hh + 1) * 128], rhs=hT[:, c, t0:t0 + n], start=(c == 0), stop=(c == 7)),
                 reads=[wbq.ds[0], dh[ti]], writes=[pb.d])
        if ti == 4:
            P.op(DVE, lambda e: e.tensor_copy(out=mqs[:, hh, :], in_=pb.t[:, 0:NS]), reads=[pb.d], writes=[d_mqs])
        elif hh % 2 == 0:
            P.op(ACT, lambda e: e.copy(out=mqT4[:, hh, :], in_=pb.t[:, :]), reads=[pb.d], writes=[d_mqT4[hh]])
        else:
            P.op(DVE, lambda e: e.tensor_copy(out=mqT4[:, hh, :], in_=pb.t[:, :]), reads=[pb.d], writes=[d_mqT4[hh]])

    def mem_unit(hh, t0):
        scb = [ps(), ps()]
        for mc in range(2):
            P.op(PE, lambda e, mc=mc: e.matmul(scb[mc].t[:, :], lhsT=mkT[:, hh, mc * 128:(mc + 1) * 128], rhs=mqT4[:, hh, :], start=True, stop=True),
                 reads=[d_mkT, d_mqT4[hh]], writes=[scb[mc].d])
            P.op(ACT, lambda e, mc=mc: e.activation(out=ptm[:, mc, :], in_=scb[mc].t[:, :], func=AF.Exp, scale=SCALE),
                 reads=[scb[mc].d], writes=[d_ptm])
        om = ps(); dn = ps()
        for mc in range(2):
            P.op(PE, lambda e, mc=mc: e.matmul(om.t[:, :], lhsT=mvb[:, mc, hh * 128:(hh + 1) * 128], rhs=ptm[:, mc, :], start=(mc == 0), stop=(mc == 1)),
                 reads=[d_mvb, d_ptm], writes=[om.d])
        for mc in range(2):
            P.op(PE, lambda e, mc=mc: e.matmul(dn.t[:, :], lhsT=ones_b, rhs=ptm[:, mc, :], start=(mc == 0), stop=(mc == 1)),
                 reads=[dc, d_ptm], writes=[dn.d])
        rc = f32r.next()
        P.op(ACT, lambda e: e.activation(out=rc.t[:, :], in_=dn.t[:, :], func=AF.Ln), reads=[dn.d], writes=[rc.d])
        P.op(ACT, lambda e: e.activation(out=rc.t[:, :], in_=rc.t[:, :], func=AF.Exp, scale=-1.0), reads=[rc.d], writes=[rc.d])
        P.op(DVE, lambda e: e.tensor_tensor(out=memoT[:, hh, t0:t0 + 512], in0=om.t[:, :], in1=rc.t[:, :], op=ALU.mult),
             reads=[om.d, rc.d], writes=[d_memo])

    def mem_sample(b):
        if True:
            P.dma(POOL, lambda e, b=b: e.dma_start(out=mks, in_=cmk[b].rearrange("(c p) n -> p c n", p=128)), writes=[d_mks])
            P.dma(POOL, lambda e, b=b: e.dma_start(out=mvs, in_=cmv[b].rearrange("(c p) n -> p c n", p=128)), writes=[d_mvs])
            for mc in range(2):
                tp = ps()
                tpv = tp.t[:].bitcast(BF16)
                for hh in range(4):
                    P.op(PE, lambda e, hh=hh, mc=mc, tpv=tpv: e.transpose(out=tpv[:, hh * 128:(hh + 1) * 128], in_=mks[:, mc, hh * 128:(hh + 1) * 128], identity=ident_b),
                         reads=[d_mks, dc], writes=[tp.d])
                P.op(ACT, lambda e, mc=mc, tpv=tpv: e.copy(out=mkTs[:, :, mc * 128:(mc + 1) * 128], in_=tpv[:, 0:512].rearrange("p (h n) -> p h n", h=4)),
                     reads=[tp.d], writes=[d_mkTs])
            scm = ps()
            for hh in range(4):
                for mc in range(2):
                    col = hh * 2 + mc
                    P.op(PE, lambda e, hh=hh, mc=mc, col=col, b=b: e.matmul(scm.t[:, col:col + 1], lhsT=mkTs[:, hh, mc * 128:(mc + 1) * 128], rhs=mqs[:, hh, b:b + 1],
                                                                        start=True, stop=True), reads=[d_mkTs, d_mqs], writes=[scm.d])
            pms = msb[:, 0:8]
            P.op(ACT, lambda e: e.activation(out=pms, in_=scm.t[:, 0:8], func=AF.Exp, scale=SCALE), reads=[scm.d], writes=[d_msm])
            oms = ps()
            for hh in range(4):
                for mc in range(2):
                    col = hh * 2 + mc
                    P.op(PE, lambda e, hh=hh, mc=mc, col=col: e.matmul(oms.t[:, col:col + 1], lhsT=mvs[:, mc, hh * 128:(hh + 1) * 128], rhs=pms[:, col:col + 1],
                                                                   start=True, stop=True), reads=[d_mvs, d_msm], writes=[oms.d])
            P.op(PE, lambda e: e.matmul(oms.t[:, 16:24], lhsT=ones_b, rhs=pms, start=True, stop=True), reads=[dc, d_msm], writes=[oms.d])
            t8 = msm[:, 0:16]
            P.op(DVE, lambda e: e.tensor_copy(out=t8[:, 0:8], in_=oms.t[:, 0:8]), reads=[oms.d], writes=[d_msm])
            P.op(DVE, lambda e: e.tensor_copy(out=t8[:, 8:16], in_=oms.t[:, 16:24]), reads=[oms.d], writes=[d_msm])
            t4 = msm[:, 16:24]
            v8 = t8.rearrange("p (k h m) -> p k h m", k=2, h=4)
            P.op(DVE, lambda e: e.tensor_tensor(out=t4.rearrange("p (k h) -> p k h", k=2), in0=v8[:, :, :, 0], in1=v8[:, :, :, 1], op=ALU.add),
                 reads=[d_msm], writes=[d_msm])
            P.op(DVE, lambda e: e.reciprocal(out=t4[:, 4:8], in_=t4[:, 4:8]), reads=[d_msm], writes=[d_msm])
            P.op(DVE, lambda e, b=b: e.tensor_tensor(out=memoT[:, :, T + b], in0=t4[:, 0:4], in1=t4[:, 4:8], op=ALU.mult), reads=[d_msm], writes=[d_memo])

    mTa = arena[:, scr0:scr0 + 4 * NT].rearrange("p (c n) -> p c n", c=4)
    mTb = aarena[:, a2_keep:a2_keep + 4 * NT].rearrange("p (c n) -> p c n", c=4)
    assert a2_keep + 4 * NT <= NJ * 1028 and scr0 + 4 * NT <= 8 * NT * 2

    def mTv(c):
        return mTa[:, c, :] if c < 4 else mTb[:, c - 4, :]
    d_mT = [Dep(f"mT{i}") for i in range(5)]
    d_xd = Dep("xd")
    dxc = [Dep(f"xc{c}") for c in range(6)]
    dxw = [Dep(f"xw{i}") for i in range(5)]
    drl = [Dep(f"xrl{i}") for i in range(5)]
    d_xd2 = [[Dep() for _ in range(5)] for _ in range(8)]

    def merge():
        brs = ((attT, d_att, w_ba), (hgoT, d_hgo, w_bb), (memoT, d_memo, w_bc))
        gate_v = w_in_v[:, :, 6656:9728].rearrange("p c (i d n) -> p c i d n", i=3, d=8)
        w_o_v = w_o.rearrange("(c p) n -> p c n", p=128)
        for dp in range(8):
            merge_dp(brs, dp, gate_v)
        bdep = [d_att, d_att, d_hgo, d_hgo, d_memo, d_memo]
        for c in range(6):
            phase_switch([dxc[c]], [bdep[c]])
            P.dma(SP, lambda e, c=c: e.dma_start(out=xT[:, c, :], in_=xd[:, c, :]), reads=[d_xd], writes=[dxc[c]])
        for hf in range(2):
            merge_out(hf, w_o_v)

    def merge_dp(brs, dp, gate_v):
        wb = wring.next()
        gv = wb.t[:, 0:3072].rearrange("p (c i n) -> p c i n", c=8, i=3)
        for i in range(3):
            P.dma(POOL, lambda e, i=i: e.dma_start(out=gv[:, :, i, :], in_=gate_v[:, :, i, dp, :]), writes=(wb.ds if i == 0 else [wb.ds[i]]), skip_waw=(i > 0))
        bvs = []
        for i in range(3):
            bv = wb.t[:, 3072 + i * 512:3072 + (i + 1) * 512].rearrange("p (c n) -> p c n", c=4)
            P.dma(POOL, lambda e, bv=bv, i=i: e.dma_start(out=bv, in_=brs[i][2].rearrange("(c p) n -> p c n", p=128)[:, :, dp * 128:(dp + 1) * 128]),
                  writes=[wb.ds[3 + i]], skip_waw=True)
            bvs.append(bv)
        for ti, (t0, n) in ALLTG:
            merge_one(brs, wb, gv, bvs, ti, t0, n, macc_bufs[ti % 2], dp)

    def merge_one(brs, wb, gv, bvs, ti, t0, n, macc, dp):
        for i in range(3):
            pg = ps(); pbr = ps()
            for c in range(8):
                P.op(PE, lambda e, c=c, pg=pg, i=i: e.matmul(pg.t[:, :n], lhsT=gv[:, c, i, :], rhs=hT[:, c, t0:t0 + n], start=(c == 0), stop=(c == 7)),
                     reads=[wb.ds[i], dh[ti]], writes=[pg.d])
            for c in range(4):
                P.op(PE, lambda e, c=c, pbr=pbr, i=i: e.matmul(pbr.t[:, :n], lhsT=bvs[i][:, c, :], rhs=brs[i][0][:, c, t0:t0 + n], start=(c == 0), stop=(c == 3)),
                     reads=[wb.ds[3 + i], brs[i][1]], writes=[pbr.d])
            sg = f32r.next()
            P.op(ACT, lambda e, sg=sg, pg=pg: e.activation(out=sg.t[:, :n], in_=pg.t[:, :n], func=AF.Sigmoid), reads=[pg.d], writes=[sg.d])
            if i == 0:
                P.op(DVE, lambda e, sg=sg, pbr=pbr: e.tensor_tensor(out=macc.t[:, :n], in0=pbr.t[:, :n], in1=sg.t[:, :n], op=ALU.mult),
                     reads=[pbr.d, sg.d], writes=[macc.d])
            else:
                P.op(DVE, lambda e, sg=sg, pbr=pbr: e.tensor_tensor(out=sg.t[:, :n], in0=pbr.t[:, :n], in1=sg.t[:, :n], op=ALU.mult),
                     reads=[pbr.d, sg.d], writes=[sg.d])
                if i == 1:
                    P.op(DVE, lambda e, sg=sg: e.tensor_tensor(out=macc.t[:, :n], in0=macc.t[:, :n], in1=sg.t[:, :n], op=ALU.add),
                         reads=[sg.d, macc.d], writes=[macc.d])
                else:
                    P.op(DVE, lambda e, sg=sg: e.tensor_tensor(out=mTv(dp)[:, t0:t0 + n], in0=macc.t[:, :n], in1=sg.t[:, :n], op=ALU.add),
                         reads=[sg.d, macc.d], writes=[d_mT[ti]])

    def merge_out(hf, w_o_v):
        wbo, wo = load_wchunk(w_o_v, hf * 512, 512)
        for d4 in range(4):
            dpo = hf * 4 + d4
            for ti, (t0, n) in ALLTG:
                merge_out_one(wbo, wo, d4, dpo, ti, t0, n)

    def merge_out_one(wbo, wo, d4, dpo, ti, t0, n):
        po = ps()
        for c in range(8):
            P.op(PE, lambda e, c=c: e.matmul(po.t[:, :n], lhsT=wo[:, c, d4 * 128:(d4 + 1) * 128], rhs=mTv(c)[:, t0:t0 + n], start=(c == 0), stop=(c == 7)),
                 reads=[wbo.ds[0], d_mT[ti]], writes=[po.d])
        if dpo < 6:
            P.op(DVE, lambda e: e.tensor_tensor(out=xT[:, dpo, t0:t0 + n], in0=po.t[:, :n], in1=xT[:, dpo, t0:t0 + n], op=ALU.add),
                 reads=[po.d, dxc[dpo]], writes=[dxc[dpo], dxw[ti]])
            return
        xo = xo_r.next()
        P.dma(ACT, lambda e: e.dma_start(out=xo.t[:, :n], in_=xd[:, dpo, t0:t0 + n]), reads=[d_xd], writes=[xo.d])
        P.op(DVE, lambda e: e.tensor_tensor(out=xo.t[:, :n], in0=po.t[:, :n], in1=xo.t[:, :n], op=ALU.add),
             reads=[po.d, xo.d], writes=[xo.d])
        P.dma(SP, lambda e: e.dma_start(out=xd2[:, dpo, t0:t0 + n], in_=xo.t[:, :n]), reads=[xo.d], writes=[d_xd2[dpo][ti]])

    yT_r = [Buf(aarena[:, k_ * 8192:(k_ + 1) * 8192].bitcast(F32).rearrange("p (c n) -> p c n", c=8)) for k_ in range(2)]
    d_yT = [b_.d for b_ in yT_r]

    def final_prep(ti, t0, n):
        yb = yT_r[ti % 2]
        rs = rstd_of(lambda c: xT[:, c, t0:t0 + n], n, dx[ti])
        for c in range(8):
            P.op(DVE, lambda e, c=c: e.scalar_tensor_tensor(out=yb.t[:, c, 0:n], in0=xT[:, c, t0:t0 + n], scalar=gfi[:, c:c + 1], in1=rs.t[:, :n],
                                                            op0=ALU.mult, op1=ALU.mult), reads=[dx[ti], rs.d, dc], writes=[yb.d])

    def final_emit(ti, t0, n):
        yb = yT_r[ti % 2]
        ntile = (n + 127) // 128
        for tl in range(ntile):
            final_tile(yb, ti, t0, n, tl)

    def final_tile(yb, ti, t0, n, tl):
        m = min(128, n - tl * 128)
        sb = stage.next()
        for hf in range(2):
            pb = ps()
            for c4 in range(4):
                c = hf * 4 + c4
                P.op(PE, lambda e, c=c, c4=c4, pb=pb: e.transpose(out=pb.t[0:m, c4 * 128:(c4 + 1) * 128], in_=yb.t[:, c, tl * 128:tl * 128 + m], identity=ident_f),
                     reads=[yb.d, dc, d_id], writes=[pb.d])
            if hf == 0:
                P.op(ACT, lambda e, pb=pb: e.copy(out=sb.t[0:m, 0:512], in_=pb.t[0:m, :]), reads=[pb.d], writes=[sb.d])
            else:
                P.op(DVE, lambda e, pb=pb: e.tensor_copy(out=sb.t[0:m, 512:1024], in_=pb.t[0:m, :]), reads=[pb.d], writes=[sb.d])
        if ti < 4:
            r0 = t0 + tl * 128
            P.dma(SP, lambda e: e.dma_start(out=y_p[r0:r0 + 128, :], in_=sb.t[:, :]), reads=[sb.d])
        else:
            P.dma(SP, lambda e: e.dma_start(out=y_s[:, :], in_=sb.t[0:NS, :]), reads=[sb.d])
        out_deps.append(sb.d)

    def final_out():
        final_prep(0, *TG[0])
        for ti, (t0, n) in ALLTG:
            if ti + 1 < 5:
                final_prep(ti + 1, *TG[ti + 1])
            final_emit(ti, t0, n)

    def phase_switch(new_deps, old_deps):
        for nd in new_deps:
            for od in old_deps:
                if od.w is not None:
                    s, k = od.w
                    if nd.r.get(s, -1) < k:
                        nd.r[s] = k
                for s, k in od.r.items():
                    if nd.r.get(s, -1) < k:
                        nd.r[s] = k

    consts()
    for i in range(16):
        load_tokmajor_to_T(x_p[i * 128:(i + 1) * 128, :], 128, xT, i * 128, dx[i // 4])
    load_tokmajor_to_T(x_s[:, :], NS, xT, T, dx[4])
    consts_late()
    KST = 9
    if KST >= 1:
        ffn(gf1, w1g, w1u, w1d)
    if KST >= 2:
        norm_to(xT, dx, gmx, hT, dh, ALLTG)
    if KST >= 2:
        for ti, (t0, n) in ALLTG:
            P.dma(SP, lambda e, t0=t0, n=n: e.dma_start(out=xd[:, :, t0:t0 + n], in_=xT[:, :, t0:t0 + n]), reads=[dx[ti]], writes=[d_xd])
        a2_small = ([d_his, d_zs, d_sm, d_hsm, d_mqs, d_msm] + d_S + d_qd4 + d_kd4 + d_kd2 + d_ebe[0] + d_ebe[1] +
                   [d_ for r_ in (ptr, ks_r, kts_r, Sbf4_r, kdt4_r, am4_r, s0_r, s1_r, ohg4_r) for b in r_.bufs for d_ in b.ds] + d_kd2)
        a2_deps = a2_small + [b.d for b in macc_bufs] + [b.d for b in xo_r.bufs]
        phase_switch(a2_deps, dact)
        att_scr = [d_acc, d_att] + d_vs + [st_[k_] for st_ in asets for k_ in ('dq', 'dk', 'dv', 'dvT')]
        phase_switch(att_scr, dx)
        attention_prompt()
        phase_switch(d_vs, [asets[1][k_] for k_ in ('dq', 'dk', 'dv', 'dvT')])
        attention_sample()
        mem_scr = [d_memT, d_mhT, d_mkT, d_mvb, d_ptm, d_mks, d_mvs, d_mkTs, d_memo] + d_mqT4
        phase_switch(mem_scr, att_scr + dx)
        mem_attention()
        hg_scr = [d_vtok, d_hgo]
        phase_switch(hg_scr, att_scr + mem_scr + dx)
        a2_att = [d_zs, d_sm] + [d_ for r_ in (ptr, ks_r, kts_r) for b in r_.bufs for d_ in b.ds]
        a2_hg = (d_S + d_qd4 + d_kd4 + d_kd2 + d_ebe[0] + d_ebe[1] + [d_hsm] +
                 [d_ for r_ in (Sbf4_r, kdt4_r, am4_r, s0_r, s1_r, ohg4_r) for b in r_.bufs for d_ in b.ds])
        phase_switch(a2_hg, a2_att)
        tokmajor_pass()
        hgrn()
        phase_switch(d_mT, hg_scr + att_scr + mem_scr + dx + a2_small)
        merge()
        allmix = d_mT + hg_scr + att_scr + mem_scr + [d_att, d_hgo, d_memo]
        phase_switch(dx, allmix)
        phase_switch(drl, allmix)
        for ti, (t0, n) in ALLTG:
            P.dma(SP, lambda e, t0=t0, n=n: e.dma_start(out=xT[:, 6:8, t0:t0 + n], in_=xd2[:, 6:8, t0:t0 + n]),
                  reads=[d_xd2[6][ti], d_xd2[7][ti]], writes=[drl[ti]])
            P.op(DVE, lambda e, ti=ti: e.tensor_copy(out=vec[:, 58 + ti:59 + ti], in_=vec[:, 57:58]), reads=[dxw[ti], drl[ti], dc], writes=[dx[ti]])
        phase_switch(dact, a2_deps)
    if KST >= 3:
        ffn(gf2, w2g, w2u, w2d)
    phase_switch(d_yT, dact)
    final_out()
    P.op(SP, lambda e: e.nop(), reads=[], writes=out_deps)
    P.emit()
    return nc


_NC = None


def _prep(inp):
    f = lambda a: np.ascontiguousarray(np.asarray(a, dtype=np.float32))
    in_maps = []
    shared = {}
    for k in ("g_ff1", "g_mix", "g_mem", "g_ff2", "g_hg_out", "w_ff1_gate", "w_ff1_up", "w_ff1_down", "w_ff2_gate", "w_ff2_up",
              "w_ff2_down", "w_in", "w_mem_k", "w_mem_v", "w_branch_att", "w_branch_hg", "w_branch_mem", "w_out"):
        shared[k] = f(inp[k][0])
    shared["g_final"] = f(inp["g_final"])
    shared["hg_lb_logits"] = f(inp["hg_lb_logits"])
    cw = {}
    for g, nm in enumerate(("1", "4", "16")):
        cw[(g, "k")] = np.asarray(inp[f"cache_win{nm}_k"])[0]
        cw[(g, "v")] = np.asarray(inp[f"cache_win{nm}_v"])[0]
    for i in range(8):
        m = dict(shared)
        m["x_p"] = f(inp["x_prompt"][i])
        m["x_s"] = f(np.asarray(inp["x_sample"])[4 * i:4 * i + 4, 0, :])
        m["mem"] = f(inp["mem_prompt"][i])
        for g in range(3):
            W = WIN[g][0]
            m[f"cw{g}k"] = f(cw[(g, "k")][4 * i:4 * i + 4].reshape(4, W, 512))
            m[f"cw{g}v"] = f(cw[(g, "v")][4 * i:4 * i + 4].reshape(4, W, 512))
        m["cmk"] = f(np.asarray(inp["cache_mem_k"])[0, 4 * i:4 * i + 4].reshape(4, 256, 512))
        m["cmv"] = f(np.asarray(inp["cache_mem_v"])[0, 4 * i:4 * i + 4].reshape(4, 256, 512))
        m["st"] = f(np.asarray(inp["state_hgrn"])[0, 4 * i:4 * i + 4].reshape(16, 128, 128))
        in_maps.append(m)
    return in_maps


def kernel(**inp):
    global _NC
    if _NC is None:
        _NC = build_nc()
    nc = _NC
    in_maps = _prep(inp)
    res = run_bass_kernel_spmd(nc, in_maps, core_ids=list(range(8)))
    return _assemble(res.results, 8)


def _assemble(R, ncore):
    cat = lambda k: np.stack([np.asarray(R[i][k], dtype=np.float32) for i in range(ncore)], axis=0)
    y_p = cat("y_p")
    y_s = cat("y_s").reshape(4 * ncore, 1, D)
    outs = [y_p, y_s]
    for g in range(3):
        W = WIN[g][0]
        outs.append(cat(f"pw{g}k").reshape(1, ncore, W, 4, 128))
        outs.append(cat(f"pw{g}v").reshape(1, ncore, W, 4, 128))
    outs.append(cat("pmk").reshape(1, ncore, 256, 4, 128))
    outs.append(cat("pmv").reshape(1, ncore, 256, 4, 128))
    outs.append(cat("phg").reshape(1, ncore, 4, 128, 128))
    for g in range(3):
        W = WIN[g][0]
        outs.append(cat(f"sw{g}k").reshape(1, 4 * ncore, W, 4, 128))
        outs.append(cat(f"sw{g}v").reshape(1, 4 * ncore, W, 4, 128))
    outs.append(cat("shg").reshape(1, 4 * ncore, 4, 128, 128))
    return tuple(outs)
```

```python
import numpy as np
import concourse.bass as bass
import concourse.mybir as mybir
from concourse.bass_utils import run_bass_kernel_spmd

F32 = mybir.dt.float32
BF16 = mybir.dt.bfloat16
ALU = mybir.AluOpType
AF = mybir.ActivationFunctionType
AX = mybir.AxisListType

PE, ACT, DVE, POOL, SP = "pe", "act", "dve", "pool", "sp"
COMPUTE = (PE, ACT, DVE, POOL, SP)


class Dep:
    __slots__ = ("w", "r", "name")

    def __init__(self, name=""):
        self.w = None
        self.r = {}
        self.name = name


class Prog:
    def __init__(self, nc, lanes=None):
        self.nc = nc
        self.lanes = lanes or {SP: 12, POOL: 12, ACT: 8, "bulk": 24}
        self.ops = {e: [] for e in COMPUTE}
        self.stream_ops = {}
        self.lane_rr = {q: 0 for q in self.lanes}
        self.waited = {e: {} for e in COMPUTE}

    def _stream_list(self, s):
        return self.stream_ops.setdefault(s, [])

    def _collect(self, reads, writes, own_stream, skip_waw=False):
        deps = {}

        def need(p):
            if p is None:
                return
            s, k = p
            if s == own_stream and s == PE:
                return
            if deps.get(s, -1) < k:
                deps[s] = k
        for d in reads:
            need(d.w)
        for d in writes:
            if not skip_waw:
                need(d.w)
            for s, k in d.r.items():
                need((s, k))
        return deps

    def op(self, eng, fn, reads=(), writes=()):
        lst = self._stream_list(eng)
        idx = len(lst)
        deps = self._collect(reads, writes, eng)
        waits = []
        wd = self.waited[eng]
        for s, k in deps.items():
            if s == eng and k >= idx:
                continue
            if wd.get(s, -1) < k:
                waits.append((s, k))
                wd[s] = k
        rec = dict(eng=eng, stream=eng, idx=idx, fn=fn, waits=waits, sig=False, dma=False)
        lst.append(rec)
        self.ops[eng].append(rec)
        for d in reads:
            if d.r.get(eng, -1) < idx:
                d.r[eng] = idx
        for d in writes:
            d.w = (eng, idx)
            d.r = {}
        return rec

    def dma(self, q, fn, reads=(), writes=(), pool=None, skip_waw=False):
        pool = pool or q
        nl = self.lanes[pool]
        lane = ("dma", pool, self.lane_rr[pool] % nl)
        self.lane_rr[pool] += 1
        lst = self._stream_list(lane)
        idx = len(lst)
        deps = self._collect(reads, writes, lane, skip_waw)
        if idx > 0:
            deps[lane] = max(deps.get(lane, -1), idx - 1)
        waits = []
        wd = self.waited[q]
        for s, k in deps.items():
            if wd.get(s, -1) < k:
                waits.append((s, k))
                wd[s] = k
        rec = dict(eng=q, stream=lane, idx=idx, fn=fn, waits=waits, sig=True, dma=True)
        lst.append(rec)
        self.ops[q].append(rec)
        for d in reads:
            if d.r.get(lane, -1) < idx:
                d.r[lane] = idx
        for d in writes:
            d.w = (lane, idx)
            d.r = {}
        return rec

    def wait_all(self, eng, deps_list):
        def fn(e):
            return e.nop()
        return self.op(eng, fn, reads=deps_list, writes=())

    def emit(self):
        nc = self.nc
        for e in COMPUTE:
            for rec in self.ops[e]:
                for s, k in rec["waits"]:
                    self.stream_ops[s][k]["sig"] = True
        semval = {}
        for s, lst in self.stream_ops.items():
            c = 0
            vals = []
            inc = 16 if isinstance(s, tuple) else 1
            for rec in lst:
                if rec["sig"]:
                    c += inc
                vals.append(c)
            semval[s] = vals
        streams = [s for s in self.stream_ops if any(r["sig"] for r in self.stream_ops[s])]
        import contextlib
        with contextlib.ExitStack() as st:
            sems = {}
            for i, s in enumerate(streams):
                nm = "s_" + (s if isinstance(s, str) else f"{s[1]}{s[2]}")
                sems[s] = st.enter_context(nc.semaphore(nm))
            block = st.enter_context(nc.Block())

            def run(eng_name):
                def body(e):
                    for rec in self.ops[eng_name]:
                        for s, k in rec["waits"]:
                            e.wait_ge(sems[s], semval[s][k])
                        inst = rec["fn"](e)
                        if rec["sig"]:
                            inst.then_inc(sems[rec["stream"]], 16 if rec["dma"] else 1)
                return body
            block.tensor(run(PE))
            block.scalar(run(ACT))
            block.vector(run(DVE))
            block.gpsimd(run(POOL))
            block.sync(run(SP))
        return len(streams)

from contextlib import ExitStack

T = 2048
NS = 4
NT = T + NS
D = 1024
DFF = 2816
NJ = DFF // 128
INW = 9728
TG = [(0, 512), (512, 512), (1024, 512), (1536, 512), (2048, 4)]
WIN = [(128, 1), (512, 4), (2048, 16)]
SCALE = 128.0 ** -0.5
EPS = 1e-6
ARENA = 30720


class Buf:
    def __init__(self, t, nd=1):
        self.t = t
        self.ds = [Dep() for _ in range(nd)]

    @property
    def d(self):
        return self.ds[0]


class Ring:
    def __init__(self, bufs):
        self.bufs = bufs
        self.i = 0

    def next(self):
        b = self.bufs[self.i % len(self.bufs)]
        self.i += 1
        return b


def build_nc():
    nc = bass.Bass("TRN2", target_bir_lowering=False)
    P = Prog(nc)

    def din(name, shape):
        return nc.dram_tensor(name, list(shape), F32, kind="ExternalInput").ap()

    def dout(name, shape):
        return nc.dram_tensor(name, list(shape), F32, kind="ExternalOutput").ap()

    x_p = din("x_p", [T, D]); x_s = din("x_s", [NS, D]); mem = din("mem", [256, D])
    cwk = [din(f"cw{g}k", [NS, WIN[g][0], 512]) for g in range(3)]
    cwv = [din(f"cw{g}v", [NS, WIN[g][0], 512]) for g in range(3)]
    cmk = din("cmk", [NS, 256, 512]); cmv = din("cmv", [NS, 256, 512])
    st_in = din("st", [NS * 4, 128, 128])
    g_ff1 = din("g_ff1", [D]); g_mix = din("g_mix", [D]); g_mem = din("g_mem", [D]); g_ff2 = din("g_ff2", [D])
    g_fin = din("g_final", [D]); g_hg = din("g_hg_out", [512]); lbl = din("hg_lb_logits", [2, 512])
    w1g = din("w_ff1_gate", [D, DFF]); w1u = din("w_ff1_up", [D, DFF]); w1d = din("w_ff1_down", [DFF, D])
    w2g = din("w_ff2_gate", [D, DFF]); w2u = din("w_ff2_up", [D, DFF]); w2d = din("w_ff2_down", [DFF, D])
    w_in = din("w_in", [D, INW]); w_mk = din("w_mem_k", [D, 512]); w_mv = din("w_mem_v", [D, 512])
    w_ba = din("w_branch_att", [512, D]); w_bb = din("w_branch_hg", [512, D]); w_bc = din("w_branch_mem", [512, D])
    w_o = din("w_out", [D, D])

    y_p = dout("y_p", [T, D]); y_s = dout("y_s", [NS, D])
    pwk = [dout(f"pw{g}k", [WIN[g][0], 512]) for g in range(3)]
    pwv = [dout(f"pw{g}v", [WIN[g][0], 512]) for g in range(3)]
    pmk = dout("pmk", [256, 512]); pmv = dout("pmv", [256, 512]); phg = dout("phg", [4, 128, 128])
    swk = [dout(f"sw{g}k", [NS, WIN[g][0], 512]) for g in range(3)]
    swv = [dout(f"sw{g}v", [NS, WIN[g][0], 512]) for g in range(3)]
    shg = dout("shg", [NS * 4, 128, 128])

    out_deps = []

    xT = nc.alloc_sbuf_tensor("xT", [128, 8, NT], F32)
    hT = nc.alloc_sbuf_tensor("hT", [128, 8, NT], BF16)
    dx = [Dep(f"x{i}") for i in range(5)]
    dh = [Dep(f"h{i}") for i in range(5)]
    aarena = nc.alloc_sbuf_tensor("aarena", [128, NJ * 1028], BF16)
    arena = xT[:, :, :].rearrange("p c n -> p (c n)").bitcast(BF16)
    xd = nc.dram_tensor("xd_scratch", [128, 8, NT], F32).ap()
    xd2 = nc.dram_tensor("xd2_scratch", [128, 8, NT], F32).ap()
    _a2 = [0]

    def A2(nel, dt=BF16, parts=128):
        o_ = _a2[0]
        n2 = nel * (2 if dt == F32 else 1)
        _a2[0] = o_ + (n2 + 15) // 16 * 16
        assert _a2[0] <= NJ * 1028, _a2[0]
        _a2.append(_a2[0])
        v = aarena[0:parts, o_:o_ + n2]
        return v.bitcast(F32) if dt == F32 else v
    macc_bufs = [Buf(A2(512, F32)) for _ in range(2)]
    xo_r = Ring([Buf(A2(512, F32)) for _ in range(3)])
    a2_keep = _a2[0]
    wring = Ring([Buf(nc.alloc_sbuf_tensor(f"wp{i}", [128, 4608], BF16), 6) for i in range(3)])
    stage = Ring([Buf(nc.alloc_sbuf_tensor(f"stg{i}", [128, 1024], F32)) for i in range(2)])
    f32r = Ring([Buf(nc.alloc_sbuf_tensor(f"f32r{i}", [128, 512], F32)) for i in range(8)])
    bfr = Ring([Buf(nc.alloc_sbuf_tensor(f"bfr{i}", [128, 512], BF16)) for i in range(3)])
    psr = Ring([Buf(nc.alloc_psum_tensor(f"ps{i}", [128, 512], F32)) for i in range(6)])
    ps_long = Buf(nc.alloc_psum_tensor("pslong", [128, 512], F32))
    ps_long2 = Buf(nc.alloc_psum_tensor("pslong2", [128, 512], F32))
    cst = nc.alloc_sbuf_tensor("cst", [128, 1280], F32)
    cbf = nc.alloc_sbuf_tensor("cbf", [128, 1280], BF16)
    vec = nc.alloc_sbuf_tensor("vec", [128, 64], F32)
    dc = Dep("const")
    d_id = Dep("ident")

    ident_f = cst[:, 0:128]; ones_f = cst[:, 128:256]; rst = cst[:, 256:768]
    sel = cst[0:4, 768:768 + 512].rearrange("p (b n) -> p b n", b=4)
    ident_b = cbf[:, 0:128]; ones_b = cbf[:, 128:256]
    mask2 = cbf[:, 256:512]
    hmask = cbf[:, 512:640]
    hmask4 = cbf[:, 640:1152]
    gf1 = vec[:, 0:8]; gmx = vec[:, 8:16]; gme = vec[:, 16:24]; gf2 = vec[:, 24:32]; gfi = vec[:, 32:40]
    ghg = vec[:, 40:44]; lb = vec[:, 44:48]; oml = vec[:, 48:52]; l1 = vec[:, 52:56]
    epsc = vec[:, 56:57]; zeroc = vec[:, 57:58]

    def ps():
        return psr.next()

    def consts():
        tmp = f32r.next()
        P.op(POOL, lambda e: e.memset(cst[:], 1.0), writes=[dc, d_id])
        P.op(POOL, lambda e: e.memset(ident_f, 0.0), writes=[d_id])
        P.op(POOL, lambda e: e.affine_select(out=ident_f, in_=ident_f, compare_op=ALU.not_equal, fill=1.0,
                                             base=0, pattern=[[-1, 128]], channel_multiplier=1), writes=[d_id])
        P.op(POOL, lambda e: e.memset(cst[:, 256:768].rearrange("p (c n) -> p c n", n=64)[:, :, 0:1], 0.0), writes=[dc])
        P.op(POOL, lambda e: e.affine_select(out=cst[0:4, 768:1280].rearrange("p (b n) -> p b n", b=4),
                                             in_=cst[0:4, 768:1280].rearrange("p (b n) -> p b n", b=4),
                                             compare_op=ALU.is_equal, fill=0.0, base=0,
                                             pattern=[[-1, 4], [0, 128]], channel_multiplier=1), writes=[dc])
        t = tmp.t
        P.op(POOL, lambda e: e.memset(t[:, 0:384], 1.0), writes=[tmp.d])
        P.op(POOL, lambda e: e.affine_select(out=t[:, 0:128], in_=t[:, 0:128], compare_op=ALU.is_ge, fill=0.0,
                                             base=0, pattern=[[1, 128]], channel_multiplier=-1), writes=[tmp.d])
        P.op(POOL, lambda e: e.affine_select(out=t[:, 128:256], in_=t[:, 128:256], compare_op=ALU.is_ge, fill=0.0,
                                             base=0, pattern=[[-1, 128]], channel_multiplier=1), writes=[tmp.d])
        P.op(POOL, lambda e: e.tensor_copy(out=t[:, 256:384], in_=t[:, 0:128]), writes=[tmp.d])
        P.op(POOL, lambda e: e.memset(t[0:64, 320:384], 0.0), writes=[tmp.d])
        P.op(POOL, lambda e: e.tensor_copy(out=cbf[:, 256:640], in_=t[:, 0:384]), reads=[tmp.d], writes=[dc])
        for j_ in range(4):
            P.op(POOL, lambda e, j_=j_: e.tensor_copy(out=cbf[:, 640 + j_ * 128:768 + j_ * 128], in_=t[:, 256:384]), reads=[tmp.d], writes=[dc])
        P.op(POOL, lambda e: e.tensor_copy(out=cbf[:, 0:256], in_=cst[:, 0:256]), reads=[d_id], writes=[dc])
        P.op(POOL, lambda e: e.memset(vec[:, 56:57], EPS), writes=[dc])
        P.op(POOL, lambda e: e.memset(vec[:, 57:58], 0.0), writes=[dc])
        dvec = []
        for dst, src, n in ((gf1, g_ff1, 8), (gmx, g_mix, 8), (gme, g_mem, 8), (gf2, g_ff2, 8), (gfi, g_fin, 8), (ghg, g_hg, 4)):
            dd_ = Dep()
            dvec.append(dd_)
            P.dma(ACT, lambda e, dst=dst, src=src: e.dma_start(out=dst, in_=src.rearrange("(c p) -> p c", p=128),
                                                                allow_slow_non_contiguous=True), writes=[dd_])
        dlb, dl1 = Dep(), Dep()
        P.dma(ACT, lambda e: e.dma_start(out=lb, in_=lbl[0].rearrange("(c p) -> p c", p=128), allow_slow_non_contiguous=True), writes=[dlb])
        P.dma(ACT, lambda e: e.dma_start(out=l1, in_=lbl[1].rearrange("(c p) -> p c", p=128), allow_slow_non_contiguous=True), writes=[dl1])
        late["deps"] = dvec + [dlb, dl1]

    late = {}

    def consts_late():
        dvec_all = late["deps"]
        P.op(DVE, lambda e: e.tensor_copy(out=vec[:, 63:64], in_=vec[:, 57:58]), reads=dvec_all + [dc], writes=[dc])
        P.op(DVE, lambda e: e.tensor_tensor(out=lb, in0=lb, in1=l1, op=ALU.subtract), reads=[dc], writes=[dc])
        P.op(ACT, lambda e: e.activation(out=lb, in_=lb, func=AF.Sigmoid), reads=[dc], writes=[dc])
        P.op(DVE, lambda e: e.tensor_scalar(out=oml, in0=lb, scalar1=-1.0, scalar2=1.0, op0=ALU.mult, op1=ALU.add), reads=[dc], writes=[dc])

    def load_tokmajor_to_T(src_ap, nrows, dstT, tok0, ddst):
        sb = stage.next()
        P.dma(SP, lambda e: e.dma_start(out=sb.t[0:nrows, :], in_=src_ap), writes=[sb.d])
        for hf in range(2):
            pb = ps()
            for c4 in range(4):
                c = hf * 4 + c4
                P.op(PE, lambda e, c=c, c4=c4, pb=pb: e.transpose(out=pb.t[:, c4 * 128:c4 * 128 + nrows],
                                                           in_=sb.t[0:nrows, c * 128:(c + 1) * 128], identity=ident_f[0:nrows, 0:nrows]),
                     reads=[sb.d, d_id], writes=[pb.d])
            eng = ACT if hf == 0 else DVE
            if eng == ACT:
                P.op(ACT, lambda e, hf=hf, pb=pb: e.copy(out=dstT[:, hf * 4:hf * 4 + 4, tok0:tok0 + nrows],
                                                  in_=pb.t[:].rearrange("p (c n) -> p c n", c=4)[:, :, 0:nrows]),
                     reads=[pb.d], writes=[ddst])
            else:
                P.op(DVE, lambda e, hf=hf, pb=pb: e.tensor_copy(out=dstT[:, hf * 4:hf * 4 + 4, tok0:tok0 + nrows],
                                                         in_=pb.t[:].rearrange("p (c n) -> p c n", c=4)[:, :, 0:nrows]),
                     reads=[pb.d], writes=[ddst])

    def rstd_of(srcT_fn, n, dsrc, nchunks=8, denom=1024.0):
        pb = ps()
        for c in range(nchunks):
            sb = bfr.next()
            P.op(ACT, lambda e, c=c, sb=sb: e.activation(out=sb.t[:, :n], in_=srcT_fn(c), func=AF.Square),
                 reads=[dsrc], writes=[sb.d])
            P.op(PE, lambda e, c=c, sb=sb: e.matmul(pb.t[:, :n], lhsT=ones_b, rhs=sb.t[:, :n], start=(c == 0), stop=(c == nchunks - 1)),
                 reads=[sb.d, dc], writes=[pb.d])
        rs = f32r.next()
        P.op(ACT, lambda e: e.activation(out=rs.t[:, :n], in_=pb.t[:, :n], func=AF.Ln, bias=epsc, scale=1.0 / denom),
             reads=[pb.d, dc], writes=[rs.d])
        P.op(ACT, lambda e: e.activation(out=rs.t[:, :n], in_=rs.t[:, :n], func=AF.Exp, scale=-0.5), reads=[rs.d], writes=[rs.d])
        return rs

    def norm_to(srcT, dsrcs, gcol, dstT, ddsts, tgs):
        for ti, (t0, n) in tgs:
            rs = rstd_of(lambda c, t0=t0, n=n: srcT[:, c, t0:t0 + n], n, dsrcs[ti])
            for c in range(8):
                P.op(DVE, lambda e, c=c, t0=t0, n=n, rs=rs: e.scalar_tensor_tensor(
                    out=dstT[:, c, t0:t0 + n], in0=srcT[:, c, t0:t0 + n], scalar=gcol[:, c:c + 1], in1=rs.t[:, :n],
                    op0=ALU.mult, op1=ALU.mult), reads=[dsrcs[ti], rs.d, dc], writes=[ddsts[ti]])

    ALLTG = list(enumerate(TG))

    act = aarena[:, 0:NJ * 1028].rearrange("p (j n) -> p j n", j=NJ)
    dact = [Dep("act0"), Dep("act1"), Dep("act2")]

    def ffn(gcol, wg, wu, wd):
        norm_to(xT, dx, gcol, hT, dh, ALLTG)
        wg_v = wg.rearrange("(c p) n -> p c n", p=128)
        wu_v = wu.rearrange("(c p) n -> p c n", p=128)
        wd_v = wd.rearrange("(j p) n -> p j n", p=128)
        for half in ([0, 1], [2, 3, 4]):
            for j in range(NJ):
                if j % 2 == 0:
                    wb = wring.next()
                    gv2 = wb.t[:, 0:2048].rearrange("p (c n) -> p c n", c=8)
                    uv2 = wb.t[:, 2048:4096].rearrange("p (c n) -> p c n", c=8)
                    P.dma(POOL, lambda e, gv2=gv2, j=j: e.dma_start(out=gv2, in_=wg_v[:, :, j * 128:(j + 2) * 128]), writes=wb.ds)
                    P.dma(POOL, lambda e, uv2=uv2, j=j: e.dma_start(out=uv2, in_=wu_v[:, :, j * 128:(j + 2) * 128]), writes=[wb.ds[1]], skip_waw=True)
                gv = gv2[:, :, (j % 2) * 128:(j % 2 + 1) * 128]
                uv = uv2[:, :, (j % 2) * 128:(j % 2 + 1) * 128]
                for si, ti in enumerate(half):
                    t0, n = TG[ti]
                    off = si * 512
                    pg = ps(); pu = ps()
                    for c in range(8):
                        P.op(PE, lambda e, c=c, gv=gv, pg=pg, t0=t0, n=n: e.matmul(pg.t[:, :n], lhsT=gv[:, c, :], rhs=hT[:, c, t0:t0 + n],
                                                                                  start=(c == 0), stop=(c == 7)),
                             reads=[wb.ds[0], dh[ti]], writes=[pg.d])
                    for c in range(8):
                        P.op(PE, lambda e, c=c, uv=uv, pu=pu, t0=t0, n=n: e.matmul(pu.t[:, :n], lhsT=uv[:, c, :], rhs=hT[:, c, t0:t0 + n],
                                                                                  start=(c == 0), stop=(c == 7)),
                             reads=[wb.ds[1], dh[ti]], writes=[pu.d])
                    sg = f32r.next()
                    P.op(ACT, lambda e, sg=sg, pg=pg, n=n: e.activation(out=sg.t[:, :n], in_=pg.t[:, :n], func=AF.Silu),
                         reads=[pg.d], writes=[sg.d])
                    P.op(DVE, lambda e, sg=sg, pu=pu, n=n, j=j, off=off: e.tensor_tensor(out=act[:, j, off:off + n], in0=pu.t[:, :n],
                                                                                       in1=sg.t[:, :n], op=ALU.mult),
                         reads=[pu.d, sg.d], writes=[dact[si]])
            for dp in range(8):
                wb = wring.next()
                dv = wb.t[:, 0:NJ * 128].rearrange("p (j n) -> p j n", j=NJ)
                P.dma(POOL, lambda e, dv=dv, dp=dp: e.dma_start(out=dv, in_=wd_v[:, :, dp * 128:(dp + 1) * 128]), writes=wb.ds)
                for si, ti in enumerate(half):
                    t0, n = TG[ti]
                    off = si * 512
                    pd = ps()
                    for j in range(NJ):
                        P.op(PE, lambda e, j=j, dv=dv, pd=pd, off=off, n=n: e.matmul(pd.t[:, :n], lhsT=dv[:, j, :], rhs=act[:, j, off:off + n],
                                                                                    start=(j == 0), stop=(j == NJ - 1)),
                             reads=[wb.ds[0], dact[si]], writes=[pd.d])
                    P.op(DVE, lambda e, pd=pd, dp=dp, t0=t0, n=n: e.scalar_tensor_tensor(
                        out=xT[:, dp, t0:t0 + n], in0=pd.t[:, :n], scalar=0.5, in1=xT[:, dp, t0:t0 + n], op0=ALU.mult, op1=ALU.add),
                        reads=[pd.d, dx[ti]], writes=[dx[ti]])

    o = 0
    attT = arena[:, o:o + 4 * NT].rearrange("p (h n) -> p h n", h=4); o += 4 * NT
    hgoT = arena[:, o:o + 4 * NT].rearrange("p (h n) -> p h n", h=4); o += 4 * NT
    memoT = arena[:, o:o + 4 * NT].rearrange("p (h n) -> p h n", h=4); o += 4 * NT
    o = (o + 15) // 16 * 16
    scr0 = o
    d_att, d_hgo, d_memo = Dep("attT"), Dep("hgoT"), Dep("memoT")
    d_scr = Dep("scratch")

    w_in_v = w_in.rearrange("(c p) n -> p c n", p=128)

    def load_wchunk(src_v, col0, ncols=128, nk=8):
        wb = wring.next()
        v = wb.t[:, 0:nk * ncols].rearrange("p (c n) -> p c n", c=nk)
        P.dma(POOL, lambda e: e.dma_start(out=v, in_=src_v[:, :, col0:col0 + ncols]), writes=wb.ds)
        return wb, v

    def proj_fm(wb, wv, srcT, dsrc_of, tgs, sink, nk=8):
        for ti, (t0, n) in tgs:
            pb = ps()
            for c in range(nk):
                P.op(PE, lambda e, c=c, pb=pb, t0=t0, n=n: e.matmul(pb.t[:, :n], lhsT=wv[:, c, :], rhs=srcT[:, c, t0:t0 + n],
                                                                   start=(c == 0), stop=(c == nk - 1)),
                     reads=[wb.ds[0], dsrc_of(ti)], writes=[pb.d])
            sink(ti, t0, n, pb)

    vtok_o = scr0
    vtok = arena[:, vtok_o:vtok_o + 16 * 512].rearrange("p (i n) -> p i n", i=16)
    d_vtok = Dep("vtok")
    his = A2(512, F32, parts=4)
    a2_after_his = _a2[0]
    d_his = Dep("his")

    def tokmajor_pass():
        jobs = []
        for g in range(3):
            keep = WIN[g][0]
            tiles = list(range(16 - keep // 128, 16))
            jobs.append((1536 + g * 512, tiles, ("k", g)))
            jobs.append((3072 + g * 512, tiles, ("v", g)))
        jobs.append((5632, list(range(16)), ("hi", 0)))
        for col0, tiles, (kind, g) in jobs:
            wb, wv = load_wchunk(w_in_v, col0, 512)
            for i in tiles + [16]:
                tok0, m = (i * 128, 128) if i < 16 else (T, NS)
                ti = min(i // 4, 4)
                pb = ps()
                for c in range(8):
                    P.op(PE, lambda e, c=c, pb=pb, tok0=tok0, m=m, wv=wv: e.matmul(pb.t[0:m, :], lhsT=hT[:, c, tok0:tok0 + m], rhs=wv[:, c, :],
                                                                                 start=(c == 0), stop=(c == 7)),
                         reads=[wb.ds[0], dh[ti]], writes=[pb.d])
                if kind == "hi":
                    if i < 16:
                        P.op(ACT, lambda e, pb=pb, i=i: e.copy(out=vtok[:, i, :], in_=pb.t[:, :]), reads=[pb.d], writes=[d_vtok])
                    else:
                        P.op(ACT, lambda e, pb=pb: e.copy(out=his[:, :], in_=pb.t[0:NS, :]), reads=[pb.d], writes=[d_his])
                else:
                    sb = stage.next()
                    eng = ACT if (i % 2 == 0) else DVE
                    if eng == ACT:
                        P.op(ACT, lambda e, pb=pb, sb=sb, m=m: e.copy(out=sb.t[0:m, 0:512], in_=pb.t[0:m, :]), reads=[pb.d], writes=[sb.d])
                    else:
                        P.op(DVE, lambda e, pb=pb, sb=sb, m=m: e.tensor_copy(out=sb.t[0:m, 0:512], in_=pb.t[0:m, :]), reads=[pb.d], writes=[sb.d])
                    keep = WIN[g][0]
                    if i < 16:
                        dst = (pwk if kind == "k" else pwv)[g]
                        r0 = i * 128 - (T - keep)
                        P.dma(SP, lambda e, dst=dst, r0=r0, sb=sb: e.dma_start(out=dst[r0:r0 + 128, :], in_=sb.t[:, 0:512]), reads=[sb.d])
                    else:
                        dst = (swk if kind == "k" else swv)[g]
                        P.dma(SP, lambda e, dst=dst, keep=keep, sb=sb: e.dma_start(out=dst[:, keep - 1, :], in_=sb.t[0:NS, 0:512]), reads=[sb.d])
                    out_deps.append(sb.d)

    o2 = 4 * NT
    asets = []
    for base_ in (o2, scr0):
        ob_ = base_
        st_ = {}
        st_["q"] = arena[:, ob_:ob_ + T]; ob_ += T
        st_["k"] = arena[:, ob_:ob_ + T]; ob_ += T
        st_["v"] = arena[:, ob_:ob_ + T].rearrange("p (i n) -> p i n", i=16); ob_ += T
        st_["vT"] = arena[:, ob_:ob_ + T]; ob_ += T
        st_["dq"], st_["dk"], st_["dv"], st_["dvT"] = Dep(), Dep(), Dep(), Dep()
        asets.append(st_)
    o2 += 4 * T
    assert scr0 + 4 * T <= 8 * NT * 2
    acc = arena[:, o2:o2 + 8192].bitcast(F32).rearrange("p (a n) -> p a n", a=2); o2 += 8192
    att_end = o2
    d_acc = Dep("acc")
    zs_att = A2(36 * NS, F32).rearrange("p (c n) -> p c n", c=36)
    d_zs = Dep("zs_att")
    ptr = Ring([Buf(A2(256)) for i in range(4)])

    bulk = []
    for g_ in range(3):
        win_ = WIN[g_][0]
        for b_ in range(NS):
            for src_, dst_ in ((cwk[g_], swk[g_]), (cwv[g_], swv[g_])):
                r_ = 0
                while r_ < win_ - 1:
                    nr_ = min(512, win_ - 1 - r_)
                    bulk.append((src_[b_].rearrange("w n -> (w n)")[(r_ + 1) * 512:(r_ + 1 + nr_) * 512],
                                 dst_[b_].rearrange("w n -> (w n)")[r_ * 512:(r_ + nr_) * 512]))
                    r_ += nr_

    def issue_bulk(k, pace):
        for _ in range(k):
            if not bulk:
                return
            src_, dst_ = bulk.pop()
            dd = Dep()
            P.dma(SP, lambda e, src_=src_, dst_=dst_: e.dma_start(out=dst_, in_=src_), reads=pace, writes=[dd], pool="bulk")
            out_deps.append(dd)

    def attention_prompt():
        order = [(s, g) for s in range(4) for g in range(3)]
        att_proj(order[0][0], order[0][1], 0)
        for i, (s, g) in enumerate(order):
            if i + 1 < len(order):
                att_proj(order[i + 1][0], order[i + 1][1], (i + 1) % 2)
            att_blocks(s, g, i % 2)
            if i >= 2:
                issue_bulk(2, [asets[i % 2]["dq"]])
            if g == 2:
                P.op(ACT, lambda e: e.activation(out=acc[:, 1, :], in_=acc[:, 1, :], func=AF.Ln), reads=[d_acc], writes=[d_acc])
                P.op(ACT, lambda e: e.activation(out=acc[:, 1, :], in_=acc[:, 1, :], func=AF.Exp, scale=-1.0), reads=[d_acc], writes=[d_acc])
                P.op(DVE, lambda e, s=s: e.tensor_tensor(out=attT[:, s, 0:T], in0=acc[:, 0, :], in1=acc[:, 1, :], op=ALU.mult),
                     reads=[d_acc], writes=[d_att])

    def att_proj(s, g, si):
        A_ = asets[si]
        qT_, kT_, vg_, vT_ = A_["q"], A_["k"], A_["v"], A_["vT"]
        win, dil = WIN[g]

        def perm_sink(dst, ddst, kidx):
            def sink(ti, t0, n, pb):
                if ti == 4:
                    zi = kidx * 12 + g * 4 + s
                    P.op(DVE, lambda e: e.tensor_copy(out=zs_att[:, zi, :], in_=pb.t[:, 0:NS]),
                         reads=[pb.d], writes=[d_zs])
                    return
                ov = dst.rearrange("p (r l) -> p r l", r=dil)[:, :, t0 // dil:(t0 + 512) // dil]
                iv = pb.t[:, 0:512].rearrange("p (i r) -> p r i", r=dil)
                if ti % 2 == 0:
                    P.op(ACT, lambda e: e.copy(out=ov, in_=iv), reads=[pb.d], writes=[ddst])
                else:
                    P.op(DVE, lambda e: e.tensor_copy(out=ov, in_=iv), reads=[pb.d], writes=[ddst])
            return sink
        wbq, wq = load_wchunk(w_in_v, g * 512 + s * 128)
        wbk, wk = load_wchunk(w_in_v, 1536 + g * 512 + s * 128)
        wbv, wvv = load_wchunk(w_in_v, 3072 + g * 512 + s * 128)
        proj_fm(wbq, wq, hT, lambda ti: dh[ti], ALLTG, perm_sink(qT_, A_["dq"], 0))
        proj_fm(wbk, wk, hT, lambda ti: dh[ti], ALLTG, perm_sink(kT_, A_["dk"], 1))
        proj_fm(wbv, wvv, hT, lambda ti: dh[ti], ALLTG, perm_sink(vT_, A_["dvT"], 2))
        for q4 in range(4):
            tp = ps()
            tpv = tp.t[:].bitcast(BF16)
            for j in range(4):
                tile = q4 * 4 + j
                P.op(PE, lambda e, j=j, tile=tile, tpv=tpv: e.transpose(out=tpv[:, j * 128:(j + 1) * 128], in_=vT_[:, tile * 128:(tile + 1) * 128], identity=ident_b),
                     reads=[A_["dvT"], dc], writes=[tp.d])
            if q4 % 2 == 0:
                P.op(ACT, lambda e, tpv=tpv, q4=q4: e.copy(out=vg_[:, q4 * 4:q4 * 4 + 4, :], in_=tpv[:, 0:512].rearrange("p (i n) -> p i n", i=4)),
                     reads=[tp.d], writes=[A_["dv"]])
            else:
                P.op(DVE, lambda e, tpv=tpv, q4=q4: e.tensor_copy(out=vg_[:, q4 * 4:q4 * 4 + 4, :], in_=tpv[:, 0:512].rearrange("p (i n) -> p i n", i=4)),
                     reads=[tp.d], writes=[A_["dv"]])

    def att_blocks(s, g, si):
        A_ = asets[si]
        qT_, kT_, vg_ = A_["q"], A_["k"], A_["v"]
        d_q_, d_k_, d_v_ = A_["dq"], A_["dk"], A_["dv"]
        win, dil = WIN[g]
        L = T // dil
        nb = L // 128

        def scores(tile):
            r, b = tile // nb, tile % nb
            p0 = tile * 128
            w = 128 if b == 0 else 256
            sc = ps()
            P.op(PE, lambda e: e.matmul(sc.t[:, 0:128], lhsT=kT_[:, p0:p0 + 128], rhs=qT_[:, p0:p0 + 128], start=True, stop=True),
                 reads=[d_q_, d_k_], writes=[sc.d])
            if b > 0:
                P.op(PE, lambda e: e.matmul(sc.t[:, 128:256], lhsT=kT_[:, p0 - 128:p0], rhs=qT_[:, p0:p0 + 128], start=True, stop=True),
                     reads=[d_q_, d_k_], writes=[sc.d])
            pt = ptr.next()
            P.op(ACT, lambda e: e.activation(out=pt.t[:, 0:w], in_=sc.t[:, 0:w], func=AF.Exp, scale=SCALE),
                 reads=[sc.d], writes=[pt.d])
            P.op(DVE, lambda e: e.tensor_tensor(out=pt.t[:, 0:w], in0=pt.t[:, 0:w], in1=mask2[:, 0:w], op=ALU.mult),
                 reads=[pt.d, dc], writes=[pt.d])
            return pt

        def pv(tile, pt):
            r, b = tile // nb, tile % nb
            ob = ps()
            P.op(PE, lambda e: e.matmul(ob.t[:, 0:128], lhsT=vg_[:, tile, :], rhs=pt.t[:, 0:128], start=True, stop=(b == 0)),
                 reads=[pt.d, d_v_], writes=[ob.d])
            if b > 0:
                P.op(PE, lambda e: e.matmul(ob.t[:, 0:128], lhsT=vg_[:, tile - 1, :], rhs=pt.t[:, 128:256], start=False, stop=True),
                     reads=[pt.d, d_v_], writes=[ob.d])
            P.op(PE, lambda e: e.matmul(ob.t[:, 128:256], lhsT=ones_b, rhs=pt.t[:, 0:128], start=True, stop=(b == 0)),
                 reads=[pt.d, dc], writes=[ob.d])
            if b > 0:
                P.op(PE, lambda e: e.matmul(ob.t[:, 128:256], lhsT=ones_b, rhs=pt.t[:, 128:256], start=False, stop=True),
                     reads=[pt.d, dc], writes=[ob.d])
            start = r + dil * 128 * b
            stop = start + dil * 127 + 1
            av = acc[:, :, start:stop:dil]
            iv = ob.t[:, 0:256].rearrange("p (a n) -> p a n", a=2)
            if g == 0:
                P.op(ACT, lambda e: e.copy(out=av, in_=iv), reads=[ob.d], writes=[d_acc])
            else:
                P.op(DVE, lambda e: e.tensor_tensor(out=av, in0=iv, in1=av, op=ALU.add), reads=[ob.d, d_acc], writes=[d_acc])

        pts = {0: scores(0)}
        for tile in range(16):
            if tile + 1 < 16:
                pts[tile + 1] = scores(tile + 1)
            pv(tile, pts.pop(tile))

    o3 = scr0
    vs_all = arena[:, o3:o3 + 12 * 512].rearrange("p (i n) -> p i n", i=12); o3 += 12 * 512
    d_vs = [Dep(f"vs{i}") for i in range(12)]
    ks_r = Ring([Buf(A2(512)) for i in range(2)])
    kts_r = Ring([Buf(A2(512)) for i in range(2)])
    smallf = A2(256, F32)
    smallb = A2(256)
    d_sm = Dep("small")

    def attention_sample():
        qs_b = smallb[:, 0:48].rearrange("p (c n) -> p c n", c=12)
        P.op(DVE, lambda e: e.tensor_copy(out=qs_b, in_=zs_att[:, 0:12, :]), reads=[d_zs], writes=[d_sm])
        scs = ps_long
        for b in range(NS):
            for g in range(3):
                win, dil = WIN[g]
                i = b * 3 + g
                kb = ks_r.next()
                P.dma(POOL, lambda e, kb=kb, b=b, g=g, win=win, dil=dil: e.dma_start(out=kb.t[:, :], in_=cwk[g][b, 0:win - dil + 1:dil, :]), writes=[kb.d])
                P.dma(POOL, lambda e, i=i, b=b, g=g, win=win, dil=dil: e.dma_start(out=vs_all[:, i, :], in_=cwv[g][b, 0:win - dil + 1:dil, :]), writes=[d_vs[i]])
                tp = ps()
                tpv = tp.t[:].bitcast(BF16)
                for h in range(4):
                    P.op(PE, lambda e, h=h, kb=kb, tpv=tpv: e.transpose(out=tpv[:, h * 128:(h + 1) * 128], in_=kb.t[:, h * 128:(h + 1) * 128], identity=ident_b),
                         reads=[kb.d, dc], writes=[tp.d])
                kt = kts_r.next()
                P.op(ACT, lambda e, kt=kt, tpv=tpv: e.copy(out=kt.t[:, :], in_=tpv[:, 0:512]), reads=[tp.d], writes=[kt.d])
                for h in range(4):
                    col = (g * 4 + h) * 4 + b
                    P.op(PE, lambda e, h=h, kt=kt, col=col, g=g, b=b: e.matmul(scs.t[:, col:col + 1], lhsT=kt.t[:, h * 128:(h + 1) * 128],
                                                                             rhs=qs_b[:, g * 4 + h, b:b + 1], start=True, stop=True),
                         reads=[kt.d, d_sm], writes=[scs.d])
        pts = smallb[:, 64:112]
        P.op(ACT, lambda e: e.activation(out=pts, in_=scs.t[:, 0:48], func=AF.Exp, scale=SCALE), reads=[scs.d], writes=[d_sm])
        obs = ps()
        for b in range(NS):
            for g in range(3):
                i = b * 3 + g
                for h in range(4):
                    col = (g * 4 + h) * 4 + b
                    P.op(PE, lambda e, i=i, h=h, col=col: e.matmul(obs.t[:, col:col + 1], lhsT=vs_all[:, i, h * 128:(h + 1) * 128], rhs=pts[:, col:col + 1],
                                                                 start=True, stop=True), reads=[d_vs[i], d_sm], writes=[obs.d])
        P.op(PE, lambda e: e.matmul(obs.t[:, 64:112], lhsT=ones_b, rhs=pts, start=True, stop=True), reads=[d_sm, dc], writes=[obs.d])
        prod = smallf[:, 0:48]
        P.op(DVE, lambda e: e.tensor_tensor(out=prod.rearrange("p (c n) -> p c n", c=12), in0=zs_att[:, 0:12, :], in1=zs_att[:, 12:24, :], op=ALU.mult),
             reads=[d_zs], writes=[d_sm])
        sn = ps()
        P.op(PE, lambda e: e.matmul(sn.t[:, 0:48], lhsT=ones_f, rhs=prod, start=True, stop=True), reads=[d_sm, dc], writes=[sn.d])
        pn = smallf[:, 48:96]
        P.op(ACT, lambda e: e.activation(out=pn, in_=sn.t[:, 0:48], func=AF.Exp, scale=SCALE), reads=[sn.d], writes=[d_sm])
        num = smallf[:, 96:144]; den = smallf[:, 144:192]
        P.op(DVE, lambda e: e.tensor_tensor(out=num.rearrange("p (c n) -> p c n", c=12), in0=pn.rearrange("p (c n) -> p c n", c=12),
                                            in1=zs_att[:, 24:36, :], op=ALU.mult), reads=[d_sm, d_zs], writes=[d_sm])
        P.op(DVE, lambda e: e.tensor_tensor(out=num, in0=obs.t[:, 0:48], in1=num, op=ALU.add), reads=[obs.d, d_sm], writes=[d_sm])
        P.op(DVE, lambda e: e.tensor_tensor(out=den, in0=obs.t[:, 64:112], in1=pn, op=ALU.add), reads=[obs.d, d_sm], writes=[d_sm])
        for tq in (num, den):
            P.op(DVE, lambda e, tq=tq: e.tensor_tensor(out=tq[:, 0:16], in0=tq[:, 0:16], in1=tq[:, 16:32], op=ALU.add), reads=[d_sm], writes=[d_sm])
            P.op(DVE, lambda e, tq=tq: e.tensor_tensor(out=tq[:, 0:16], in0=tq[:, 0:16], in1=tq[:, 32:48], op=ALU.add), reads=[d_sm], writes=[d_sm])
        P.op(DVE, lambda e: e.reciprocal(out=den[:, 0:16], in_=den[:, 0:16]), reads=[d_sm], writes=[d_sm])
        P.op(DVE, lambda e: e.tensor_tensor(out=attT[:, :, T:NT], in0=num[:, 0:16].rearrange("p (h b) -> p h b", h=4),
                                            in1=den[:, 0:16].rearrange("p (h b) -> p h b", h=4), op=ALU.mult), reads=[d_sm], writes=[d_att])

    _a2_att_end = _a2[0]
    _a2[0] = a2_after_his
    Sbf4_r = Ring([Buf(A2(512)) for i in range(2)])
    kdt4_r = Ring([Buf(A2(512)) for i in range(2)])
    am4_r = Ring([Buf(A2(512)) for i in range(2)])
    qd4 = A2(4 * 512).rearrange("p (h n) -> p h n", h=4)
    kd4 = A2(4 * 512).rearrange("p (h n) -> p h n", h=4)
    d_qd4 = [Dep(f"qd4{h}") for h in range(4)]
    d_kd4 = [Dep(f"kd4{h}") for h in range(4)]
    S4 = A2(512, F32)
    d_S = [Dep(f"S{h}") for h in range(4)]
    kd2_4 = A2(4 * 512).rearrange("p (h n) -> p h n", h=4)
    d_kd2 = [Dep(f"kd2{h}") for h in range(4)]
    s0_r = Ring([Buf(A2(128, F32)) for i in range(2)])
    s1_r = Ring([Buf(A2(128, F32)) for i in range(2)])
    ohg4_r = Ring([Buf(A2(512, F32)) for i in range(2)])
    hsm = A2(64, F32)
    d_hsm = Dep("hsm")
    ebe4 = A2(64, F32)
    d_ebe = [[Dep() for h in range(4)] for _ in range(2)]
    oo_banks = [ps_long, ps_long2]
    _a2[0] = max(_a2[0], _a2_att_end)

    def hgrn():
        wbq, wq = load_wchunk(w_in_v, 4608, 512)
        wbf, wf = load_wchunk(w_in_v, 5120, 512)
        for h in range(4):
            P.op(POOL, lambda e, h=h: e.memset(S4[:, h * 128:(h + 1) * 128], 0.0), reads=[], writes=[d_S[h]])
        sb0 = Sbf4_r.next()
        P.op(POOL, lambda e: e.memset(sb0.t, 0.0), writes=[sb0.d])
        cur = {"sbf": sb0, "pair": 0}
        for ti, (t0, n) in ALLTG:
            if ti < 4:
                hg_gates2((0, 1), ti, t0, n, wbq, wq, wbf, wf)
                hg_gates2((2, 3), ti, t0, n, wbq, wq, wbf, wf)
            else:
                for h in range(4):
                    g_ = hg_gates(h, ti, t0, n, wbq, wq, wbf, wf)
                    hg_sample(h, t0, g_[0], g_[1])
            if ti < 4:
                pre = hg_pair_pre(ti, 0, cur)
                for pp in range(4):
                    nxt = hg_pair_pre(ti, pp + 1, cur) if pp + 1 < 4 else None
                    hg_pair(ti, t0, pp, cur, pre)
                    pre = nxt
                    issue_bulk(2, [d_hgo])
            if ti == 3:
                for h in range(4):
                    P.dma(SP, lambda e, h=h: e.dma_start(out=phg[h], in_=S4[:, h * 128:(h + 1) * 128]), reads=[d_S[h]])
        out_deps.extend(d_S)
        issue_bulk(1000, [])

    def hg_gates2(hs_, ti, t0, n, wbq, wq, wbf, wf):
        par = ti % 2
        st = {}
        for h in hs_:
            hsl = slice(h * 128, (h + 1) * 128)
            pbq = ps()
            for c in range(8):
                P.op(PE, lambda e, c=c, pbq=pbq, hsl=hsl: e.matmul(pbq.t[:, :n], lhsT=wq[:, c, hsl], rhs=hT[:, c, t0:t0 + n], start=(c == 0), stop=(c == 7)),
                     reads=[wbq.ds[0], dh[ti]], writes=[pbq.d])
            qg = f32r.next()
            P.op(ACT, lambda e, qg=qg, pbq=pbq: e.activation(out=qg.t[:, :n], in_=pbq.t[:, :n], func=AF.Sigmoid), reads=[pbq.d], writes=[qg.d])
            pbf = ps()
            for c in range(8):
                P.op(PE, lambda e, c=c, pbf=pbf, hsl=hsl: e.matmul(pbf.t[:, :n], lhsT=wf[:, c, hsl], rhs=hT[:, c, t0:t0 + n], start=(c == 0), stop=(c == 7)),
                     reads=[wbf.ds[0], dh[ti]], writes=[pbf.d])
            kk = f32r.next()
            P.op(ACT, lambda e, kk=kk, pbf=pbf: e.activation(out=kk.t[:, :n], in_=pbf.t[:, :n], func=AF.Sigmoid), reads=[pbf.d], writes=[kk.d])
            st[h] = dict(qg=qg, kk=kk)
        for h in hs_:
            kk = st[h]["kk"]
            lf = f32r.next()
            st[h]["lf"] = lf
            P.op(ACT, lambda e, lf=lf, kk=kk, h=h: e.activation(out=lf.t[:, :], in_=kk.t[:, :], func=AF.Ln, bias=lb[:, h:h + 1], scale=oml[:, h:h + 1]),
                 reads=[kk.d, dc], writes=[lf.d])
            P.op(DVE, lambda e, kk=kk, h=h: e.tensor_scalar(out=kk.t[:, :n], in0=kk.t[:, :n], scalar1=oml[:, h:h + 1], scalar2=None, op0=ALU.mult),
                 reads=[kk.d, dc], writes=[kk.d])
        for h in hs_:
            lf = st[h]["lf"]
            P.op(DVE, lambda e, lf=lf: e.tensor_tensor_scan(out=lf.t[:, :], data0=rst, data1=lf.t[:, :], initial=zeroc, op0=ALU.mult, op1=ALU.add),
                 reads=[lf.d, dc], writes=[lf.d])
        for h in hs_:
            lf = st[h]["lf"]
            eb = f32r.next()
            st[h]["eb"] = eb
            P.op(ACT, lambda e, eb=eb, lf=lf: e.activation(out=eb.t[:, :], in_=lf.t[:, :], func=AF.Exp), reads=[lf.d], writes=[eb.d])
        for h in hs_:
            eb, qg = st[h]["eb"], st[h]["qg"]
            ev = ebe4[:, par * 32 + h * 8:par * 32 + h * 8 + 8]
            st[h]["ev"] = ev
            P.op(DVE, lambda e, ev=ev, eb=eb: e.tensor_copy(out=ev, in_=eb.t[:, 63:512:64]), reads=[eb.d], writes=[d_ebe[par][h]])
            P.op(DVE, lambda e, h=h, qg=qg, eb=eb: e.tensor_tensor(out=qd4[:, h, :], in0=qg.t[:, :], in1=eb.t[:, :], op=ALU.mult),
                 reads=[qg.d, eb.d], writes=[d_qd4[h]])
        for h in hs_:
            eb, lf = st[h]["eb"], st[h]["lf"]
            P.op(ACT, lambda e, eb=eb, lf=lf: e.activation(out=eb.t[:, :], in_=lf.t[:, :], func=AF.Exp, scale=-1.0), reads=[lf.d], writes=[eb.d])
        for h in hs_:
            eb, kk, ev = st[h]["eb"], st[h]["kk"], st[h]["ev"]
            P.op(DVE, lambda e, h=h, kk=kk, eb=eb: e.tensor_tensor(out=kd4[:, h, :], in0=kk.t[:, :], in1=eb.t[:, :], op=ALU.mult),
                 reads=[kk.d, eb.d], writes=[d_kd4[h]])
            P.op(DVE, lambda e, h=h, ev=ev: e.tensor_tensor(out=kd2_4[:, h, :].rearrange("p (c s) -> p c s", s=64), in0=kd4[:, h, :].rearrange("p (c s) -> p c s", s=64),
                                                     in1=ev.unsqueeze(2).to_broadcast([128, 8, 64]), op=ALU.mult),
                 reads=[d_kd4[h], d_ebe[par][h]], writes=[d_kd2[h]])

    def hg_gates(h, ti, t0, n, wbq, wq, wbf, wf):
        hs = slice(h * 128, (h + 1) * 128)
        pbq = ps()
        for c in range(8):
            P.op(PE, lambda e, c=c: e.matmul(pbq.t[:, :n], lhsT=wq[:, c, hs], rhs=hT[:, c, t0:t0 + n], start=(c == 0), stop=(c == 7)),
                 reads=[wbq.ds[0], dh[ti]], writes=[pbq.d])
        qg = f32r.next()
        P.op(ACT, lambda e: e.activation(out=qg.t[:, :n], in_=pbq.t[:, :n], func=AF.Sigmoid), reads=[pbq.d], writes=[qg.d])
        pbf = ps()
        for c in range(8):
            P.op(PE, lambda e, c=c: e.matmul(pbf.t[:, :n], lhsT=wf[:, c, hs], rhs=hT[:, c, t0:t0 + n], start=(c == 0), stop=(c == 7)),
                 reads=[wbf.ds[0], dh[ti]], writes=[pbf.d])
        kk = f32r.next()
        P.op(ACT, lambda e: e.activation(out=kk.t[:, :n], in_=pbf.t[:, :n], func=AF.Sigmoid), reads=[pbf.d], writes=[kk.d])
        P.op(DVE, lambda e: e.tensor_scalar(out=kk.t[:, :n], in0=kk.t[:, :n], scalar1=oml[:, h:h + 1], scalar2=None, op0=ALU.mult),
             reads=[kk.d, dc], writes=[kk.d])
        if ti == 4:
            return qg, kk
        par = ti % 2
        lf = f32r.next()
        P.op(ACT, lambda e: e.activation(out=lf.t[:, :], in_=kk.t[:, :], func=AF.Ln, bias=lb[:, h:h + 1], scale=1.0),
             reads=[kk.d, dc], writes=[lf.d])
        P.op(DVE, lambda e: e.tensor_tensor_scan(out=lf.t[:, :], data0=rst, data1=lf.t[:, :], initial=zeroc, op0=ALU.mult, op1=ALU.add),
             reads=[lf.d, dc], writes=[lf.d])
        eb = f32r.next()
        P.op(ACT, lambda e: e.activation(out=eb.t[:, :], in_=lf.t[:, :], func=AF.Exp), reads=[lf.d], writes=[eb.d])
        ev = ebe4[:, par * 32 + h * 8:par * 32 + h * 8 + 8]
        P.op(DVE, lambda e: e.tensor_copy(out=ev, in_=eb.t[:, 63:512:64]), reads=[eb.d], writes=[d_ebe[par][h]])
        P.op(DVE, lambda e: e.tensor_tensor(out=qd4[:, h, :], in0=qg.t[:, :], in1=eb.t[:, :], op=ALU.mult),
             reads=[qg.d, eb.d], writes=[d_qd4[h]])
        P.op(ACT, lambda e: e.activation(out=eb.t[:, :], in_=lf.t[:, :], func=AF.Exp, scale=-1.0), reads=[lf.d], writes=[eb.d])
        P.op(DVE, lambda e: e.tensor_tensor(out=kd4[:, h, :], in0=kk.t[:, :], in1=eb.t[:, :], op=ALU.mult),
             reads=[kk.d, eb.d], writes=[d_kd4[h]])
        P.op(DVE, lambda e: e.tensor_tensor(out=kd2_4[:, h, :].rearrange("p (c s) -> p c s", s=64), in0=kd4[:, h, :].rearrange("p (c s) -> p c s", s=64),
                                            in1=ev.unsqueeze(2).to_broadcast([128, 8, 64]), op=ALU.mult),
             reads=[d_kd4[h], d_ebe[par][h]], writes=[d_kd2[h]])
        return None

    def hg_pair_pre(ti, pp, cur):
        tile = ti * 4 + pp
        c0 = pp * 128
        tp = ps()
        tpv = tp.t[:].bitcast(BF16)
        for h in range(4):
            P.op(PE, lambda e, h=h: e.transpose(out=tpv[:, h * 128:(h + 1) * 128], in_=kd2_4[:, h, c0:c0 + 128], identity=ident_b),
                 reads=[d_kd2[h], dc], writes=[tp.d])
        kdt = kdt4_r.next()
        P.op(ACT, lambda e: e.copy(out=kdt.t, in_=tpv[:, 0:512]), reads=[tp.d], writes=[kdt.d])
        at = ps()
        for h in range(4):
            P.op(PE, lambda e, h=h: e.matmul(at.t[:, h * 128:(h + 1) * 128], lhsT=kd4[:, h, c0:c0 + 128], rhs=qd4[:, h, c0:c0 + 128], start=True, stop=True),
                 reads=[d_kd4[h], d_qd4[h]], writes=[at.d])
        am = am4_r.next()
        P.op(DVE, lambda e: e.tensor_tensor(out=am.t, in0=at.t[:, :], in1=hmask4, op=ALU.mult), reads=[at.d, dc], writes=[am.d])
        oo = oo_banks[cur["pair"] % 2]
        cur["pair"] += 1
        P.op(DVE, lambda e: e.memset(oo.t[:, :], 0.0), reads=[], writes=[oo.d])
        for h in range(4):
            P.op(PE, lambda e, h=h: e.matmul(oo.t[:, h * 128:(h + 1) * 128], lhsT=vtok[:, tile, h * 128:(h + 1) * 128], rhs=am.t[:, h * 128:(h + 1) * 128],
                                             start=False, stop=True, skip_group_check=True), reads=[am.d, d_vtok], writes=[oo.d])
        return kdt, oo

    def hg_pair(ti, t0, pp, cur, pre):
        kdt, oo = pre
        par = ti % 2
        tile = ti * 4 + pp
        c0 = pp * 128
        for cc in range(2):
            base = cc * 64
            sbf = cur["sbf"]
            for h in range(4):
                P.op(PE, lambda e, h=h, sbf=sbf, base=base: e.matmul(oo.t[:, h * 128 + base:h * 128 + base + 64], lhsT=sbf.t[:, h * 128:(h + 1) * 128],
                                                                     rhs=qd4[:, h, c0 + base:c0 + base + 64], start=False, stop=True, skip_group_check=True),
                     reads=[sbf.d, d_qd4[h]], writes=[oo.d])
            su = ps()
            for h in range(4):
                P.op(PE, lambda e, h=h, su=su, base=base: e.matmul(su.t[:, h * 128:(h + 1) * 128], lhsT=kdt.t[base:base + 64, h * 128:(h + 1) * 128],
                                                                   rhs=vtok[base:base + 64, tile, h * 128:(h + 1) * 128], start=True, stop=True),
                     reads=[kdt.d, d_vtok], writes=[su.d])
            for h in range(4):
                ecol = ebe4[:, par * 32 + h * 8 + pp * 2 + cc:par * 32 + h * 8 + pp * 2 + cc + 1]
                P.op(DVE, lambda e, h=h, su=su, ecol=ecol: e.scalar_tensor_tensor(out=S4[:, h * 128:(h + 1) * 128], in0=S4[:, h * 128:(h + 1) * 128],
                                                                                scalar=ecol, in1=su.t[:, h * 128:(h + 1) * 128],
                                                                                op0=ALU.mult, op1=ALU.add),
                     reads=[su.d, d_ebe[par][h], d_S[h]], writes=[d_S[h]])
            nsb = Sbf4_r.next()
            P.op(ACT, lambda e, nsb=nsb: e.copy(out=nsb.t, in_=S4[:, :]), reads=d_S, writes=[nsb.d])
            cur["sbf"] = nsb
        ohg = ohg4_r.next()
        P.op(ACT, lambda e: e.copy(out=ohg.t[:, :], in_=oo.t[:, :]), reads=[oo.d], writes=[ohg.d])
        rs = rstd_of(lambda c: ohg.t[:, :], 512, ohg.d, nchunks=1, denom=128.0)
        for h in range(4):
            P.op(DVE, lambda e, h=h: e.scalar_tensor_tensor(out=hgoT[:, h, t0 + c0:t0 + c0 + 128], in0=ohg.t[:, h * 128:(h + 1) * 128], scalar=ghg[:, h:h + 1],
                                                            in1=rs.t[:, h * 128:(h + 1) * 128], op0=ALU.mult, op1=ALU.mult),
                 reads=[ohg.d, rs.d, dc], writes=[d_hgo])

    def hg_sample(h, t0, qg, kk):
        ohg = ohg4_r.next()
        fS = hsm[:, h * 4:h * 4 + 4]
        P.op(DVE, lambda e: e.tensor_scalar(out=fS, in0=kk.t[:, 0:NS], scalar1=lb[:, h:h + 1], scalar2=None, op0=ALU.add),
             reads=[kk.d, dc], writes=[d_hsm])
        osp = ps()
        for b in range(NS):
            hg_sample_b(h, b, qg, kk, fS, osp)
        P.op(ACT, lambda e: e.copy(out=ohg.t[:, 0:NS], in_=osp.t[:, 0:NS]), reads=[osp.d], writes=[ohg.d])
        rs = rstd_of(lambda c: ohg.t[:, 0:NS], NS, ohg.d, nchunks=1, denom=128.0)
        P.op(DVE, lambda e: e.scalar_tensor_tensor(out=hgoT[:, h, t0:t0 + NS], in0=ohg.t[:, 0:NS], scalar=ghg[:, h:h + 1],
                                                   in1=rs.t[:, 0:NS], op0=ALU.mult, op1=ALU.mult),
             reads=[ohg.d, rs.d, dc], writes=[d_hgo])

    def hg_sample_b(h, b, qg, kk, fS, osp):
        s0 = s0_r.next()
        P.dma(SP, lambda e: e.dma_start(out=s0.t[:, :], in_=st_in[b * 4 + h]), writes=[s0.d])
        vb = ps()
        P.op(PE, lambda e: e.matmul(vb.t[:, 0:128], lhsT=sel[:, b, :], rhs=his[:, h * 128:(h + 1) * 128], start=True, stop=True),
             reads=[dc, d_his], writes=[vb.d])
        P.op(POOL, lambda e: e.tensor_scalar(out=s0.t[:, :], in0=s0.t[:, :], scalar1=fS[:, b:b + 1], scalar2=None, op0=ALU.mult),
             reads=[s0.d, d_hsm], writes=[s0.d])
        s1 = s1_r.next()
        P.op(DVE, lambda e: e.scalar_tensor_tensor(out=s1.t[:, :], in0=vb.t[:, 0:128], scalar=kk.t[:, b:b + 1],
                                                   in1=s0.t[:, :], op0=ALU.mult, op1=ALU.add),
             reads=[vb.d, kk.d, s0.d], writes=[s1.d])
        P.dma(SP, lambda e: e.dma_start(out=shg[b * 4 + h], in_=s1.t[:, :]), reads=[s1.d])
        out_deps.append(s1.d)
        P.op(PE, lambda e: e.matmul(osp.t[:, b:b + 1], lhsT=s1.t[:, :], rhs=qg.t[:, b:b + 1], start=True, stop=True),
             reads=[s1.d, qg.d], writes=[osp.d])

    o5 = 4 * NT
    memT = arena[:, o5:o5 + 8 * 256 * 2].bitcast(F32).rearrange("p (c n) -> p c n", c=8); o5 += 8 * 256 * 2
    mhT = arena[:, o5:o5 + 8 * 256].rearrange("p (c n) -> p c n", c=8); o5 += 8 * 256
    o5 = scr0
    mkT = arena[:, o5:o5 + 4 * 256].rearrange("p (h n) -> p h n", h=4); o5 += 4 * 256
    mvb = arena[:, o5:o5 + 2 * 512].rearrange("p (c n) -> p c n", c=2); o5 += 2 * 512
    ptm = arena[:, o5:o5 + 2 * 512].rearrange("p (c n) -> p c n", c=2); o5 += 2 * 512
    mqT4 = arena[:, o5:o5 + 4 * 512].rearrange("p (h n) -> p h n", h=4); o5 += 4 * 512
    d_mqT4 = [Dep() for _ in range(4)]
    mks = arena[:, o5:o5 + 2 * 512].rearrange("p (c n) -> p c n", c=2); o5 += 1024
    mvs = arena[:, o5:o5 + 2 * 512].rearrange("p (c n) -> p c n", c=2); o5 += 1024
    mkTs = arena[:, o5:o5 + 4 * 256].rearrange("p (h n) -> p h n", h=4); o5 += 1024
    mem_end = o5
    d_memT, d_mhT, d_mkT, d_mvb, d_ptm = Dep(), Dep(), Dep(), Dep(), Dep()
    d_mks, d_mvs, d_mkTs = Dep(), Dep(), Dep()
    assert mem_end <= 8 * NT * 2, mem_end
    mqs = A2(4 * NS).rearrange("p (h n) -> p h n", h=4)
    d_mqs = Dep("mqs")
    msm = A2(64, F32)
    msb = A2(16)
    d_msm = Dep("msm")

    def mem_attention():
        for i in range(2):
            load_tokmajor_to_T(mem[i * 128:(i + 1) * 128, :], 128, memT, i * 128, d_memT)
        norm_to(memT, {0: d_memT}, gme, mhT, {0: d_mhT}, [(0, (0, 256))])
        w_mk_v = w_mk.rearrange("(c p) n -> p c n", p=128)
        w_mv_v = w_mv.rearrange("(c p) n -> p c n", p=128)
        wbk, wk = load_wchunk(w_mk_v, 0, 512)
        wbv, wv_ = load_wchunk(w_mv_v, 0, 512)
        for hh in range(4):
            pb = ps()
            for c in range(8):
                P.op(PE, lambda e, c=c, pb=pb, hh=hh: e.matmul(pb.t[:, 0:256], lhsT=wk[:, c, hh * 128:(hh + 1) * 128], rhs=mhT[:, c, :], start=(c == 0), stop=(c == 7)),
                     reads=[wbk.ds[0], d_mhT], writes=[pb.d])
            P.op(ACT, lambda e, pb=pb, hh=hh: e.copy(out=mkT[:, hh, :], in_=pb.t[:, 0:256]), reads=[pb.d], writes=[d_mkT])
        for (wb_, w_, dst, isv) in ((wbk, wk, pmk, False), (wbv, wv_, pmv, True)):
            for i in range(2):
                pb = ps()
                for c in range(8):
                    P.op(PE, lambda e, c=c, pb=pb, i=i, w_=w_: e.matmul(pb.t[:, :], lhsT=mhT[:, c, i * 128:(i + 1) * 128], rhs=w_[:, c, :], start=(c == 0), stop=(c == 7)),
                         reads=[wb_.ds[0], d_mhT], writes=[pb.d])
                sb = stage.next()
                P.op(ACT, lambda e, pb=pb, sb=sb: e.copy(out=sb.t[:, 0:512], in_=pb.t[:, :]), reads=[pb.d], writes=[sb.d])
                if isv:
                    P.op(DVE, lambda e, sb=sb, i=i: e.tensor_copy(out=mvb[:, i, :], in_=sb.t[:, 0:512]), reads=[sb.d], writes=[d_mvb])
                P.dma(SP, lambda e, dst=dst, sb=sb, i=i: e.dma_start(out=dst[i * 128:(i + 1) * 128, :], in_=sb.t[:, 0:512]), reads=[sb.d])
                out_deps.append(sb.d)
        wbq, wq = load_wchunk(w_in_v, 6144, 512)
        for ti, (t0, n) in ALLTG:
            for hh in range(4):
                mem_q(wbq, wq, hh, ti, t0, n)
            if ti < 4:
                for hh in range(4):
                    mem_unit(hh, t0)
        for b in range(NS):
            mem_sample(b)

    def mem_q(wbq, wq, hh, ti, t0, n):
        pb = ps()
        for c in range(8):
            P.op(PE, lambda e, c=c: e.matmul(pb.t[:, :n], lhsT=wq[:, c, hh * 128:(hh + 1) * 128], rhs=hT[:, c, t0:t0 + n], start=(c == 0), stop=(c == 7)),
                 reads=[wbq.ds[0], dh[ti]], writes=[pb.d])
        if ti == 4:
            P.op(DVE, lambda e: e.tensor_copy(out=mqs[:, hh, :], in_=pb.t[:, 0:NS]), reads=[pb.d], writes=[d_mqs])
        elif hh % 2 == 0:
            P.op(ACT, lambda e: e.copy(out=mqT4[:, hh, :], in_=pb.t[:, :]), reads=[pb.d], writes=[d_mqT4[hh]])
        else:
            P.op(DVE, lambda e: e.tensor_copy(out=mqT4[:, hh, :], in_=pb.t[:, :]), reads=[pb.d], writes=[d_mqT4[hh]])

    def mem_unit(hh, t0):
        scb = [ps(), ps()]
        for mc in range(2):
            P.op(PE, lambda e, mc=mc: e.matmul(scb[mc].t[:, :], lhsT=mkT[:, hh, mc * 128:(mc + 1) * 128], rhs=mqT4[:, hh, :], start=True, stop=True),
                 reads=[d_mkT, d_mqT4[hh]], writes=[scb[mc].d])
            P.op(ACT, lambda e, mc=mc: e.activation(out=ptm[:, mc, :], in_=scb[mc].t[:, :], func=AF.Exp, scale=SCALE),
                 reads=[scb[mc].d], writes=[d_ptm])
        om = ps(); dn = ps()
        for mc in range(2):
            P.op(PE, lambda e, mc=mc: e.matmul(om.t[:, :], lhsT=mvb[:, mc, hh * 128:(hh + 1) * 128], rhs=ptm[:, mc, :], start=(mc == 0), stop=(mc == 1)),
                 reads=[d_mvb, d_ptm], writes=[om.d])
        for mc in range(2):
            P.op(PE, lambda e, mc=mc: e.matmul(dn.t[:, :], lhsT=ones_b, rhs=ptm[:, mc, :], start=(mc == 0), stop=(mc == 1)),
                 reads=[dc, d_ptm], writes=[dn.d])
        rc = f32r.next()
        P.op(ACT, lambda e: e.activation(out=rc.t[:, :], in_=dn.t[:, :], func=AF.Ln), reads=[dn.d], writes=[rc.d])
        P.op(ACT, lambda e: e.activation(out=rc.t[:, :], in_=rc.t[:, :], func=AF.Exp, scale=-1.0), reads=[rc.d], writes=[rc.d])
        P.op(DVE, lambda e: e.tensor_tensor(out=memoT[:, hh, t0:t0 + 512], in0=om.t[:, :], in1=rc.t[:, :], op=ALU.mult),
             reads=[om.d, rc.d], writes=[d_memo])

    def mem_sample(b):
        if True:
            P.dma(POOL, lambda e, b=b: e.dma_start(out=mks, in_=cmk[b].rearrange("(c p) n -> p c n", p=128)), writes=[d_mks])
            P.dma(POOL, lambda e, b=b: e.dma_start(out=mvs, in_=cmv[b].rearrange("(c p) n -> p c n", p=128)), writes=[d_mvs])
            for mc in range(2):
                tp = ps()
                tpv = tp.t[:].bitcast(BF16)
                for hh in range(4):
                    P.op(PE, lambda e, hh=hh, mc=mc, tpv=tpv: e.transpose(out=tpv[:, hh * 128:(hh + 1) * 128], in_=mks[:, mc, hh * 128:(hh + 1) * 128], identity=ident_b),
                         reads=[d_mks, dc], writes=[tp.d])
                P.op(ACT, lambda e, mc=mc, tpv=tpv: e.copy(out=mkTs[:, :, mc * 128:(mc + 1) * 128], in_=tpv[:, 0:512].rearrange("p (h n) -> p h n", h=4)),
                     reads=[tp.d], writes=[d_mkTs])
            scm = ps()
            for hh in range(4):
                for mc in range(2):
                    col = hh * 2 + mc
                    P.op(PE, lambda e, hh=hh, mc=mc, col=col, b=b: e.matmul(scm.t[:, col:col + 1], lhsT=mkTs[:, hh, mc * 128:(mc + 1) * 128], rhs=mqs[:, hh, b:b + 1],
                                                                        start=True, stop=True), reads=[d_mkTs, d_mqs], writes=[scm.d])
            pms = msb[:, 0:8]
            P.op(ACT, lambda e: e.activation(out=pms, in_=scm.t[:, 0:8], func=AF.Exp, scale=SCALE), reads=[scm.d], writes=[d_msm])
            oms = ps()
            for hh in range(4):
                for mc in range(2):
                    col = hh * 2 + mc
                    P.op(PE, lambda e, hh=hh, mc=mc, col=col: e.matmul(oms.t[:, col:col + 1], lhsT=mvs[:, mc, hh * 128:(hh + 1) * 128], rhs=pms[:, col:col + 1],
                                                                   start=True, stop=True), reads=[d_mvs, d_msm], writes=[oms.d])
            P.op(PE, lambda e: e.matmul(oms.t[:, 16:24], lhsT=ones_b, rhs=pms, start=True, stop=True), reads=[dc, d_msm], writes=[oms.d])
            t8 = msm[:, 0:16]
            P.op(DVE, lambda e: e.tensor_copy(out=t8[:, 0:8], in_=oms.t[:, 0:8]), reads=[oms.d], writes=[d_msm])
            P.op(DVE, lambda e: e.tensor_copy(out=t8[:, 8:16], in_=oms.t[:, 16:24]), reads=[oms.d], writes=[d_msm])
            t4 = msm[:, 16:24]
            v8 = t8.rearrange("p (k h m) -> p k h m", k=2, h=4)
            P.op(DVE, lambda e: e.tensor_tensor(out=t4.rearrange("p (k h) -> p k h", k=2), in0=v8[:, :, :, 0], in1=v8[:, :, :, 1], op=ALU.add),
                 reads=[d_msm], writes=[d_msm])
            P.op(DVE, lambda e: e.reciprocal(out=t4[:, 4:8], in_=t4[:, 4:8]), reads=[d_msm], writes=[d_msm])
            P.op(DVE, lambda e, b=b: e.tensor_tensor(out=memoT[:, :, T + b], in0=t4[:, 0:4], in1=t4[:, 4:8], op=ALU.mult), reads=[d_msm], writes=[d_memo])

    mTa = arena[:, scr0:scr0 + 4 * NT].rearrange("p (c n) -> p c n", c=4)
    mTb = aarena[:, a2_keep:a2_keep + 4 * NT].rearrange("p (c n) -> p c n", c=4)
    assert a2_keep + 4 * NT <= NJ * 1028 and scr0 + 4 * NT <= 8 * NT * 2

    def mTv(c):
        return mTa[:, c, :] if c < 4 else mTb[:, c - 4, :]
    d_mT = [Dep(f"mT{i}") for i in range(5)]
    d_xd = Dep("xd")
    dxc = [Dep(f"xc{c}") for c in range(6)]
    dxw = [Dep(f"xw{i}") for i in range(5)]
    drl = [Dep(f"xrl{i}") for i in range(5)]
    d_xd2 = [[Dep() for _ in range(5)] for _ in range(8)]

    def merge():
        brs = ((attT, d_att, w_ba), (hgoT, d_hgo, w_bb), (memoT, d_memo, w_bc))
        gate_v = w_in_v[:, :, 6656:9728].rearrange("p c (i d n) -> p c i d n", i=3, d=8)
        w_o_v = w_o.rearrange("(c p) n -> p c n", p=128)
        for dp in range(8):
            merge_dp(brs, dp, gate_v)
        bdep = [d_att, d_att, d_hgo, d_hgo, d_memo, d_memo]
        for c in range(6):
            phase_switch([dxc[c]], [bdep[c]])
            P.dma(SP, lambda e, c=c: e.dma_start(out=xT[:, c, :], in_=xd[:, c, :]), reads=[d_xd], writes=[dxc[c]])
        for hf in range(2):
            merge_out(hf, w_o_v)

    def merge_dp(brs, dp, gate_v):
        wb = wring.next()
        gv = wb.t[:, 0:3072].rearrange("p (c i n) -> p c i n", c=8, i=3)
        for i in range(3):
            P.dma(POOL, lambda e, i=i: e.dma_start(out=gv[:, :, i, :], in_=gate_v[:, :, i, dp, :]), writes=(wb.ds if i == 0 else [wb.ds[i]]), skip_waw=(i > 0))
        bvs = []
        for i in range(3):
            bv = wb.t[:, 3072 + i * 512:3072 + (i + 1) * 512].rearrange("p (c n) -> p c n", c=4)
            P.dma(POOL, lambda e, bv=bv, i=i: e.dma_start(out=bv, in_=brs[i][2].rearrange("(c p) n -> p c n", p=128)[:, :, dp * 128:(dp + 1) * 128]),
                  writes=[wb.ds[3 + i]], skip_waw=True)
            bvs.append(bv)
        for ti, (t0, n) in ALLTG:
            merge_one(brs, wb, gv, bvs, ti, t0, n, macc_bufs[ti % 2], dp)

    def merge_one(brs, wb, gv, bvs, ti, t0, n, macc, dp):
        for i in range(3):
            pg = ps(); pbr = ps()
            for c in range(8):
                P.op(PE, lambda e, c=c, pg=pg, i=i: e.matmul(pg.t[:, :n], lhsT=gv[:, c, i, :], rhs=hT[:, c, t0:t0 + n], start=(c == 0), stop=(c == 7)),
                     reads=[wb.ds[i], dh[ti]], writes=[pg.d])
            for c in range(4):
                P.op(PE, lambda e, c=c, pbr=pbr, i=i: e.matmul(pbr.t[:, :n], lhsT=bvs[i][:, c, :], rhs=brs[i][0][:, c, t0:t0 + n], start=(c == 0), stop=(c == 3)),
                     reads=[wb.ds[3 + i], brs[i][1]], writes=[pbr.d])
            sg = f32r.next()
            P.op(ACT, lambda e, sg=sg, pg=pg: e.activation(out=sg.t[:, :n], in_=pg.t[:, :n], func=AF.Sigmoid), reads=[pg.d], writes=[sg.d])
            if i == 0:
                P.op(DVE, lambda e, sg=sg, pbr=pbr: e.tensor_tensor(out=macc.t[:, :n], in0=pbr.t[:, :n], in1=sg.t[:, :n], op=ALU.mult),
                     reads=[pbr.d, sg.d], writes=[macc.d])
            else:
                P.op(DVE, lambda e, sg=sg, pbr=pbr: e.tensor_tensor(out=sg.t[:, :n], in0=pbr.t[:, :n], in1=sg.t[:, :n], op=ALU.mult),
                     reads=[pbr.d, sg.d], writes=[sg.d])
                if i == 1:
                    P.op(DVE, lambda e, sg=sg: e.tensor_tensor(out=macc.t[:, :n], in0=macc.t[:, :n], in1=sg.t[:, :n], op=ALU.add),
                         reads=[sg.d, macc.d], writes=[macc.d])
                else:
                    P.op(DVE, lambda e, sg=sg: e.tensor_tensor(out=mTv(dp)[:, t0:t0 + n], in0=macc.t[:, :n], in1=sg.t[:, :n], op=ALU.add),
                         reads=[sg.d, macc.d], writes=[d_mT[ti]])

    def merge_out(hf, w_o_v):
        wbo, wo = load_wchunk(w_o_v, hf * 512, 512)
        for d4 in range(4):
            dpo = hf * 4 + d4
            for ti, (t0, n) in ALLTG:
                merge_out_one(wbo, wo, d4, dpo, ti, t0, n)

    def merge_out_one(wbo, wo, d4, dpo, ti, t0, n):
        po = ps()
        for c in range(8):
            P.op(PE, lambda e, c=c: e.matmul(po.t[:, :n], lhsT=wo[:, c, d4 * 128:(d4 + 1) * 128], rhs=mTv(c)[:, t0:t0 + n], start=(c == 0), stop=(c == 7)),
                 reads=[wbo.ds[0], d_mT[ti]], writes=[po.d])
        if dpo < 6:
            P.op(DVE, lambda e: e.tensor_tensor(out=xT[:, dpo, t0:t0 + n], in0=po.t[:, :n], in1=xT[:, dpo, t0:t0 + n], op=ALU.add),
                 reads=[po.d, dxc[dpo]], writes=[dxc[dpo], dxw[ti]])
            return
        xo = xo_r.next()
        P.dma(ACT, lambda e: e.dma_start(out=xo.t[:, :n], in_=xd[:, dpo, t0:t0 + n]), reads=[d_xd], writes=[xo.d])
        P.op(DVE, lambda e: e.tensor_tensor(out=xo.t[:, :n], in0=po.t[:, :n], in1=xo.t[:, :n], op=ALU.add),
             reads=[po.d, xo.d], writes=[xo.d])
        P.dma(SP, lambda e: e.dma_start(out=xd2[:, dpo, t0:t0 + n], in_=xo.t[:, :n]), reads=[xo.d], writes=[d_xd2[dpo][ti]])

    yT_r = [Buf(aarena[:, k_ * 8192:(k_ + 1) * 8192].bitcast(F32).rearrange("p (c n) -> p c n", c=8)) for k_ in range(2)]
    d_yT = [b_.d for b_ in yT_r]

    def final_prep(ti, t0, n):
        yb = yT_r[ti % 2]
        rs = rstd_of(lambda c: xT[:, c, t0:t0 + n], n, dx[ti])
        for c in range(8):
            P.op(DVE, lambda e, c=c: e.scalar_tensor_tensor(out=yb.t[:, c, 0:n], in0=xT[:, c, t0:t0 + n], scalar=gfi[:, c:c + 1], in1=rs.t[:, :n],
                                                            op0=ALU.mult, op1=ALU.mult), reads=[dx[ti], rs.d, dc], writes=[yb.d])

    def final_emit(ti, t0, n):
        yb = yT_r[ti % 2]
        ntile = (n + 127) // 128
        for tl in range(ntile):
            final_tile(yb, ti, t0, n, tl)

    def final_tile(yb, ti, t0, n, tl):
        m = min(128, n - tl * 128)
        sb = stage.next()
        for hf in range(2):
            pb = ps()
            for c4 in range(4):
                c = hf * 4 + c4
                P.op(PE, lambda e, c=c, c4=c4, pb=pb: e.transpose(out=pb.t[0:m, c4 * 128:(c4 + 1) * 128], in_=yb.t[:, c, tl * 128:tl * 128 + m], identity=ident_f),
                     reads=[yb.d, dc, d_id], writes=[pb.d])
            if hf == 0:
                P.op(ACT, lambda e, pb=pb: e.copy(out=sb.t[0:m, 0:512], in_=pb.t[0:m, :]), reads=[pb.d], writes=[sb.d])
            else:
                P.op(DVE, lambda e, pb=pb: e.tensor_copy(out=sb.t[0:m, 512:1024], in_=pb.t[0:m, :]), reads=[pb.d], writes=[sb.d])
        if ti < 4:
            r0 = t0 + tl * 128
            P.dma(SP, lambda e: e.dma_start(out=y_p[r0:r0 + 128, :], in_=sb.t[:, :]), reads=[sb.d])
        else:
            P.dma(SP, lambda e: e.dma_start(out=y_s[:, :], in_=sb.t[0:NS, :]), reads=[sb.d])
        out_deps.append(sb.d)

    def final_out():
        final_prep(0, *TG[0])
        for ti, (t0, n) in ALLTG:
            if ti + 1 < 5:
                final_prep(ti + 1, *TG[ti + 1])
            final_emit(ti, t0, n)

    def phase_switch(new_deps, old_deps):
        for nd in new_deps:
            for od in old_deps:
                if od.w is not None:
                    s, k = od.w
                    if nd.r.get(s, -1) < k:
                        nd.r[s] = k
                for s, k in od.r.items():
                    if nd.r.get(s, -1) < k:
                        nd.r[s] = k

    consts()
    for i in range(16):
        load_tokmajor_to_T(x_p[i * 128:(i + 1) * 128, :], 128, xT, i * 128, dx[i // 4])
    load_tokmajor_to_T(x_s[:, :], NS, xT, T, dx[4])
    consts_late()
    KST = 9
    if KST >= 1:
        ffn(gf1, w1g, w1u, w1d)
    if KST >= 2:
        norm_to(xT, dx, gmx, hT, dh, ALLTG)
    if KST >= 2:
        for ti, (t0, n) in ALLTG:
            P.dma(SP, lambda e, t0=t0, n=n: e.dma_start(out=xd[:, :, t0:t0 + n], in_=xT[:, :, t0:t0 + n]), reads=[dx[ti]], writes=[d_xd])
        a2_small = ([d_his, d_zs, d_sm, d_hsm, d_mqs, d_msm] + d_S + d_qd4 + d_kd4 + d_kd2 + d_ebe[0] + d_ebe[1] +
                   [d_ for r_ in (ptr, ks_r, kts_r, Sbf4_r, kdt4_r, am4_r, s0_r, s1_r, ohg4_r) for b in r_.bufs for d_ in b.ds] + d_kd2)
        a2_deps = a2_small + [b.d for b in macc_bufs] + [b.d for b in xo_r.bufs]
        phase_switch(a2_deps, dact)
        att_scr = [d_acc, d_att] + d_vs + [st_[k_] for st_ in asets for k_ in ('dq', 'dk', 'dv', 'dvT')]
        phase_switch(att_scr, dx)
        attention_prompt()
        phase_switch(d_vs, [asets[1][k_] for k_ in ('dq', 'dk', 'dv', 'dvT')])
        attention_sample()
        mem_scr = [d_memT, d_mhT, d_mkT, d_mvb, d_ptm, d_mks, d_mvs, d_mkTs, d_memo] + d_mqT4
        phase_switch(mem_scr, att_scr + dx)
        mem_attention()
        hg_scr = [d_vtok, d_hgo]
        phase_switch(hg_scr, att_scr + mem_scr + dx)
        a2_att = [d_zs, d_sm] + [d_ for r_ in (ptr, ks_r, kts_r) for b in r_.bufs for d_ in b.ds]
        a2_hg = (d_S + d_qd4 + d_kd4 + d_kd2 + d_ebe[0] + d_ebe[1] + [d_hsm] +
                 [d_ for r_ in (Sbf4_r, kdt4_r, am4_r, s0_r, s1_r, ohg4_r) for b in r_.bufs for d_ in b.ds])
        phase_switch(a2_hg, a2_att)
        tokmajor_pass()
        hgrn()
        phase_switch(d_mT, hg_scr + att_scr + mem_scr + dx + a2_small)
        merge()
        allmix = d_mT + hg_scr + att_scr + mem_scr + [d_att, d_hgo, d_memo]
        phase_switch(dx, allmix)
        phase_switch(drl, allmix)
        for ti, (t0, n) in ALLTG:
            P.dma(SP, lambda e, t0=t0, n=n: e.dma_start(out=xT[:, 6:8, t0:t0 + n], in_=xd2[:, 6:8, t0:t0 + n]),
                  reads=[d_xd2[6][ti], d_xd2[7][ti]], writes=[drl[ti]])
            P.op(DVE, lambda e, ti=ti: e.tensor_copy(out=vec[:, 58 + ti:59 + ti], in_=vec[:, 57:58]), reads=[dxw[ti], drl[ti], dc], writes=[dx[ti]])
        phase_switch(dact, a2_deps)
    if KST >= 3:
        ffn(gf2, w2g, w2u, w2d)
    phase_switch(d_yT, dact)
    final_out()
    P.op(SP, lambda e: e.nop(), reads=[], writes=out_deps)
    P.emit()
    return nc


_NC = None


def _prep(inp):
    f = lambda a: np.ascontiguousarray(np.asarray(a, dtype=np.float32))
    in_maps = []
    shared = {}
    for k in ("g_ff1", "g_mix", "g_mem", "g_ff2", "g_hg_out", "w_ff1_gate", "w_ff1_up", "w_ff1_down", "w_ff2_gate", "w_ff2_up",
              "w_ff2_down", "w_in", "w_mem_k", "w_mem_v", "w_branch_att", "w_branch_hg", "w_branch_mem", "w_out"):
        shared[k] = f(inp[k][0])
    shared["g_final"] = f(inp["g_final"])
    shared["hg_lb_logits"] = f(inp["hg_lb_logits"])
    cw = {}
    for g, nm in enumerate(("1", "4", "16")):
        cw[(g, "k")] = np.asarray(inp[f"cache_win{nm}_k"])[0]
        cw[(g, "v")] = np.asarray(inp[f"cache_win{nm}_v"])[0]
    for i in range(8):
        m = dict(shared)
        m["x_p"] = f(inp["x_prompt"][i])
        m["x_s"] = f(np.asarray(inp["x_sample"])[4 * i:4 * i + 4, 0, :])
        m["mem"] = f(inp["mem_prompt"][i])
        for g in range(3):
            W = WIN[g][0]
            m[f"cw{g}k"] = f(cw[(g, "k")][4 * i:4 * i + 4].reshape(4, W, 512))
            m[f"cw{g}v"] = f(cw[(g, "v")][4 * i:4 * i + 4].reshape(4, W, 512))
        m["cmk"] = f(np.asarray(inp["cache_mem_k"])[0, 4 * i:4 * i + 4].reshape(4, 256, 512))
        m["cmv"] = f(np.asarray(inp["cache_mem_v"])[0, 4 * i:4 * i + 4].reshape(4, 256, 512))
        m["st"] = f(np.asarray(inp["state_hgrn"])[0, 4 * i:4 * i + 4].reshape(16, 128, 128))
        in_maps.append(m)
    return in_maps


def kernel(**inp):
    global _NC
    if _NC is None:
        _NC = build_nc()
    nc = _NC
    in_maps = _prep(inp)
    res = run_bass_kernel_spmd(nc, in_maps, core_ids=list(range(8)))
    return _assemble(res.results, 8)


def _assemble(R, ncore):
    cat = lambda k: np.stack([np.asarray(R[i][k], dtype=np.float32) for i in range(ncore)], axis=0)
    y_p = cat("y_p")
    y_s = cat("y_s").reshape(4 * ncore, 1, D)
    outs = [y_p, y_s]
    for g in range(3):
        W = WIN[g][0]
        outs.append(cat(f"pw{g}k").reshape(1, ncore, W, 4, 128))
        outs.append(cat(f"pw{g}v").reshape(1, ncore, W, 4, 128))
    outs.append(cat("pmk").reshape(1, ncore, 256, 4, 128))
    outs.append(cat("pmv").reshape(1, ncore, 256, 4, 128))
    outs.append(cat("phg").reshape(1, ncore, 4, 128, 128))
    for g in range(3):
        W = WIN[g][0]
        outs.append(cat(f"sw{g}k").reshape(1, 4 * ncore, W, 4, 128))
        outs.append(cat(f"sw{g}v").reshape(1, 4 * ncore, W, 4, 128))
    outs.append(cat("shg").reshape(1, 4 * ncore, 4, 128, 128))
    return tuple(outs)
```

```python
import numpy as np
import concourse.bass as bass
import concourse.mybir as mybir
from concourse.bass_utils import run_bass_kernel_spmd

F32 = mybir.dt.float32
BF16 = mybir.dt.bfloat16
ALU = mybir.AluOpType
AF = mybir.ActivationFunctionType
AX = mybir.AxisListType

PE, ACT, DVE, POOL, SP = "pe", "act", "dve", "pool", "sp"
COMPUTE = (PE, ACT, DVE, POOL, SP)


class Dep:
    __slots__ = ("w", "r", "name")

    def __init__(self, name=""):
        self.w = None
        self.r = {}
        self.name = name


class Prog:
    def __init__(self, nc, lanes=None):
        self.nc = nc
        self.lanes = lanes or {SP: 12, POOL: 12, ACT: 8, "bulk": 24}
        self.ops = {e: [] for e in COMPUTE}
        self.stream_ops = {}
        self.lane_rr = {q: 0 for q in self.lanes}
        self.waited = {e: {} for e in COMPUTE}

    def _stream_list(self, s):
        return self.stream_ops.setdefault(s, [])

    def _collect(self, reads, writes, own_stream, skip_waw=False):
        deps = {}

        def need(p):
            if p is None:
                return
            s, k = p
            if s == own_stream and s == PE:
                return
            if deps.get(s, -1) < k:
                deps[s] = k
        for d in reads:
            need(d.w)
        for d in writes:
            if not skip_waw:
                need(d.w)
            for s, k in d.r.items():
                need((s, k))
        return deps

    def op(self, eng, fn, reads=(), writes=()):
        lst = self._stream_list(eng)
        idx = len(lst)
        deps = self._collect(reads, writes, eng)
        waits = []
        wd = self.waited[eng]
        for s, k in deps.items():
            if s == eng and k >= idx:
                continue
            if wd.get(s, -1) < k:
                waits.append((s, k))
                wd[s] = k
        rec = dict(eng=eng, stream=eng, idx=idx, fn=fn, waits=waits, sig=False, dma=False)
        lst.append(rec)
        self.ops[eng].append(rec)
        for d in reads:
            if d.r.get(eng, -1) < idx:
                d.r[eng] = idx
        for d in writes:
            d.w = (eng, idx)
            d.r = {}
        return rec

    def dma(self, q, fn, reads=(), writes=(), pool=None, skip_waw=False):
        pool = pool or q
        nl = self.lanes[pool]
        lane = ("dma", pool, self.lane_rr[pool] % nl)
        self.lane_rr[pool] += 1
        lst = self._stream_list(lane)
        idx = len(lst)
        deps = self._collect(reads, writes, lane, skip_waw)
        if idx > 0:
            deps[lane] = max(deps.get(lane, -1), idx - 1)
        waits = []
        wd = self.waited[q]
        for s, k in deps.items():
            if wd.get(s, -1) < k:
                waits.append((s, k))
                wd[s] = k
        rec = dict(eng=q, stream=lane, idx=idx, fn=fn, waits=waits, sig=True, dma=True)
        lst.append(rec)
        self.ops[q].append(rec)
        for d in reads:
            if d.r.get(lane, -1) < idx:
                d.r[lane] = idx
        for d in writes:
            d.w = (lane, idx)
            d.r = {}
        return rec

    def wait_all(self, eng, deps_list):
        def fn(e):
            return e.nop()
        return self.op(eng, fn, reads=deps_list, writes=())

    def emit(self):
        nc = self.nc
        for e in COMPUTE:
            for rec in self.ops[e]:
                for s, k in rec["waits"]:
                    self.stream_ops[s][k]["sig"] = True
        semval = {}
        for s, lst in self.stream_ops.items():
            c = 0
            vals = []
            inc = 16 if isinstance(s, tuple) else 1
            for rec in lst:
                if rec["sig"]:
                    c += inc
                vals.append(c)
            semval[s] = vals
        streams = [s for s in self.stream_ops if any(r["sig"] for r in self.stream_ops[s])]
        import contextlib
        with contextlib.ExitStack() as st:
            sems = {}
            for i, s in enumerate(streams):
                nm = "s_" + (s if isinstance(s, str) else f"{s[1]}{s[2]}")
                sems[s] = st.enter_context(nc.semaphore(nm))
            block = st.enter_context(nc.Block())

            def run(eng_name):
                def body(e):
                    for rec in self.ops[eng_name]:
                        for s, k in rec["waits"]:
                            e.wait_ge(sems[s], semval[s][k])
                        inst = rec["fn"](e)
                        if rec["sig"]:
                            inst.then_inc(sems[rec["stream"]], 16 if rec["dma"] else 1)
                return body
            block.tensor(run(PE))
            block.scalar(run(ACT))
            block.vector(run(DVE))
            block.gpsimd(run(POOL))
            block.sync(run(SP))
        return len(streams)

from contextlib import ExitStack

T = 2048
NS = 4
NT = T + NS
D = 1024
DFF = 2816
NJ = DFF // 128
INW = 9728
TG = [(0, 512), (512, 512), (1024, 512), (1536, 512), (2048, 4)]
WIN = [(128, 1), (512, 4), (2048, 16)]
SCALE = 128.0 ** -0.5
EPS = 1e-6
ARENA = 30720


class Buf:
    def __init__(self, t, nd=1):
        self.t = t
        self.ds = [Dep() for _ in range(nd)]

    @property
    def d(self):
        return self.ds[0]


class Ring:
    def __init__(self, bufs):
        self.bufs = bufs
        self.i = 0

    def next(self):
        b = self.bufs[self.i % len(self.bufs)]
        self.i += 1
        return b


def build_nc():
    nc = bass.Bass("TRN2", target_bir_lowering=False)
    P = Prog(nc)

    def din(name, shape):
        return nc.dram_tensor(name, list(shape), F32, kind="ExternalInput").ap()

    def dout(name, shape):
        return nc.dram_tensor(name, list(shape), F32, kind="ExternalOutput").ap()

    x_p = din("x_p", [T, D]); x_s = din("x_s", [NS, D]); mem = din("mem", [256, D])
    cwk = [din(f"cw{g}k", [NS, WIN[g][0], 512]) for g in range(3)]
    cwv = [din(f"cw{g}v", [NS, WIN[g][0], 512]) for g in range(3)]
    cmk = din("cmk", [NS, 256, 512]); cmv = din("cmv", [NS, 256, 512])
    st_in = din("st", [NS * 4, 128, 128])
    g_ff1 = din("g_ff1", [D]); g_mix = din("g_mix", [D]); g_mem = din("g_mem", [D]); g_ff2 = din("g_ff2", [D])
    g_fin = din("g_final", [D]); g_hg = din("g_hg_out", [512]); lbl = din("hg_lb_logits", [2, 512])
    w1g = din("w_ff1_gate", [D, DFF]); w1u = din("w_ff1_up", [D, DFF]); w1d = din("w_ff1_down", [DFF, D])
    w2g = din("w_ff2_gate", [D, DFF]); w2u = din("w_ff2_up", [D, DFF]); w2d = din("w_ff2_down", [DFF, D])
    w_in = din("w_in", [D, INW]); w_mk = din("w_mem_k", [D, 512]); w_mv = din("w_mem_v", [D, 512])
    w_ba = din("w_branch_att", [512, D]); w_bb = din("w_branch_hg", [512, D]); w_bc = din("w_branch_mem", [512, D])
    w_o = din("w_out", [D, D])

    y_p = dout("y_p", [T, D]); y_s = dout("y_s", [NS, D])
    pwk = [dout(f"pw{g}k", [WIN[g][0], 512]) for g in range(3)]
    pwv = [dout(f"pw{g}v", [WIN[g][0], 512]) for g in range(3)]
    pmk = dout("pmk", [256, 512]); pmv = dout("pmv", [256, 512]); phg = dout("phg", [4, 128, 128])
    swk = [dout(f"sw{g}k", [NS, WIN[g][0], 512]) for g in range(3)]
    swv = [dout(f"sw{g}v", [NS, WIN[g][0], 512]) for g in range(3)]
    shg = dout("shg", [NS * 4, 128, 128])

    out_deps = []

    xT = nc.alloc_sbuf_tensor("xT", [128, 8, NT], F32)
    hT = nc.alloc_sbuf_tensor("hT", [128, 8, NT], BF16)
    dx = [Dep(f"x{i}") for i in range(5)]
    dh = [Dep(f"h{i}") for i in range(5)]
    aarena = nc.alloc_sbuf_tensor("aarena", [128, NJ * 1028], BF16)
    arena = xT[:, :, :].rearrange("p c n -> p (c n)").bitcast(BF16)
    xd = nc.dram_tensor("xd_scratch", [128, 8, NT], F32).ap()
    xd2 = nc.dram_tensor("xd2_scratch", [128, 8, NT], F32).ap()
    _a2 = [0]

    def A2(nel, dt=BF16, parts=128):
        o_ = _a2[0]
        n2 = nel * (2 if dt == F32 else 1)
        _a2[0] = o_ + (n2 + 15) // 16 * 16
        assert _a2[0] <= NJ * 1028, _a2[0]
        _a2.append(_a2[0])
        v = aarena[0:parts, o_:o_ + n2]
        return v.bitcast(F32) if dt == F32 else v
    macc_bufs = [Buf(A2(512, F32)) for _ in range(2)]
    xo_r = Ring([Buf(A2(512, F32)) for _ in range(3)])
    a2_keep = _a2[0]
    wring = Ring([Buf(nc.alloc_sbuf_tensor(f"wp{i}", [128, 4608], BF16), 6) for i in range(3)])
    stage = Ring([Buf(nc.alloc_sbuf_tensor(f"stg{i}", [128, 1024], F32)) for i in range(2)])
    f32r = Ring([Buf(nc.alloc_sbuf_tensor(f"f32r{i}", [128, 512], F32)) for i in range(8)])
    bfr = Ring([Buf(nc.alloc_sbuf_tensor(f"bfr{i}", [128, 512], BF16)) for i in range(3)])
    psr = Ring([Buf(nc.alloc_psum_tensor(f"ps{i}", [128, 512], F32)) for i in range(6)])
    ps_long = Buf(nc.alloc_psum_tensor("pslong", [128, 512], F32))
    ps_long2 = Buf(nc.alloc_psum_tensor("pslong2", [128, 512], F32))
    cst = nc.alloc_sbuf_tensor("cst", [128, 1280], F32)
    cbf = nc.alloc_sbuf_tensor("cbf", [128, 1280], BF16)
    vec = nc.alloc_sbuf_tensor("vec", [128, 64], F32)
    dc = Dep("const")
    d_id = Dep("ident")

    ident_f = cst[:, 0:128]; ones_f = cst[:, 128:256]; rst = cst[:, 256:768]
    sel = cst[0:4, 768:768 + 512].rearrange("p (b n) -> p b n", b=4)
    ident_b = cbf[:, 0:128]; ones_b = cbf[:, 128:256]
    mask2 = cbf[:, 256:512]
    hmask = cbf[:, 512:640]
    hmask4 = cbf[:, 640:1152]
    gf1 = vec[:, 0:8]; gmx = vec[:, 8:16]; gme = vec[:, 16:24]; gf2 = vec[:, 24:32]; gfi = vec[:, 32:40]
    ghg = vec[:, 40:44]; lb = vec[:, 44:48]; oml = vec[:, 48:52]; l1 = vec[:, 52:56]
    epsc = vec[:, 56:57]; zeroc = vec[:, 57:58]

    def ps():
        return psr.next()

    def consts():
        tmp = f32r.next()
        P.op(POOL, lambda e: e.memset(cst[:], 1.0), writes=[dc, d_id])
        P.op(POOL, lambda e: e.memset(ident_f, 0.0), writes=[d_id])
        P.op(POOL, lambda e: e.affine_select(out=ident_f, in_=ident_f, compare_op=ALU.not_equal, fill=1.0,
                                             base=0, pattern=[[-1, 128]], channel_multiplier=1), writes=[d_id])
        P.op(POOL, lambda e: e.memset(cst[:, 256:768].rearrange("p (c n) -> p c n", n=64)[:, :, 0:1], 0.0), writes=[dc])
        P.op(POOL, lambda e: e.affine_select(out=cst[0:4, 768:1280].rearrange("p (b n) -> p b n", b=4),
                                             in_=cst[0:4, 768:1280].rearrange("p (b n) -> p b n", b=4),
                                             compare_op=ALU.is_equal, fill=0.0, base=0,
                                             pattern=[[-1, 4], [0, 128]], channel_multiplier=1), writes=[dc])
        t = tmp.t
        P.op(POOL, lambda e: e.memset(t[:, 0:384], 1.0), writes=[tmp.d])
        P.op(POOL, lambda e: e.affine_select(out=t[:, 0:128], in_=t[:, 0:128], compare_op=ALU.is_ge, fill=0.0,
                                             base=0, pattern=[[1, 128]], channel_multiplier=-1), writes=[tmp.d])
        P.op(POOL, lambda e: e.affine_select(out=t[:, 128:256], in_=t[:, 128:256], compare_op=ALU.is_ge, fill=0.0,
                                             base=0, pattern=[[-1, 128]], channel_multiplier=1), writes=[tmp.d])
        P.op(POOL, lambda e: e.tensor_copy(out=t[:, 256:384], in_=t[:, 0:128]), writes=[tmp.d])
        P.op(POOL, lambda e: e.memset(t[0:64, 320:384], 0.0), writes=[tmp.d])
        P.op(POOL, lambda e: e.tensor_copy(out=cbf[:, 256:640], in_=t[:, 0:384]), reads=[tmp.d], writes=[dc])
        for j_ in range(4):
            P.op(POOL, lambda e, j_=j_: e.tensor_copy(out=cbf[:, 640 + j_ * 128:768 + j_ * 128], in_=t[:, 256:384]), reads=[tmp.d], writes=[dc])
        P.op(POOL, lambda e: e.tensor_copy(out=cbf[:, 0:256], in_=cst[:, 0:256]), reads=[d_id], writes=[dc])
        P.op(POOL, lambda e: e.memset(vec[:, 56:57], EPS), writes=[dc])
        P.op(POOL, lambda e: e.memset(vec[:, 57:58], 0.0), writes=[dc])
        dvec = []
        for dst, src, n in ((gf1, g_ff1, 8), (gmx, g_mix, 8), (gme, g_mem, 8), (gf2, g_ff2, 8), (gfi, g_fin, 8), (ghg, g_hg, 4)):
            dd_ = Dep()
            dvec.append(dd_)
            P.dma(ACT, lambda e, dst=dst, src=src: e.dma_start(out=dst, in_=src.rearrange("(c p) -> p c", p=128),
                                                                allow_slow_non_contiguous=True), writes=[dd_])
        dlb, dl1 = Dep(), Dep()
        P.dma(ACT, lambda e: e.dma_start(out=lb, in_=lbl[0].rearrange("(c p) -> p c", p=128), allow_slow_non_contiguous=True), writes=[dlb])
        P.dma(ACT, lambda e: e.dma_start(out=l1, in_=lbl[1].rearrange("(c p) -> p c", p=128), allow_slow_non_contiguous=True), writes=[dl1])
        late["deps"] = dvec + [dlb, dl1]

    late = {}

    def consts_late():
        dvec_all = late["deps"]
        P.op(DVE, lambda e: e.tensor_copy(out=vec[:, 63:64], in_=vec[:, 57:58]), reads=dvec_all + [dc], writes=[dc])
        P.op(DVE, lambda e: e.tensor_tensor(out=lb, in0=lb, in1=l1, op=ALU.subtract), reads=[dc], writes=[dc])
        P.op(ACT, lambda e: e.activation(out=lb, in_=lb, func=AF.Sigmoid), reads=[dc], writes=[dc])
        P.op(DVE, lambda e: e.tensor_scalar(out=oml, in0=lb, scalar1=-1.0, scalar2=1.0, op0=ALU.mult, op1=ALU.add), reads=[dc], writes=[dc])

    def load_tokmajor_to_T(src_ap, nrows, dstT, tok0, ddst):
        sb = stage.next()
        P.dma(SP, lambda e: e.dma_start(out=sb.t[0:nrows, :], in_=src_ap), writes=[sb.d])
        for hf in range(2):
            pb = ps()
            for c4 in range(4):
                c = hf * 4 + c4
                P.op(PE, lambda e, c=c, c4=c4, pb=pb: e.transpose(out=pb.t[:, c4 * 128:c4 * 128 + nrows],
                                                           in_=sb.t[0:nrows, c * 128:(c + 1) * 128], identity=ident_f[0:nrows, 0:nrows]),
                     reads=[sb.d, d_id], writes=[pb.d])
            eng = ACT if hf == 0 else DVE
            if eng == ACT:
                P.op(ACT, lambda e, hf=hf, pb=pb: e.copy(out=dstT[:, hf * 4:hf * 4 + 4, tok0:tok0 + nrows],
                                                  in_=pb.t[:].rearrange("p (c n) -> p c n", c=4)[:, :, 0:nrows]),
                     reads=[pb.d], writes=[ddst])
            else:
                P.op(DVE, lambda e, hf=hf, pb=pb: e.tensor_copy(out=dstT[:, hf * 4:hf * 4 + 4, tok0:tok0 + nrows],
                                                         in_=pb.t[:].rearrange("p (c n) -> p c n", c=4)[:, :, 0:nrows]),
                     reads=[pb.d], writes=[ddst])

    def rstd_of(srcT_fn, n, dsrc, nchunks=8, denom=1024.0):
        pb = ps()
        for c in range(nchunks):
            sb = bfr.next()
            P.op(ACT, lambda e, c=c, sb=sb: e.activation(out=sb.t[:, :n], in_=srcT_fn(c), func=AF.Square),
                 reads=[dsrc], writes=[sb.d])
            P.op(PE, lambda e, c=c, sb=sb: e.matmul(pb.t[:, :n], lhsT=ones_b, rhs=sb.t[:, :n], start=(c == 0), stop=(c == nchunks - 1)),
                 reads=[sb.d, dc], writes=[pb.d])
        rs = f32r.next()
        P.op(ACT, lambda e: e.activation(out=rs.t[:, :n], in_=pb.t[:, :n], func=AF.Ln, bias=epsc, scale=1.0 / denom),
             reads=[pb.d, dc], writes=[rs.d])
        P.op(ACT, lambda e: e.activation(out=rs.t[:, :n], in_=rs.t[:, :n], func=AF.Exp, scale=-0.5), reads=[rs.d], writes=[rs.d])
        return rs

    def norm_to(srcT, dsrcs, gcol, dstT, ddsts, tgs):
        for ti, (t0, n) in tgs:
            rs = rstd_of(lambda c, t0=t0, n=n: srcT[:, c, t0:t0 + n], n, dsrcs[ti])
            for c in range(8):
                P.op(DVE, lambda e, c=c, t0=t0, n=n, rs=rs: e.scalar_tensor_tensor(
                    out=dstT[:, c, t0:t0 + n], in0=srcT[:, c, t0:t0 + n], scalar=gcol[:, c:c + 1], in1=rs.t[:, :n],
                    op0=ALU.mult, op1=ALU.mult), reads=[dsrcs[ti], rs.d, dc], writes=[ddsts[ti]])

    ALLTG = list(enumerate(TG))

    act = aarena[:, 0:NJ * 1028].rearrange("p (j n) -> p j n", j=NJ)
    dact = [Dep("act0"), Dep("act1"), Dep("act2")]

    def ffn(gcol, wg, wu, wd):
        norm_to(xT, dx, gcol, hT, dh, ALLTG)
        wg_v = wg.rearrange("(c p) n -> p c n", p=128)
        wu_v = wu.rearrange("(c p) n -> p c n", p=128)
        wd_v = wd.rearrange("(j p) n -> p j n", p=128)
        for half in ([0, 1], [2, 3, 4]):
            for j in range(NJ):
                if j % 2 == 0:
                    wb = wring.next()
                    gv2 = wb.t[:, 0:2048].rearrange("p (c n) -> p c n", c=8)
                    uv2 = wb.t[:, 2048:4096].rearrange("p (c n) -> p c n", c=8)
                    P.dma(POOL, lambda e, gv2=gv2, j=j: e.dma_start(out=gv2, in_=wg_v[:, :, j * 128:(j + 2) * 128]), writes=wb.ds)
                    P.dma(POOL, lambda e, uv2=uv2, j=j: e.dma_start(out=uv2, in_=wu_v[:, :, j * 128:(j + 2) * 128]), writes=[wb.ds[1]], skip_waw=True)
                gv = gv2[:, :, (j % 2) * 128:(j % 2 + 1) * 128]
                uv = uv2[:, :, (j % 2) * 128:(j % 2 + 1) * 128]
                for si, ti in enumerate(half):
                    t0, n = TG[ti]
                    off = si * 512
                    pg = ps(); pu = ps()
                    for c in range(8):
                        P.op(PE, lambda e, c=c, gv=gv, pg=pg, t0=t0, n=n: e.matmul(pg.t[:, :n], lhsT=gv[:, c, :], rhs=hT[:, c, t0:t0 + n],
                                                                                  start=(c == 0), stop=(c == 7)),
                             reads=[wb.ds[0], dh[ti]], writes=[pg.d])
                    for c in range(8):
                        P.op(PE, lambda e, c=c, uv=uv, pu=pu, t0=t0, n=n: e.matmul(pu.t[:, :n], lhsT=uv[:, c, :], rhs=hT[:, c, t0:t0 + n],
                                                                                  start=(c == 0), stop=(c == 7)),
                             reads=[wb.ds[1], dh[ti]], writes=[pu.d])
                    sg = f32r.next()
                    P.op(ACT, lambda e, sg=sg, pg=pg, n=n: e.activation(out=sg.t[:, :n], in_=pg.t[:, :n], func=AF.Silu),
                         reads=[pg.d], writes=[sg.d])
                    P.op(DVE, lambda e, sg=sg, pu=pu, n=n, j=j, off=off: e.tensor_tensor(out=act[:, j, off:off + n], in0=pu.t[:, :n],
                                                                                       in1=sg.t[:, :n], op=ALU.mult),
                         reads=[pu.d, sg.d], writes=[dact[si]])
            for dp in range(8):
                wb = wring.next()
                dv = wb.t[:, 0:NJ * 128].rearrange("p (j n) -> p j n", j=NJ)
                P.dma(POOL, lambda e, dv=dv, dp=dp: e.dma_start(out=dv, in_=wd_v[:, :, dp * 128:(dp + 1) * 128]), writes=wb.ds)
                for si, ti in enumerate(half):
                    t0, n = TG[ti]
                    off = si * 512
                    pd = ps()
                    for j in range(NJ):
                        P.op(PE, lambda e, j=j, dv=dv, pd=pd, off=off, n=n: e.matmul(pd.t[:, :n], lhsT=dv[:, j, :], rhs=act[:, j, off:off + n],
                                                                                    start=(j == 0), stop=(j == NJ - 1)),
                             reads=[wb.ds[0], dact[si]], writes=[pd.d])
                    P.op(DVE, lambda e, pd=pd, dp=dp, t0=t0, n=n: e.scalar_tensor_tensor(
                        out=xT[:, dp, t0:t0 + n], in0=pd.t[:, :n], scalar=0.5, in1=xT[:, dp, t0:t0 + n], op0=ALU.mult, op1=ALU.add),
                        reads=[pd.d, dx[ti]], writes=[dx[ti]])

    o = 0
    attT = arena[:, o:o + 4 * NT].rearrange("p (h n) -> p h n", h=4); o += 4 * NT
    hgoT = arena[:, o:o + 4 * NT].rearrange("p (h n) -> p h n", h=4); o += 4 * NT
    memoT = arena[:, o:o + 4 * NT].rearrange("p (h n) -> p h n", h=4); o += 4 * NT
    o = (o + 15) // 16 * 16
    scr0 = o
    d_att, d_hgo, d_memo = Dep("attT"), Dep("hgoT"), Dep("memoT")
    d_scr = Dep("scratch")

    w_in_v = w_in.rearrange("(c p) n -> p c n", p=128)

    def load_wchunk(src_v, col0, ncols=128, nk=8):
        wb = wring.next()
        v = wb.t[:, 0:nk * ncols].rearrange("p (c n) -> p c n", c=nk)
        P.dma(POOL, lambda e: e.dma_start(out=v, in_=src_v[:, :, col0:col0 + ncols]), writes=wb.ds)
        return wb, v

    def proj_fm(wb, wv, srcT, dsrc_of, tgs, sink, nk=8):
        for ti, (t0, n) in tgs:
            pb = ps()
            for c in range(nk):
                P.op(PE, lambda e, c=c, pb=pb, t0=t0, n=n: e.matmul(pb.t[:, :n], lhsT=wv[:, c, :], rhs=srcT[:, c, t0:t0 + n],
                                                                   start=(c == 0), stop=(c == nk - 1)),
                     reads=[wb.ds[0], dsrc_of(ti)], writes=[pb.d])
            sink(ti, t0, n, pb)

    vtok_o = scr0
    vtok = arena[:, vtok_o:vtok_o + 16 * 512].rearrange("p (i n) -> p i n", i=16)
    d_vtok = Dep("vtok")
    his = A2(512, F32, parts=4)
    a2_after_his = _a2[0]
    d_his = Dep("his")

    def tokmajor_pass():
        jobs = []
        for g in range(3):
            keep = WIN[g][0]
            tiles = list(range(16 - keep // 128, 16))
            jobs.append((1536 + g * 512, tiles, ("k", g)))
            jobs.append((3072 + g * 512, tiles, ("v", g)))
        jobs.append((5632, list(range(16)), ("hi", 0)))
        for col0, tiles, (kind, g) in jobs:
            wb, wv = load_wchunk(w_in_v, col0, 512)
            for i in tiles + [16]:
                tok0, m = (i * 128, 128) if i < 16 else (T, NS)
                ti = min(i // 4, 4)
                pb = ps()
                for c in range(8):
                    P.op(PE, lambda e, c=c, pb=pb, tok0=tok0, m=m, wv=wv: e.matmul(pb.t[0:m, :], lhsT=hT[:, c, tok0:tok0 + m], rhs=wv[:, c, :],
                                                                                 start=(c == 0), stop=(c == 7)),
                         reads=[wb.ds[0], dh[ti]], writes=[pb.d])
                if kind == "hi":
                    if i < 16:
                        P.op(ACT, lambda e, pb=pb, i=i: e.copy(out=vtok[:, i, :], in_=pb.t[:, :]), reads=[pb.d], writes=[d_vtok])
                    else:
                        P.op(ACT, lambda e, pb=pb: e.copy(out=his[:, :], in_=pb.t[0:NS, :]), reads=[pb.d], writes=[d_his])
                else:
                    sb = stage.next()
                    eng = ACT if (i % 2 == 0) else DVE
                    if eng == ACT:
                        P.op(ACT, lambda e, pb=pb, sb=sb, m=m: e.copy(out=sb.t[0:m, 0:512], in_=pb.t[0:m, :]), reads=[pb.d], writes=[sb.d])
                    else:
                        P.op(DVE, lambda e, pb=pb, sb=sb, m=m: e.tensor_copy(out=sb.t[0:m, 0:512], in_=pb.t[0:m, :]), reads=[pb.d], writes=[sb.d])
                    keep = WIN[g][0]
                    if i < 16:
                        dst = (pwk if kind == "k" else pwv)[g]
                        r0 = i * 128 - (T - keep)
                        P.dma(SP, lambda e, dst=dst, r0=r0, sb=sb: e.dma_start(out=dst[r0:r0 + 128, :], in_=sb.t[:, 0:512]), reads=[sb.d])
                    else:
                        dst = (swk if kind == "k" else swv)[g]
                        P.dma(SP, lambda e, dst=dst, keep=keep, sb=sb: e.dma_start(out=dst[:, keep - 1, :], in_=sb.t[0:NS, 0:512]), reads=[sb.d])
                    out_deps.append(sb.d)

    o2 = 4 * NT
    asets = []
    for base_ in (o2, scr0):
        ob_ = base_
        st_ = {}
        st_["q"] = arena[:, ob_:ob_ + T]; ob_ += T
        st_["k"] = arena[:, ob_:ob_ + T]; ob_ += T
        st_["v"] = arena[:, ob_:ob_ + T].rearrange("p (i n) -> p i n", i=16); ob_ += T
        st_["vT"] = arena[:, ob_:ob_ + T]; ob_ += T
        st_["dq"], st_["dk"], st_["dv"], st_["dvT"] = Dep(), Dep(), Dep(), Dep()
        asets.append(st_)
    o2 += 4 * T
    assert scr0 + 4 * T <= 8 * NT * 2
    acc = arena[:, o2:o2 + 8192].bitcast(F32).rearrange("p (a n) -> p a n", a=2); o2 += 8192
    att_end = o2
    d_acc = Dep("acc")
    zs_att = A2(36 * NS, F32).rearrange("p (c n) -> p c n", c=36)
    d_zs = Dep("zs_att")
    ptr = Ring([Buf(A2(256)) for i in range(4)])

    bulk = []
    for g_ in range(3):
        win_ = WIN[g_][0]
        for b_ in range(NS):
            for src_, dst_ in ((cwk[g_], swk[g_]), (cwv[g_], swv[g_])):
                r_ = 0
                while r_ < win_ - 1:
                    nr_ = min(512, win_ - 1 - r_)
                    bulk.append((src_[b_].rearrange("w n -> (w n)")[(r_ + 1) * 512:(r_ + 1 + nr_) * 512],
                                 dst_[b_].rearrange("w n -> (w n)")[r_ * 512:(r_ + nr_) * 512]))
                    r_ += nr_

    def issue_bulk(k, pace):
        for _ in range(k):
            if not bulk:
                return
            src_, dst_ = bulk.pop()
            dd = Dep()
            P.dma(SP, lambda e, src_=src_, dst_=dst_: e.dma_start(out=dst_, in_=src_), reads=pace, writes=[dd], pool="bulk")
            out_deps.append(dd)

    def attention_prompt():
        order = [(s, g) for s in range(4) for g in range(3)]
        att_proj(order[0][0], order[0][1], 0)
        for i, (s, g) in enumerate(order):
            if i + 1 < len(order):
                att_proj(order[i + 1][0], order[i + 1][1], (i + 1) % 2)
            att_blocks(s, g, i % 2)
            if i >= 2:
                issue_bulk(2, [asets[i % 2]["dq"]])
            if g == 2:
                P.op(ACT, lambda e: e.activation(out=acc[:, 1, :], in_=acc[:, 1, :], func=AF.Ln), reads=[d_acc], writes=[d_acc])
                P.op(ACT, lambda e: e.activation(out=acc[:, 1, :], in_=acc[:, 1, :], func=AF.Exp, scale=-1.0), reads=[d_acc], writes=[d_acc])
                P.op(DVE, lambda e, s=s: e.tensor_tensor(out=attT[:, s, 0:T], in0=acc[:, 0, :], in1=acc[:, 1, :], op=ALU.mult),
                     reads=[d_acc], writes=[d_att])

    def att_proj(s, g, si):
        A_ = asets[si]
        qT_, kT_, vg_, vT_ = A_["q"], A_["k"], A_["v"], A_["vT"]
        win, dil = WIN[g]

        def perm_sink(dst, ddst, kidx):
            def sink(ti, t0, n, pb):
                if ti == 4:
                    zi = kidx * 12 + g * 4 + s
                    P.op(DVE, lambda e: e.tensor_copy(out=zs_att[:, zi, :], in_=pb.t[:, 0:NS]),
                         reads=[pb.d], writes=[d_zs])
                    return
                ov = dst.rearrange("p (r l) -> p r l", r=dil)[:, :, t0 // dil:(t0 + 512) // dil]
                iv = pb.t[:, 0:512].rearrange("p (i r) -> p r i", r=dil)
                if ti % 2 == 0:
                    P.op(ACT, lambda e: e.copy(out=ov, in_=iv), reads=[pb.d], writes=[ddst])
                else:
                    P.op(DVE, lambda e: e.tensor_copy(out=ov, in_=iv), reads=[pb.d], writes=[ddst])
            return sink
        wbq, wq = load_wchunk(w_in_v, g * 512 + s * 128)
        wbk, wk = load_wchunk(w_in_v, 1536 + g * 512 + s * 128)
        wbv, wvv = load_wchunk(w_in_v, 3072 + g * 512 + s * 128)
        proj_fm(wbq, wq, hT, lambda ti: dh[ti], ALLTG, perm_sink(qT_, A_["dq"], 0))
        proj_fm(wbk, wk, hT, lambda ti: dh[ti], ALLTG, perm_sink(kT_, A_["dk"], 1))
        proj_fm(wbv, wvv, hT, lambda ti: dh[ti], ALLTG, perm_sink(vT_, A_["dvT"], 2))
        for q4 in range(4):
            tp = ps()
            tpv = tp.t[:].bitcast(BF16)
            for j in range(4):
                tile = q4 * 4 + j
                P.op(PE, lambda e, j=j, tile=tile, tpv=tpv: e.transpose(out=tpv[:, j * 128:(j + 1) * 128], in_=vT_[:, tile * 128:(tile + 1) * 128], identity=ident_b),
                     reads=[A_["dvT"], dc], writes=[tp.d])
            if q4 % 2 == 0:
                P.op(ACT, lambda e, tpv=tpv, q4=q4: e.copy(out=vg_[:, q4 * 4:q4 * 4 + 4, :], in_=tpv[:, 0:512].rearrange("p (i n) -> p i n", i=4)),
                     reads=[tp.d], writes=[A_["dv"]])
            else:
                P.op(DVE, lambda e, tpv=tpv, q4=q4: e.tensor_copy(out=vg_[:, q4 * 4:q4 * 4 + 4, :], in_=tpv[:, 0:512].rearrange("p (i n) -> p i n", i=4)),
                     reads=[tp.d], writes=[A_["dv"]])

    def att_blocks(s, g, si):
        A_ = asets[si]
        qT_, kT_, vg_ = A_["q"], A_["k"], A_["v"]
        d_q_, d_k_, d_v_ = A_["dq"], A_["dk"], A_["dv"]
        win, dil = WIN[g]
        L = T // dil
        nb = L // 128

        def scores(tile):
            r, b = tile // nb, tile % nb
            p0 = tile * 128
            w = 128 if b == 0 else 256
            sc = ps()
            P.op(PE, lambda e: e.matmul(sc.t[:, 0:128], lhsT=kT_[:, p0:p0 + 128], rhs=qT_[:, p0:p0 + 128], start=True, stop=True),
                 reads=[d_q_, d_k_], writes=[sc.d])
            if b > 0:
                P.op(PE, lambda e: e.matmul(sc.t[:, 128:256], lhsT=kT_[:, p0 - 128:p0], rhs=qT_[:, p0:p0 + 128], start=True, stop=True),
                     reads=[d_q_, d_k_], writes=[sc.d])
            pt = ptr.next()
            P.op(ACT, lambda e: e.activation(out=pt.t[:, 0:w], in_=sc.t[:, 0:w], func=AF.Exp, scale=SCALE),
                 reads=[sc.d], writes=[pt.d])
            P.op(DVE, lambda e: e.tensor_tensor(out=pt.t[:, 0:w], in0=pt.t[:, 0:w], in1=mask2[:, 0:w], op=ALU.mult),
                 reads=[pt.d, dc], writes=[pt.d])
            return pt

        def pv(tile, pt):
            r, b = tile // nb, tile % nb
            ob = ps()
            P.op(PE, lambda e: e.matmul(ob.t[:, 0:128], lhsT=vg_[:, tile, :], rhs=pt.t[:, 0:128], start=True, stop=(b == 0)),
                 reads=[pt.d, d_v_], writes=[ob.d])
            if b > 0:
                P.op(PE, lambda e: e.matmul(ob.t[:, 0:128], lhsT=vg_[:, tile - 1, :], rhs=pt.t[:, 128:256], start=False, stop=True),
                     reads=[pt.d, d_v_], writes=[ob.d])
            P.op(PE, lambda e: e.matmul(ob.t[:, 128:256], lhsT=ones_b, rhs=pt.t[:, 0:128], start=True, stop=(b == 0)),
                 reads=[pt.d, dc], writes=[ob.d])
            if b > 0:
                P.op(PE, lambda e: e.matmul(ob.t[:, 128:256], lhsT=ones_b, rhs=pt.t[:, 128:256], start=False, stop=True),
                     reads=[pt.d, dc], writes=[ob.d])
            start = r + dil * 128 * b
            stop = start + dil * 127 + 1
            av = acc[:, :, start:stop:dil]
            iv = ob.t[:, 0:256].rearrange("p (a n) -> p a n", a=2)
            if g == 0:
                P.op(ACT, lambda e: e.copy(out=av, in_=iv), reads=[ob.d], writes=[d_acc])
            else:
                P.op(DVE, lambda e: e.tensor_tensor(out=av, in0=iv, in1=av, op=ALU.add), reads=[ob.d, d_acc], writes=[d_acc])

        pts = {0: scores(0)}
        for tile in range(16):
            if tile + 1 < 16:
                pts[tile + 1] = scores(tile + 1)
            pv(tile, pts.pop(tile))

    o3 = scr0
    vs_all = arena[:, o3:o3 + 12 * 512].rearrange("p (i n) -> p i n", i=12); o3 += 12 * 512
    d_vs = [Dep(f"vs{i}") for i in range(12)]
    ks_r = Ring([Buf(A2(512)) for i in range(2)])
    kts_r = Ring([Buf(A2(512)) for i in range(2)])
    smallf = A2(256, F32)
    smallb = A2(256)
    d_sm = Dep("small")

    def attention_sample():
        qs_b = smallb[:, 0:48].rearrange("p (c n) -> p c n", c=12)
        P.op(DVE, lambda e: e.tensor_copy(out=qs_b, in_=zs_att[:, 0:12, :]), reads=[d_zs], writes=[d_sm])
        scs = ps_long
        for b in range(NS):
            for g in range(3):
                win, dil = WIN[g]
                i = b * 3 + g
                kb = ks_r.next()
                P.dma(POOL, lambda e, kb=kb, b=b, g=g, win=win, dil=dil: e.dma_start(out=kb.t[:, :], in_=cwk[g][b, 0:win - dil + 1:dil, :]), writes=[kb.d])
                P.dma(POOL, lambda e, i=i, b=b, g=g, win=win, dil=dil: e.dma_start(out=vs_all[:, i, :], in_=cwv[g][b, 0:win - dil + 1:dil, :]), writes=[d_vs[i]])
                tp = ps()
                tpv = tp.t[:].bitcast(BF16)
                for h in range(4):
                    P.op(PE, lambda e, h=h, kb=kb, tpv=tpv: e.transpose(out=tpv[:, h * 128:(h + 1) * 128], in_=kb.t[:, h * 128:(h + 1) * 128], identity=ident_b),
                         reads=[kb.d, dc], writes=[tp.d])
                kt = kts_r.next()
                P.op(ACT, lambda e, kt=kt, tpv=tpv: e.copy(out=kt.t[:, :], in_=tpv[:, 0:512]), reads=[tp.d], writes=[kt.d])
                for h in range(4):
                    col = (g * 4 + h) * 4 + b
                    P.op(PE, lambda e, h=h, kt=kt, col=col, g=g, b=b: e.matmul(scs.t[:, col:col + 1], lhsT=kt.t[:, h * 128:(h + 1) * 128],
                                                                             rhs=qs_b[:, g * 4 + h, b:b + 1], start=True, stop=True),
                         reads=[kt.d, d_sm], writes=[scs.d])
        pts = smallb[:, 64:112]
        P.op(ACT, lambda e: e.activation(out=pts, in_=scs.t[:, 0:48], func=AF.Exp, scale=SCALE), reads=[scs.d], writes=[d_sm])
        obs = ps()
        for b in range(NS):
            for g in range(3):
                i = b * 3 + g
                for h in range(4):
                    col = (g * 4 + h) * 4 + b
                    P.op(PE, lambda e, i=i, h=h, col=col: e.matmul(obs.t[:, col:col + 1], lhsT=vs_all[:, i, h * 128:(h + 1) * 128], rhs=pts[:, col:col + 1],
                                                                 start=True, stop=True), reads=[d_vs[i], d_sm], writes=[obs.d])
        P.op(PE, lambda e: e.matmul(obs.t[:, 64:112], lhsT=ones_b, rhs=pts, start=True, stop=True), reads=[d_sm, dc], writes=[obs.d])
        prod = smallf[:, 0:48]
        P.op(DVE, lambda e: e.tensor_tensor(out=prod.rearrange("p (c n) -> p c n", c=12), in0=zs_att[:, 0:12, :], in1=zs_att[:, 12:24, :], op=ALU.mult),
             reads=[d_zs], writes=[d_sm])
        sn = ps()
        P.op(PE, lambda e: e.matmul(sn.t[:, 0:48], lhsT=ones_f, rhs=prod, start=True, stop=True), reads=[d_sm, dc], writes=[sn.d])
        pn = smallf[:, 48:96]
        P.op(ACT, lambda e: e.activation(out=pn, in_=sn.t[:, 0:48], func=AF.Exp, scale=SCALE), reads=[sn.d], writes=[d_sm])
        num = smallf[:, 96:144]; den = smallf[:, 144:192]
        P.op(DVE, lambda e: e.tensor_tensor(out=num.rearrange("p (c n) -> p c n", c=12), in0=pn.rearrange("p (c n) -> p c n", c=12),
                                            in1=zs_att[:, 24:36, :], op=ALU.mult), reads=[d_sm, d_zs], writes=[d_sm])
        P.op(DVE, lambda e: e.tensor_tensor(out=num, in0=obs.t[:, 0:48], in1=num, op=ALU.add), reads=[obs.d, d_sm], writes=[d_sm])
        P.op(DVE, lambda e: e.tensor_tensor(out=den, in0=obs.t[:, 64:112], in1=pn, op=ALU.add), reads=[obs.d, d_sm], writes=[d_sm])
        for tq in (num, den):
            P.op(DVE, lambda e, tq=tq: e.tensor_tensor(out=tq[:, 0:16], in0=tq[:, 0:16], in1=tq[:, 16:32], op=ALU.add), reads=[d_sm], writes=[d_sm])
            P.op(DVE, lambda e, tq=tq: e.tensor_tensor(out=tq[:, 0:16], in0=tq[:, 0:16], in1=tq[:, 32:48], op=ALU.add), reads=[d_sm], writes=[d_sm])
        P.op(DVE, lambda e: e.reciprocal(out=den[:, 0:16], in_=den[:, 0:16]), reads=[d_sm], writes=[d_sm])
        P.op(DVE, lambda e: e.tensor_tensor(out=attT[:, :, T:NT], in0=num[:, 0:16].rearrange("p (h b) -> p h b", h=4),
                                            in1=den[:, 0:16].rearrange("p (h b) -> p h b", h=4), op=ALU.mult), reads=[d_sm], writes=[d_att])

    _a2_att_end = _a2[0]
    _a2[0] = a2_after_his
    Sbf4_r = Ring([Buf(A2(512)) for i in range(2)])
    kdt4_r = Ring([Buf(A2(512)) for i in range(2)])
    am4_r = Ring([Buf(A2(512)) for i in range(2)])
    qd4 = A2(4 * 512).rearrange("p (h n) -> p h n", h=4)
    kd4 = A2(4 * 512).rearrange("p (h n) -> p h n", h=4)
    d_qd4 = [Dep(f"qd4{h}") for h in range(4)]
    d_kd4 = [Dep(f"kd4{h}") for h in range(4)]
    S4 = A2(512, F32)
    d_S = [Dep(f"S{h}") for h in range(4)]
    kd2_4 = A2(4 * 512).rearrange("p (h n) -> p h n", h=4)
    d_kd2 = [Dep(f"kd2{h}") for h in range(4)]
    s0_r = Ring([Buf(A2(128, F32)) for i in range(2)])
    s1_r = Ring([Buf(A2(128, F32)) for i in range(2)])
    ohg4_r = Ring([Buf(A2(512, F32)) for i in range(2)])
    hsm = A2(64, F32)
    d_hsm = Dep("hsm")
    ebe4 = A2(64, F32)
    d_ebe = [[Dep() for h in range(4)] for _ in range(2)]
    oo_banks = [ps_long, ps_long2]
    _a2[0] = max(_a2[0], _a2_att_end)

    def hgrn():
        wbq, wq = load_wchunk(w_in_v, 4608, 512)
        wbf, wf = load_wchunk(w_in_v, 5120, 512)
        for h in range(4):
            P.op(POOL, lambda e, h=h: e.memset(S4[:, h * 128:(h + 1) * 128], 0.0), reads=[], writes=[d_S[h]])
        sb0 = Sbf4_r.next()
        P.op(POOL, lambda e: e.memset(sb0.t, 0.0), writes=[sb0.d])
        cur = {"sbf": sb0, "pair": 0}
        for ti, (t0, n) in ALLTG:
            if ti < 4:
                hg_gates2((0, 1), ti, t0, n, wbq, wq, wbf, wf)
                hg_gates2((2, 3), ti, t0, n, wbq, wq, wbf, wf)
            else:
                for h in range(4):
                    g_ = hg_gates(h, ti, t0, n, wbq, wq, wbf, wf)
                    hg_sample(h, t0, g_[0], g_[1])
            if ti < 4:
                pre = hg_pair_pre(ti, 0, cur)
                tail = None
                for pp in range(4):
                    nxt = hg_pair_pre(ti, pp + 1, cur) if pp + 1 < 4 else None
                    ntail = hg_pair(ti, t0, pp, cur, pre)
                    if tail is not None:
                        tail()
                    tail = ntail
                    pre = nxt
                    issue_bulk(2, [d_hgo])
                tail()
            if ti == 3:
                for h in range(4):
                    P.dma(SP, lambda e, h=h: e.dma_start(out=phg[h], in_=S4[:, h * 128:(h + 1) * 128]), reads=[d_S[h]])
        out_deps.extend(d_S)
        issue_bulk(1000, [])

    def hg_gates2(hs_, ti, t0, n, wbq, wq, wbf, wf):
        par = ti % 2
        st = {}
        for h in hs_:
            hsl = slice(h * 128, (h + 1) * 128)
            pbq = ps()
            for c in range(8):
                P.op(PE, lambda e, c=c, pbq=pbq, hsl=hsl: e.matmul(pbq.t[:, :n], lhsT=wq[:, c, hsl], rhs=hT[:, c, t0:t0 + n], start=(c == 0), stop=(c == 7)),
                     reads=[wbq.ds[0], dh[ti]], writes=[pbq.d])
            qg = f32r.next()
            P.op(ACT, lambda e, qg=qg, pbq=pbq: e.activation(out=qg.t[:, :n], in_=pbq.t[:, :n], func=AF.Sigmoid), reads=[pbq.d], writes=[qg.d])
            pbf = ps()
            for c in range(8):
                P.op(PE, lambda e, c=c, pbf=pbf, hsl=hsl: e.matmul(pbf.t[:, :n], lhsT=wf[:, c, hsl], rhs=hT[:, c, t0:t0 + n], start=(c == 0), stop=(c == 7)),
                     reads=[wbf.ds[0], dh[ti]], writes=[pbf.d])
            kk = f32r.next()
            P.op(ACT, lambda e, kk=kk, pbf=pbf: e.activation(out=kk.t[:, :n], in_=pbf.t[:, :n], func=AF.Sigmoid), reads=[pbf.d], writes=[kk.d])
            st[h] = dict(qg=qg, kk=kk)
        for h in hs_:
            kk = st[h]["kk"]
            lf = f32r.next()
            st[h]["lf"] = lf
            P.op(ACT, lambda e, lf=lf, kk=kk, h=h: e.activation(out=lf.t[:, :], in_=kk.t[:, :], func=AF.Ln, bias=lb[:, h:h + 1], scale=oml[:, h:h + 1]),
                 reads=[kk.d, dc], writes=[lf.d])
            P.op(DVE, lambda e, kk=kk, h=h: e.tensor_scalar(out=kk.t[:, :n], in0=kk.t[:, :n], scalar1=oml[:, h:h + 1], scalar2=None, op0=ALU.mult),
                 reads=[kk.d, dc], writes=[kk.d])
        for h in hs_:
            lf = st[h]["lf"]
            P.op(DVE, lambda e, lf=lf: e.tensor_tensor_scan(out=lf.t[:, :], data0=rst, data1=lf.t[:, :], initial=zeroc, op0=ALU.mult, op1=ALU.add),
                 reads=[lf.d, dc], writes=[lf.d])
        for h in hs_:
            lf = st[h]["lf"]
            eb = f32r.next()
            st[h]["eb"] = eb
            P.op(ACT, lambda e, eb=eb, lf=lf: e.activation(out=eb.t[:, :], in_=lf.t[:, :], func=AF.Exp), reads=[lf.d], writes=[eb.d])
        for h in hs_:
            eb, qg = st[h]["eb"], st[h]["qg"]
            ev = ebe4[:, par * 32 + h * 8:par * 32 + h * 8 + 8]
            st[h]["ev"] = ev
            P.op(DVE, lambda e, ev=ev, eb=eb: e.tensor_copy(out=ev, in_=eb.t[:, 63:512:64]), reads=[eb.d], writes=[d_ebe[par][h]])
            P.op(DVE, lambda e, h=h, qg=qg, eb=eb: e.tensor_tensor(out=qd4[:, h, :], in0=qg.t[:, :], in1=eb.t[:, :], op=ALU.mult),
                 reads=[qg.d, eb.d], writes=[d_qd4[h]])
        for h in hs_:
            eb, lf = st[h]["eb"], st[h]["lf"]
            P.op(ACT, lambda e, eb=eb, lf=lf: e.activation(out=eb.t[:, :], in_=lf.t[:, :], func=AF.Exp, scale=-1.0), reads=[lf.d], writes=[eb.d])
        for h in hs_:
            eb, kk, ev = st[h]["eb"], st[h]["kk"], st[h]["ev"]
            P.op(DVE, lambda e, h=h, kk=kk, eb=eb: e.tensor_tensor(out=kd4[:, h, :], in0=kk.t[:, :], in1=eb.t[:, :], op=ALU.mult),
                 reads=[kk.d, eb.d], writes=[d_kd4[h]])
            P.op(DVE, lambda e, h=h, ev=ev: e.tensor_tensor(out=kd2_4[:, h, :].rearrange("p (c s) -> p c s", s=64), in0=kd4[:, h, :].rearrange("p (c s) -> p c s", s=64),
                                                     in1=ev.unsqueeze(2).to_broadcast([128, 8, 64]), op=ALU.mult),
                 reads=[d_kd4[h], d_ebe[par][h]], writes=[d_kd2[h]])

    def hg_gates(h, ti, t0, n, wbq, wq, wbf, wf):
        hs = slice(h * 128, (h + 1) * 128)
        pbq = ps()
        for c in range(8):
            P.op(PE, lambda e, c=c: e.matmul(pbq.t[:, :n], lhsT=wq[:, c, hs], rhs=hT[:, c, t0:t0 + n], start=(c == 0), stop=(c == 7)),
                 reads=[wbq.ds[0], dh[ti]], writes=[pbq.d])
        qg = f32r.next()
        P.op(ACT, lambda e: e.activation(out=qg.t[:, :n], in_=pbq.t[:, :n], func=AF.Sigmoid), reads=[pbq.d], writes=[qg.d])
        pbf = ps()
        for c in range(8):
            P.op(PE, lambda e, c=c: e.matmul(pbf.t[:, :n], lhsT=wf[:, c, hs], rhs=hT[:, c, t0:t0 + n], start=(c == 0), stop=(c == 7)),
                 reads=[wbf.ds[0], dh[ti]], writes=[pbf.d])
        kk = f32r.next()
        P.op(ACT, lambda e: e.activation(out=kk.t[:, :n], in_=pbf.t[:, :n], func=AF.Sigmoid), reads=[pbf.d], writes=[kk.d])
        P.op(DVE, lambda e: e.tensor_scalar(out=kk.t[:, :n], in0=kk.t[:, :n], scalar1=oml[:, h:h + 1], scalar2=None, op0=ALU.mult),
             reads=[kk.d, dc], writes=[kk.d])
        if ti == 4:
            return qg, kk
        par = ti % 2
        lf = f32r.next()
        P.op(ACT, lambda e: e.activation(out=lf.t[:, :], in_=kk.t[:, :], func=AF.Ln, bias=lb[:, h:h + 1], scale=1.0),
             reads=[kk.d, dc], writes=[lf.d])
        P.op(DVE, lambda e: e.tensor_tensor_scan(out=lf.t[:, :], data0=rst, data1=lf.t[:, :], initial=zeroc, op0=ALU.mult, op1=ALU.add),
             reads=[lf.d, dc], writes=[lf.d])
        eb = f32r.next()
        P.op(ACT, lambda e: e.activation(out=eb.t[:, :], in_=lf.t[:, :], func=AF.Exp), reads=[lf.d], writes=[eb.d])
        ev = ebe4[:, par * 32 + h * 8:par * 32 + h * 8 + 8]
        P.op(DVE, lambda e: e.tensor_copy(out=ev, in_=eb.t[:, 63:512:64]), reads=[eb.d], writes=[d_ebe[par][h]])
        P.op(DVE, lambda e: e.tensor_tensor(out=qd4[:, h, :], in0=qg.t[:, :], in1=eb.t[:, :], op=ALU.mult),
             reads=[qg.d, eb.d], writes=[d_qd4[h]])
        P.op(ACT, lambda e: e.activation(out=eb.t[:, :], in_=lf.t[:, :], func=AF.Exp, scale=-1.0), reads=[lf.d], writes=[eb.d])
        P.op(DVE, lambda e: e.tensor_tensor(out=kd4[:, h, :], in0=kk.t[:, :], in1=eb.t[:, :], op=ALU.mult),
             reads=[kk.d, eb.d], writes=[d_kd4[h]])
        P.op(DVE, lambda e: e.tensor_tensor(out=kd2_4[:, h, :].rearrange("p (c s) -> p c s", s=64), in0=kd4[:, h, :].rearrange("p (c s) -> p c s", s=64),
                                            in1=ev.unsqueeze(2).to_broadcast([128, 8, 64]), op=ALU.mult),
             reads=[d_kd4[h], d_ebe[par][h]], writes=[d_kd2[h]])
        return None

    def hg_pair_pre(ti, pp, cur):
        tile = ti * 4 + pp
        c0 = pp * 128
        tp = ps()
        tpv = tp.t[:].bitcast(BF16)
        for h in range(4):
            P.op(PE, lambda e, h=h: e.transpose(out=tpv[:, h * 128:(h + 1) * 128], in_=kd2_4[:, h, c0:c0 + 128], identity=ident_b),
                 reads=[d_kd2[h], dc], writes=[tp.d])
        kdt = kdt4_r.next()
        P.op(ACT, lambda e: e.copy(out=kdt.t, in_=tpv[:, 0:512]), reads=[tp.d], writes=[kdt.d])
        at = ps()
        for h in range(4):
            P.op(PE, lambda e, h=h: e.matmul(at.t[:, h * 128:(h + 1) * 128], lhsT=kd4[:, h, c0:c0 + 128], rhs=qd4[:, h, c0:c0 + 128], start=True, stop=True),
                 reads=[d_kd4[h], d_qd4[h]], writes=[at.d])
        am = am4_r.next()
        P.op(DVE, lambda e: e.tensor_tensor(out=am.t, in0=at.t[:, :], in1=hmask4, op=ALU.mult), reads=[at.d, dc], writes=[am.d])
        oo = oo_banks[cur["pair"] % 2]
        cur["pair"] += 1
        P.op(DVE, lambda e: e.memset(oo.t[:, :], 0.0), reads=[], writes=[oo.d])
        for h in range(4):
            P.op(PE, lambda e, h=h: e.matmul(oo.t[:, h * 128:(h + 1) * 128], lhsT=vtok[:, tile, h * 128:(h + 1) * 128], rhs=am.t[:, h * 128:(h + 1) * 128],
                                             start=False, stop=True, skip_group_check=True), reads=[am.d, d_vtok], writes=[oo.d])
        return kdt, oo

    def hg_pair(ti, t0, pp, cur, pre):
        kdt, oo = pre
        par = ti % 2
        tile = ti * 4 + pp
        c0 = pp * 128
        sus = []
        for cc in range(2):
            base = cc * 64
            su = ps()
            for h in range(4):
                P.op(PE, lambda e, h=h, su=su, base=base: e.matmul(su.t[:, h * 128:(h + 1) * 128], lhsT=kdt.t[base:base + 64, h * 128:(h + 1) * 128],
                                                                   rhs=vtok[base:base + 64, tile, h * 128:(h + 1) * 128], start=True, stop=True),
                     reads=[kdt.d, d_vtok], writes=[su.d])
            sus.append(su)
        for cc in range(2):
            base = cc * 64
            sbf = cur["sbf"]
            su = sus[cc]
            for h in range(4):
                P.op(PE, lambda e, h=h, sbf=sbf, base=base: e.matmul(oo.t[:, h * 128 + base:h * 128 + base + 64], lhsT=sbf.t[:, h * 128:(h + 1) * 128],
                                                                     rhs=qd4[:, h, c0 + base:c0 + base + 64], start=False, stop=True, skip_group_check=True),
                     reads=[sbf.d, d_qd4[h]], writes=[oo.d])
            for h in range(4):
                ecol = ebe4[:, par * 32 + h * 8 + pp * 2 + cc:par * 32 + h * 8 + pp * 2 + cc + 1]
                P.op(DVE, lambda e, h=h, su=su, ecol=ecol: e.scalar_tensor_tensor(out=S4[:, h * 128:(h + 1) * 128], in0=S4[:, h * 128:(h + 1) * 128],
                                                                                scalar=ecol, in1=su.t[:, h * 128:(h + 1) * 128],
                                                                                op0=ALU.mult, op1=ALU.add),
                     reads=[su.d, d_ebe[par][h], d_S[h]], writes=[d_S[h]])
            nsb = Sbf4_r.next()
            P.op(ACT, lambda e, nsb=nsb: e.copy(out=nsb.t, in_=S4[:, :]), reads=d_S, writes=[nsb.d])
            cur["sbf"] = nsb
        ohg = ohg4_r.next()
        P.op(ACT, lambda e: e.copy(out=ohg.t[:, :], in_=oo.t[:, :]), reads=[oo.d], writes=[ohg.d])

        def tail():
            rs = rstd_of(lambda c: ohg.t[:, :], 512, ohg.d, nchunks=1, denom=128.0)
            for h in range(4):
                P.op(DVE, lambda e, h=h: e.scalar_tensor_tensor(out=hgoT[:, h, t0 + c0:t0 + c0 + 128], in0=ohg.t[:, h * 128:(h + 1) * 128], scalar=ghg[:, h:h + 1],
                                                                in1=rs.t[:, h * 128:(h + 1) * 128], op0=ALU.mult, op1=ALU.mult),
                     reads=[ohg.d, rs.d, dc], writes=[d_hgo])
        return tail

    def hg_sample(h, t0, qg, kk):
        ohg = ohg4_r.next()
        fS = hsm[:, h * 4:h * 4 + 4]
        P.op(DVE, lambda e: e.tensor_scalar(out=fS, in0=kk.t[:, 0:NS], scalar1=lb[:, h:h + 1], scalar2=None, op0=ALU.add),
             reads=[kk.d, dc], writes=[d_hsm])
        osp = ps()
        for b in range(NS):
            hg_sample_b(h, b, qg, kk, fS, osp)
        P.op(ACT, lambda e: e.copy(out=ohg.t[:, 0:NS], in_=osp.t[:, 0:NS]), reads=[osp.d], writes=[ohg.d])
        rs = rstd_of(lambda c: ohg.t[:, 0:NS], NS, ohg.d, nchunks=1, denom=128.0)
        P.op(DVE, lambda e: e.scalar_tensor_tensor(out=hgoT[:, h, t0:t0 + NS], in0=ohg.t[:, 0:NS], scalar=ghg[:, h:h + 1],
                                                   in1=rs.t[:, 0:NS], op0=ALU.mult, op1=ALU.mult),
             reads=[ohg.d, rs.d, dc], writes=[d_hgo])

    def hg_sample_b(h, b, qg, kk, fS, osp):
        s0 = s0_r.next()
        P.dma(ACT, lambda e: e.dma_start(out=s0.t[:, :], in_=st_in[b * 4 + h]), writes=[s0.d])
        vb = ps()
        P.op(PE, lambda e: e.matmul(vb.t[:, 0:128], lhsT=sel[:, b, :], rhs=his[:, h * 128:(h + 1) * 128], start=True, stop=True),
             reads=[dc, d_his], writes=[vb.d])
        P.op(DVE, lambda e: e.tensor_scalar(out=s0.t[:, :], in0=s0.t[:, :], scalar1=fS[:, b:b + 1], scalar2=None, op0=ALU.mult),
             reads=[s0.d, d_hsm], writes=[s0.d])
        s1 = s1_r.next()
        P.op(DVE, lambda e: e.scalar_tensor_tensor(out=s1.t[:, :], in0=vb.t[:, 0:128], scalar=kk.t[:, b:b + 1],
                                                   in1=s0.t[:, :], op0=ALU.mult, op1=ALU.add),
             reads=[vb.d, kk.d, s0.d], writes=[s1.d])
        P.dma(SP, lambda e: e.dma_start(out=shg[b * 4 + h], in_=s1.t[:, :]), reads=[s1.d])
        out_deps.append(s1.d)
        P.op(PE, lambda e: e.matmul(osp.t[:, b:b + 1], lhsT=s1.t[:, :], rhs=qg.t[:, b:b + 1], start=True, stop=True),
             reads=[s1.d, qg.d], writes=[osp.d])

    o5 = 4 * NT
    memT = arena[:, o5:o5 + 8 * 256 * 2].bitcast(F32).rearrange("p (c n) -> p c n", c=8); o5 += 8 * 256 * 2
    mhT = arena[:, o5:o5 + 8 * 256].rearrange("p (c n) -> p c n", c=8); o5 += 8 * 256
    o5 = scr0
    mkT = arena[:, o5:o5 + 4 * 256].rearrange("p (h n) -> p h n", h=4); o5 += 4 * 256
    mvb = arena[:, o5:o5 + 2 * 512].rearrange("p (c n) -> p c n", c=2); o5 += 2 * 512
    ptm = arena[:, o5:o5 + 2 * 512].rearrange("p (c n) -> p c n", c=2); o5 += 2 * 512
    mqT4 = arena[:, o5:o5 + 4 * 512].rearrange("p (h n) -> p h n", h=4); o5 += 4 * 512
    d_mqT4 = [Dep() for _ in range(4)]
    mks = arena[:, o5:o5 + 2 * 512].rearrange("p (c n) -> p c n", c=2); o5 += 1024
    mvs = arena[:, o5:o5 + 2 * 512].rearrange("p (c n) -> p c n", c=2); o5 += 1024
    mkTs = arena[:, o5:o5 + 4 * 256].rearrange("p (h n) -> p h n", h=4); o5 += 1024
    mem_end = o5
    d_memT, d_mhT, d_mkT, d_mvb, d_ptm = Dep(), Dep(), Dep(), Dep(), Dep()
    d_mks, d_mvs, d_mkTs = Dep(), Dep(), Dep()
    assert mem_end <= 8 * NT * 2, mem_end
    mqs = A2(4 * NS).rearrange("p (h n) -> p h n", h=4)
    d_mqs = Dep("mqs")
    msm = A2(64, F32)
    msb = A2(16)
    d_msm = Dep("msm")

    def mem_attention():
        for i in range(2):
            load_tokmajor_to_T(mem[i * 128:(i + 1) * 128, :], 128, memT, i * 128, d_memT)
        norm_to(memT, {0: d_memT}, gme, mhT, {0: d_mhT}, [(0, (0, 256))])
        w_mk_v = w_mk.rearrange("(c p) n -> p c n", p=128)
        w_mv_v = w_mv.rearrange("(c p) n -> p c n", p=128)
        wbk, wk = load_wchunk(w_mk_v, 0, 512)
        wbv, wv_ = load_wchunk(w_mv_v, 0, 512)
        for hh in range(4):
            pb = ps()
            for c in range(8):
                P.op(PE, lambda e, c=c, pb=pb, hh=hh: e.matmul(pb.t[:, 0:256], lhsT=wk[:, c, hh * 128:(hh + 1) * 128], rhs=mhT[:, c, :], start=(c == 0), stop=(c == 7)),
                     reads=[wbk.ds[0], d_mhT], writes=[pb.d])
            P.op(ACT, lambda e, pb=pb, hh=hh: e.copy(out=mkT[:, hh, :], in_=pb.t[:, 0:256]), reads=[pb.d], writes=[d_mkT])
        for (wb_, w_, dst, isv) in ((wbk, wk, pmk, False), (wbv, wv_, pmv, True)):
            for i in range(2):
                pb = ps()
                for c in range(8):
                    P.op(PE, lambda e, c=c, pb=pb, i=i, w_=w_: e.matmul(pb.t[:, :], lhsT=mhT[:, c, i * 128:(i + 1) * 128], rhs=w_[:, c, :], start=(c == 0), stop=(c == 7)),
                         reads=[wb_.ds[0], d_mhT], writes=[pb.d])
                sb = stage.next()
                P.op(ACT, lambda e, pb=pb, sb=sb: e.copy(out=sb.t[:, 0:512], in_=pb.t[:, :]), reads=[pb.d], writes=[sb.d])
                if isv:
                    P.op(DVE, lambda e, sb=sb, i=i: e.tensor_copy(out=mvb[:, i, :], in_=sb.t[:, 0:512]), reads=[sb.d], writes=[d_mvb])
                P.dma(SP, lambda e, dst=dst, sb=sb, i=i: e.dma_start(out=dst[i * 128:(i + 1) * 128, :], in_=sb.t[:, 0:512]), reads=[sb.d])
                out_deps.append(sb.d)
        wbq, wq = load_wchunk(w_in_v, 6144, 512)
        for ti, (t0, n) in ALLTG:
            for hh in range(4):
                mem_q(wbq, wq, hh, ti, t0, n)
            if ti < 4:
                for hh in range(4):
                    mem_unit(hh, t0)
        for b in range(NS):
            mem_sample(b)

    def mem_q(wbq, wq, hh, ti, t0, n):
        pb = ps()
        for c in range(8):
            P.op(PE, lambda e, c=c: e.matmul(pb.t[:, :n], lhsT=wq[:, c, hh * 128:(hh + 1) * 128], rhs=hT[:, c, t0:t0 + n], start=(c == 0), stop=(c == 7)),
                 reads=[wbq.ds[0], dh[ti]], writes=[pb.d])
        if ti == 4:
            P.op(DVE, lambda e: e.tensor_copy(out=mqs[:, hh, :], in_=pb.t[:, 0:NS]), reads=[pb.d], writes=[d_mqs])
        elif hh % 2 == 0:
            P.op(ACT, lambda e: e.copy(out=mqT4[:, hh, :], in_=pb.t[:, :]), reads=[pb.d], writes=[d_mqT4[hh]])
        else:
            P.op(DVE, lambda e: e.tensor_copy(out=mqT4[:, hh, :], in_=pb.t[:, :]), reads=[pb.d], writes=[d_mqT4[hh]])

    def mem_unit(hh, t0):
        scb = [ps(), ps()]
        for mc in range(2):
            P.op(PE, lambda e, mc=mc: e.matmul(scb[mc].t[:, :], lhsT=mkT[:, hh, mc * 128:(mc + 1) * 128], rhs=mqT4[:, hh, :], start=True, stop=True),
                 reads=[d_mkT, d_mqT4[hh]], writes=[scb[mc].d])
            P.op(ACT, lambda e, mc=mc: e.activation(out=ptm[:, mc, :], in_=scb[mc].t[:, :], func=AF.Exp, scale=SCALE),
                 reads=[scb[mc].d], writes=[d_ptm])
        om = ps(); dn = ps()
        for mc in range(2):
            P.op(PE, lambda e, mc=mc: e.matmul(om.t[:, :], lhsT=mvb[:, mc, hh * 128:(hh + 1) * 128], rhs=ptm[:, mc, :], start=(mc == 0), stop=(mc == 1)),
                 reads=[d_mvb, d_ptm], writes=[om.d])
        for mc in range(2):
            P.op(PE, lambda e, mc=mc: e.matmul(dn.t[:, :], lhsT=ones_b, rhs=ptm[:, mc, :], start=(mc == 0), stop=(mc == 1)),
                 reads=[dc, d_ptm], writes=[dn.d])
        rc = f32r.next()
        P.op(ACT, lambda e: e.activation(out=rc.t[:, :], in_=dn.t[:, :], func=AF.Ln), reads=[dn.d], writes=[rc.d])
        P.op(ACT, lambda e: e.activation(out=rc.t[:, :], in_=rc.t[:, :], func=AF.Exp, scale=-1.0), reads=[rc.d], writes=[rc.d])
        P.op(DVE, lambda e: e.tensor_tensor(out=memoT[:, hh, t0:t0 + 512], in0=om.t[:, :], in1=rc.t[:, :], op=ALU.mult),
             reads=[om.d, rc.d], writes=[d_memo])

    def mem_sample(b):
        if True:
            P.dma(POOL, lambda e, b=b: e.dma_start(out=mks, in_=cmk[b].rearrange("(c p) n -> p c n", p=128)), writes=[d_mks])
            P.dma(POOL, lambda e, b=b: e.dma_start(out=mvs, in_=cmv[b].rearrange("(c p) n -> p c n", p=128)), writes=[d_mvs])
            for mc in range(2):
                tp = ps()
                tpv = tp.t[:].bitcast(BF16)
                for hh in range(4):
                    P.op(PE, lambda e, hh=hh, mc=mc, tpv=tpv: e.transpose(out=tpv[:, hh * 128:(hh + 1) * 128], in_=mks[:, mc, hh * 128:(hh + 1) * 128], identity=ident_b),
                         reads=[d_mks, dc], writes=[tp.d])
                P.op(ACT, lambda e, mc=mc, tpv=tpv: e.copy(out=mkTs[:, :, mc * 128:(mc + 1) * 128], in_=tpv[:, 0:512].rearrange("p (h n) -> p h n", h=4)),
                     reads=[tp.d], writes=[d_mkTs])
            scm = ps()
            for hh in range(4):
                for mc in range(2):
                    col = hh * 2 + mc
                    P.op(PE, lambda e, hh=hh, mc=mc, col=col, b=b: e.matmul(scm.t[:, col:col + 1], lhsT=mkTs[:, hh, mc * 128:(mc + 1) * 128], rhs=mqs[:, hh, b:b + 1],
                                                                        start=True, stop=True), reads=[d_mkTs, d_mqs], writes=[scm.d])
            pms = msb[:, 0:8]
            P.op(ACT, lambda e: e.activation(out=pms, in_=scm.t[:, 0:8], func=AF.Exp, scale=SCALE), reads=[scm.d], writes=[d_msm])
            oms = ps()
            for hh in range(4):
                for mc in range(2):
                    col = hh * 2 + mc
                    P.op(PE, lambda e, hh=hh, mc=mc, col=col: e.matmul(oms.t[:, col:col + 1], lhsT=mvs[:, mc, hh * 128:(hh + 1) * 128], rhs=pms[:, col:col + 1],
                                                                   start=True, stop=True), reads=[d_mvs, d_msm], writes=[oms.d])
            P.op(PE, lambda e: e.matmul(oms.t[:, 16:24], lhsT=ones_b, rhs=pms, start=True, stop=True), reads=[dc, d_msm], writes=[oms.d])
            t8 = msm[:, 0:16]
            P.op(DVE, lambda e: e.tensor_copy(out=t8[:, 0:8], in_=oms.t[:, 0:8]), reads=[oms.d], writes=[d_msm])
            P.op(DVE, lambda e: e.tensor_copy(out=t8[:, 8:16], in_=oms.t[:, 16:24]), reads=[oms.d], writes=[d_msm])
            t4 = msm[:, 16:24]
            v8 = t8.rearrange("p (k h m) -> p k h m", k=2, h=4)
            P.op(DVE, lambda e: e.tensor_tensor(out=t4.rearrange("p (k h) -> p k h", k=2), in0=v8[:, :, :, 0], in1=v8[:, :, :, 1], op=ALU.add),
                 reads=[d_msm], writes=[d_msm])
            P.op(DVE, lambda e: e.reciprocal(out=t4[:, 4:8], in_=t4[:, 4:8]), reads=[d_msm], writes=[d_msm])
            P.op(DVE, lambda e, b=b: e.tensor_tensor(out=memoT[:, :, T + b], in0=t4[:, 0:4], in1=t4[:, 4:8], op=ALU.mult), reads=[d_msm], writes=[d_memo])

    mTa = arena[:, scr0:scr0 + 4 * NT].rearrange("p (c n) -> p c n", c=4)
    mTb = aarena[:, a2_keep:a2_keep + 4 * NT].rearrange("p (c n) -> p c n", c=4)
    assert a2_keep + 4 * NT <= NJ * 1028 and scr0 + 4 * NT <= 8 * NT * 2

    def mTv(c):
        return mTa[:, c, :] if c < 4 else mTb[:, c - 4, :]
    d_mT = [Dep(f"mT{i}") for i in range(5)]
    d_xd = Dep("xd")
    dxc = [Dep(f"xc{c}") for c in range(6)]
    dxw = [Dep(f"xw{i}") for i in range(5)]
    drl = [Dep(f"xrl{i}") for i in range(5)]
    d_xd2 = [[Dep() for _ in range(5)] for _ in range(8)]

    def merge():
        brs = ((attT, d_att, w_ba), (hgoT, d_hgo, w_bb), (memoT, d_memo, w_bc))
        gate_v = w_in_v[:, :, 6656:9728].rearrange("p c (i d n) -> p c i d n", i=3, d=8)
        w_o_v = w_o.rearrange("(c p) n -> p c n", p=128)
        for dp in range(8):
            merge_dp(brs, dp, gate_v)
        bdep = [d_att, d_att, d_hgo, d_hgo, d_memo, d_memo]
        for c in range(6):
            phase_switch([dxc[c]], [bdep[c]])
            P.dma(SP, lambda e, c=c: e.dma_start(out=xT[:, c, :], in_=xd[:, c, :]), reads=[d_xd], writes=[dxc[c]])
        for hf in range(2):
            merge_out(hf, w_o_v)

    def merge_dp(brs, dp, gate_v):
        wb = wring.next()
        gv = wb.t[:, 0:3072].rearrange("p (c i n) -> p c i n", c=8, i=3)
        for i in range(3):
            P.dma(POOL, lambda e, i=i: e.dma_start(out=gv[:, :, i, :], in_=gate_v[:, :, i, dp, :]), writes=(wb.ds if i == 0 else [wb.ds[i]]), skip_waw=(i > 0))
        bvs = []
        for i in range(3):
            bv = wb.t[:, 3072 + i * 512:3072 + (i + 1) * 512].rearrange("p (c n) -> p c n", c=4)
            P.dma(POOL, lambda e, bv=bv, i=i: e.dma_start(out=bv, in_=brs[i][2].rearrange("(c p) n -> p c n", p=128)[:, :, dp * 128:(dp + 1) * 128]),
                  writes=[wb.ds[3 + i]], skip_waw=True)
            bvs.append(bv)
        for ti, (t0, n) in ALLTG:
            merge_one(brs, wb, gv, bvs, ti, t0, n, macc_bufs[ti % 2], dp)

    def merge_one(brs, wb, gv, bvs, ti, t0, n, macc, dp):
        for i in range(3):
            pg = ps(); pbr = ps()
            for c in range(8):
                P.op(PE, lambda e, c=c, pg=pg, i=i: e.matmul(pg.t[:, :n], lhsT=gv[:, c, i, :], rhs=hT[:, c, t0:t0 + n], start=(c == 0), stop=(c == 7)),
                     reads=[wb.ds[i], dh[ti]], writes=[pg.d])
            for c in range(4):
                P.op(PE, lambda e, c=c, pbr=pbr, i=i: e.matmul(pbr.t[:, :n], lhsT=bvs[i][:, c, :], rhs=brs[i][0][:, c, t0:t0 + n], start=(c == 0), stop=(c == 3)),
                     reads=[wb.ds[3 + i], brs[i][1]], writes=[pbr.d])
            sg = f32r.next()
            P.op(ACT, lambda e, sg=sg, pg=pg: e.activation(out=sg.t[:, :n], in_=pg.t[:, :n], func=AF.Sigmoid), reads=[pg.d], writes=[sg.d])
            if i == 0:
                P.op(DVE, lambda e, sg=sg, pbr=pbr: e.tensor_tensor(out=macc.t[:, :n], in0=pbr.t[:, :n], in1=sg.t[:, :n], op=ALU.mult),
                     reads=[pbr.d, sg.d], writes=[macc.d])
            else:
                P.op(DVE, lambda e, sg=sg, pbr=pbr: e.tensor_tensor(out=sg.t[:, :n], in0=pbr.t[:, :n], in1=sg.t[:, :n], op=ALU.mult),
                     reads=[pbr.d, sg.d], writes=[sg.d])
                if i == 1:
                    P.op(DVE, lambda e, sg=sg: e.tensor_tensor(out=macc.t[:, :n], in0=macc.t[:, :n], in1=sg.t[:, :n], op=ALU.add),
                         reads=[sg.d, macc.d], writes=[macc.d])
                else:
                    P.op(DVE, lambda e, sg=sg: e.tensor_tensor(out=mTv(dp)[:, t0:t0 + n], in0=macc.t[:, :n], in1=sg.t[:, :n], op=ALU.add),
                         reads=[sg.d, macc.d], writes=[d_mT[ti]])

    def merge_out(hf, w_o_v):
        wbo, wo = load_wchunk(w_o_v, hf * 512, 512)
        for d4 in range(4):
            dpo = hf * 4 + d4
            for ti, (t0, n) in ALLTG:
                merge_out_one(wbo, wo, d4, dpo, ti, t0, n)

    def merge_out_one(wbo, wo, d4, dpo, ti, t0, n):
        po = ps()
        for c in range(8):
            P.op(PE, lambda e, c=c: e.matmul(po.t[:, :n], lhsT=wo[:, c, d4 * 128:(d4 + 1) * 128], rhs=mTv(c)[:, t0:t0 + n], start=(c == 0), stop=(c == 7)),
                 reads=[wbo.ds[0], d_mT[ti]], writes=[po.d])
        if dpo < 6:
            P.op(DVE, lambda e: e.tensor_tensor(out=xT[:, dpo, t0:t0 + n], in0=po.t[:, :n], in1=xT[:, dpo, t0:t0 + n], op=ALU.add),
                 reads=[po.d, dxc[dpo]], writes=[dxc[dpo], dxw[ti]])
            return
        xo = xo_r.next()
        P.dma(ACT, lambda e: e.dma_start(out=xo.t[:, :n], in_=xd[:, dpo, t0:t0 + n]), reads=[d_xd], writes=[xo.d])
        P.op(DVE, lambda e: e.tensor_tensor(out=xo.t[:, :n], in0=po.t[:, :n], in1=xo.t[:, :n], op=ALU.add),
             reads=[po.d, xo.d], writes=[xo.d])
        P.dma(SP, lambda e: e.dma_start(out=xd2[:, dpo, t0:t0 + n], in_=xo.t[:, :n]), reads=[xo.d], writes=[d_xd2[dpo][ti]])

    yT_r = [Buf(aarena[:, k_ * 8192:(k_ + 1) * 8192].bitcast(F32).rearrange("p (c n) -> p c n", c=8)) for k_ in range(2)]
    d_yT = [b_.d for b_ in yT_r]

    def final_prep(ti, t0, n):
        yb = yT_r[ti % 2]
        rs = rstd_of(lambda c: xT[:, c, t0:t0 + n], n, dx[ti])
        for c in range(8):
            P.op(DVE, lambda e, c=c: e.scalar_tensor_tensor(out=yb.t[:, c, 0:n], in0=xT[:, c, t0:t0 + n], scalar=gfi[:, c:c + 1], in1=rs.t[:, :n],
                                                            op0=ALU.mult, op1=ALU.mult), reads=[dx[ti], rs.d, dc], writes=[yb.d])

    def final_emit(ti, t0, n):
        yb = yT_r[ti % 2]
        ntile = (n + 127) // 128
        for tl in range(ntile):
            final_tile(yb, ti, t0, n, tl)

    def final_tile(yb, ti, t0, n, tl):
        m = min(128, n - tl * 128)
        sb = stage.next()
        for hf in range(2):
            pb = ps()
            for c4 in range(4):
                c = hf * 4 + c4
                P.op(PE, lambda e, c=c, c4=c4, pb=pb: e.transpose(out=pb.t[0:m, c4 * 128:(c4 + 1) * 128], in_=yb.t[:, c, tl * 128:tl * 128 + m], identity=ident_f),
                     reads=[yb.d, dc, d_id], writes=[pb.d])
            if hf == 0:
                P.op(ACT, lambda e, pb=pb: e.copy(out=sb.t[0:m, 0:512], in_=pb.t[0:m, :]), reads=[pb.d], writes=[sb.d])
            else:
                P.op(DVE, lambda e, pb=pb: e.tensor_copy(out=sb.t[0:m, 512:1024], in_=pb.t[0:m, :]), reads=[pb.d], writes=[sb.d])
        if ti < 4:
            r0 = t0 + tl * 128
            P.dma(SP, lambda e: e.dma_start(out=y_p[r0:r0 + 128, :], in_=sb.t[:, :]), reads=[sb.d])
        else:
            P.dma(SP, lambda e: e.dma_start(out=y_s[:, :], in_=sb.t[0:NS, :]), reads=[sb.d])
        out_deps.append(sb.d)

    def final_out():
        final_prep(0, *TG[0])
        for ti, (t0, n) in ALLTG:
            if ti + 1 < 5:
                final_prep(ti + 1, *TG[ti + 1])
            final_emit(ti, t0, n)

    def phase_switch(new_deps, old_deps):
        for nd in new_deps:
            for od in old_deps:
                if od.w is not None:
                    s, k = od.w
                    if nd.r.get(s, -1) < k:
                        nd.r[s] = k
                for s, k in od.r.items():
                    if nd.r.get(s, -1) < k:
                        nd.r[s] = k

    consts()
    for i in range(16):
        load_tokmajor_to_T(x_p[i * 128:(i + 1) * 128, :], 128, xT, i * 128, dx[i // 4])
    load_tokmajor_to_T(x_s[:, :], NS, xT, T, dx[4])
    consts_late()
    KST = 9
    if KST >= 1:
        ffn(gf1, w1g, w1u, w1d)
    if KST >= 2:
        norm_to(xT, dx, gmx, hT, dh, ALLTG)
    if KST >= 2:
        for ti, (t0, n) in ALLTG:
            P.dma(SP, lambda e, t0=t0, n=n: e.dma_start(out=xd[:, :, t0:t0 + n], in_=xT[:, :, t0:t0 + n]), reads=[dx[ti]], writes=[d_xd])
        a2_small = ([d_his, d_zs, d_sm, d_hsm, d_mqs, d_msm] + d_S + d_qd4 + d_kd4 + d_kd2 + d_ebe[0] + d_ebe[1] +
                   [d_ for r_ in (ptr, ks_r, kts_r, Sbf4_r, kdt4_r, am4_r, s0_r, s1_r, ohg4_r) for b in r_.bufs for d_ in b.ds] + d_kd2)
        a2_deps = a2_small + [b.d for b in macc_bufs] + [b.d for b in xo_r.bufs]
        phase_switch(a2_deps, dact)
        att_scr = [d_acc, d_att] + d_vs + [st_[k_] for st_ in asets for k_ in ('dq', 'dk', 'dv', 'dvT')]
        phase_switch(att_scr, dx)
        attention_prompt()
        phase_switch(d_vs, [asets[1][k_] for k_ in ('dq', 'dk', 'dv', 'dvT')])
        attention_sample()
        mem_scr = [d_memT, d_mhT, d_mkT, d_mvb, d_ptm, d_mks, d_mvs, d_mkTs, d_memo] + d_mqT4
        phase_switch(mem_scr, att_scr + dx)
        mem_attention()
        hg_scr = [d_vtok, d_hgo]
        phase_switch(hg_scr, att_scr + mem_scr + dx)
        a2_att = [d_zs, d_sm] + [d_ for r_ in (ptr, ks_r, kts_r) for b in r_.bufs for d_ in b.ds]
        a2_hg = (d_S + d_qd4 + d_kd4 + d_kd2 + d_ebe[0] + d_ebe[1] + [d_hsm] +
                 [d_ for r_ in (Sbf4_r, kdt4_r, am4_r, s0_r, s1_r, ohg4_r) for b in r_.bufs for d_ in b.ds])
        phase_switch(a2_hg, a2_att)
        tokmajor_pass()
        hgrn()
        phase_switch(d_mT, hg_scr + att_scr + mem_scr + dx + a2_small)
        merge()
        allmix = d_mT + hg_scr + att_scr + mem_scr + [d_att, d_hgo, d_memo]
        phase_switch(dx, allmix)
        phase_switch(drl, allmix)
        for ti, (t0, n) in ALLTG:
            P.dma(SP, lambda e, t0=t0, n=n: e.dma_start(out=xT[:, 6:8, t0:t0 + n], in_=xd2[:, 6:8, t0:t0 + n]),
                  reads=[d_xd2[6][ti], d_xd2[7][ti]], writes=[drl[ti]])
            P.op(DVE, lambda e, ti=ti: e.tensor_copy(out=vec[:, 58 + ti:59 + ti], in_=vec[:, 57:58]), reads=[dxw[ti], drl[ti], dc], writes=[dx[ti]])
        phase_switch(dact, a2_deps)
    if KST >= 3:
        ffn(gf2, w2g, w2u, w2d)
    phase_switch(d_yT, dact)
    final_out()
    P.op(SP, lambda e: e.nop(), reads=[], writes=out_deps)
    P.emit()
    return nc


_NC = None


def _prep(inp):
    f = lambda a: np.ascontiguousarray(np.asarray(a, dtype=np.float32))
    in_maps = []
    shared = {}
    for k in ("g_ff1", "g_mix", "g_mem", "g_ff2", "g_hg_out", "w_ff1_gate", "w_ff1_up", "w_ff1_down", "w_ff2_gate", "w_ff2_up",
              "w_ff2_down", "w_in", "w_mem_k", "w_mem_v", "w_branch_att", "w_branch_hg", "w_branch_mem", "w_out"):
        shared[k] = f(inp[k][0])
    shared["g_final"] = f(inp["g_final"])
    shared["hg_lb_logits"] = f(inp["hg_lb_logits"])
    cw = {}
    for g, nm in enumerate(("1", "4", "16")):
        cw[(g, "k")] = np.asarray(inp[f"cache_win{nm}_k"])[0]
        cw[(g, "v")] = np.asarray(inp[f"cache_win{nm}_v"])[0]
    for i in range(8):
        m = dict(shared)
        m["x_p"] = f(inp["x_prompt"][i])
        m["x_s"] = f(np.asarray(inp["x_sample"])[4 * i:4 * i + 4, 0, :])
        m["mem"] = f(inp["mem_prompt"][i])
        for g in range(3):
            W = WIN[g][0]
            m[f"cw{g}k"] = f(cw[(g, "k")][4 * i:4 * i + 4].reshape(4, W, 512))
            m[f"cw{g}v"] = f(cw[(g, "v")][4 * i:4 * i + 4].reshape(4, W, 512))
        m["cmk"] = f(np.asarray(inp["cache_mem_k"])[0, 4 * i:4 * i + 4].reshape(4, 256, 512))
        m["cmv"] = f(np.asarray(inp["cache_mem_v"])[0, 4 * i:4 * i + 4].reshape(4, 256, 512))
        m["st"] = f(np.asarray(inp["state_hgrn"])[0, 4 * i:4 * i + 4].reshape(16, 128, 128))
        in_maps.append(m)
    return in_maps


def kernel(**inp):
    global _NC
    if _NC is None:
        _NC = build_nc()
    nc = _NC
    in_maps = _prep(inp)
    res = run_bass_kernel_spmd(nc, in_maps, core_ids=list(range(8)))
    return _assemble(res.results, 8)


def _assemble(R, ncore):
    cat = lambda k: np.stack([np.asarray(R[i][k], dtype=np.float32) for i in range(ncore)], axis=0)
    y_p = cat("y_p")
    y_s = cat("y_s").reshape(4 * ncore, 1, D)
    outs = [y_p, y_s]
    for g in range(3):
        W = WIN[g][0]
        outs.append(cat(f"pw{g}k").reshape(1, ncore, W, 4, 128))
        outs.append(cat(f"pw{g}v").reshape(1, ncore, W, 4, 128))
    outs.append(cat("pmk").reshape(1, ncore, 256, 4, 128))
    outs.append(cat("pmv").reshape(1, ncore, 256, 4, 128))
    outs.append(cat("phg").reshape(1, ncore, 4, 128, 128))
    for g in range(3):
        W = WIN[g][0]
        outs.append(cat(f"sw{g}k").reshape(1, 4 * ncore, W, 4, 128))
        outs.append(cat(f"sw{g}v").reshape(1, 4 * ncore, W, 4, 128))
    outs.append(cat("shg").reshape(1, 4 * ncore, 4, 128, 128))
    return tuple(outs)
```
